# Optimizing a Trainium2 kernel written in Bass

```python
import jax, jax.numpy as jnp
from jax import lax
import numpy as np

D_MODEL = 1024
BATCH = 8
SEQ = 2048
DEPTH = 2
DEC_BATCH = 128
DEC_SEQ = 1
PAST_LEN = 16384
PAGE_SIZE = 128

D_POOL = D_MODEL // 2
POOL_WINDOWS = (2, 4, 8, 16)
POOL_GROUPS = len(POOL_WINDOWS)
POOL_GW = D_POOL // POOL_GROUPS
POOL_STATE = max(POOL_WINDOWS) - 1
D_RNN = D_MODEL
RNN_HEADS = 8
RNN_HD = D_RNN // RNN_HEADS
CONV_WIDTH = 4
LRU_C = 8.0
D_CHUNK = D_MODEL // 2
CHUNK = 128
CHUNK_GROUPS = 4
CHUNK_GW = D_CHUNK // CHUNK_GROUPS
N_BRANCH = 3
D_FF = 3 * D_MODEL
FFN_CONV = 3
EPS = 1e-6
IN_COLS = D_POOL + 2 * D_RNN + 2 * D_CHUNK + N_BRANCH * D_MODEL
IN_SPLITS = (D_POOL, D_POOL + D_RNN, D_POOL + 2 * D_RNN, D_POOL + 2 * D_RNN + 2 * D_CHUNK)

kernel_name = 'hybrid_pool_rglru_chunkmlp_decode_step'


def rmsnorm(x, g):
    xf = x.astype(jnp.float32)
    y = xf * lax.rsqrt(jnp.mean(xf * xf, axis=-1, keepdims=True) + EPS)
    return (y * g.astype(jnp.float32)).astype(x.dtype)


def causal_dwconv(x_ext, w, b):
    c = x_ext.shape[-1]
    y = lax.conv_general_dilated(x_ext, w[:, None, :].astype(x_ext.dtype), window_strides=(1,),
                                 padding='VALID', dimension_numbers=('NWC', 'WIO', 'NWC'),
                                 feature_group_count=c)
    return y + b


def multiscale_pool(a_ext, pos, pool_w, pool_scale):
    bn, length, _ = a_ext.shape
    t = length - POOL_STATE
    af = a_ext.astype(jnp.float32)
    cs = jnp.concatenate([jnp.zeros((bn, 1, D_POOL), jnp.float32), jnp.cumsum(af, axis=1)], axis=1)
    hi = cs[:, POOL_STATE + 1:]
    lo = jnp.concatenate([cs[:, POOL_STATE + 1 - w: POOL_STATE + 1 - w + t, g * POOL_GW:(g + 1) * POOL_GW]
                          for g, w in enumerate(POOL_WINDOWS)], axis=-1)
    win = jnp.repeat(jnp.array(POOL_WINDOWS, jnp.int32), POOL_GW)
    cnt = jnp.minimum(pos[:, None] + 1, win[None, :]).astype(jnp.float32)
    d = (hi - lo) / cnt - af[:, POOL_STATE:]
    d = d.reshape(bn, t, POOL_GROUPS, POOL_GW)
    y = jnp.einsum('btgi,gij->btgj', d, pool_w.astype(jnp.float32)).reshape(bn, t, D_POOL)
    return (y * pool_scale.astype(jnp.float32)).astype(a_ext.dtype)


def block_diag(x, w, b):
    xh = x.reshape(*x.shape[:-1], RNN_HEADS, RNN_HD)
    return jnp.einsum('bthi,hij->bthj', xh, w).reshape(x.shape) + b


def _lin_combine(left, right):
    a_l, b_l = left
    a_r, b_r = right
    return a_l * a_r, a_r * b_l + b_r


def rg_lru(xc, h0, wa, ba, wx, bx, lam):
    r = jax.nn.sigmoid(block_diag(xc, wa, ba).astype(jnp.float32))
    i = jax.nn.sigmoid(block_diag(xc, wx, bx).astype(jnp.float32))
    log_a = -LRU_C * r * jax.nn.softplus(-lam.astype(jnp.float32))
    a = jnp.exp(log_a)
    b = jnp.sqrt(-jnp.expm1(2.0 * log_a)) * (i * xc.astype(jnp.float32))
    b = b.at[:, 0].add(a[:, 0] * h0.astype(jnp.float32))
    _, h = lax.associative_scan(_lin_combine, (a, b), axis=1)
    return h.astype(xc.dtype), h[:, -1].astype(xc.dtype)


def chunk_mlp(uv, vnorm_g, ws, bs):
    u, v = jnp.split(uv, 2, axis=-1)
    v = rmsnorm(v, vnorm_g)
    bn, t, _ = v.shape
    n_chunks = -(-t // CHUNK)
    tp = n_chunks * CHUNK
    vp = jnp.pad(v, ((0, 0), (0, tp - t), (0, 0))).reshape(bn, n_chunks, CHUNK, CHUNK_GROUPS, CHUNK_GW)
    mask = jnp.tril(jnp.ones((CHUNK, CHUNK), dtype=bool))
    ws_c = jnp.where(mask[None], ws, 0)
    mix = jnp.einsum('gij,bcjgd->bcigd', ws_c, vp) + jnp.swapaxes(bs, 0, 1)[:, :, None]
    mix = mix.reshape(bn, tp, D_CHUNK)[:, :t]
    return u * mix, v


def conv_ffn(xn, prefix, wg, wu, cw, cb, wd):
    g_pre = xn @ wg
    ext = jnp.concatenate([prefix.astype(g_pre.dtype), g_pre], axis=1)
    h = jax.nn.gelu(causal_dwconv(ext, cw, cb)) * (xn @ wu)
    return h @ wd, ext[:, -(FFN_CONV - 1):]


def decoder_layer(x, pos, pool_prefix, rconv_prefix, h0, ffn_prefix, lp):
    xn = rmsnorm(x, lp['norm1_g'])
    z = xn @ lp['w_in']
    a_in, b_x, b_gate, c_uv, gates = jnp.split(z, IN_SPLITS, axis=-1)
    a_ext = jnp.concatenate([pool_prefix.astype(a_in.dtype), a_in], axis=1)
    ya = multiscale_pool(a_ext, pos, lp['pool_w'], lp['pool_scale'])
    b_ext = jnp.concatenate([rconv_prefix.astype(b_x.dtype), b_x], axis=1)
    bc = causal_dwconv(b_ext, lp['rnn_conv_w'], lp['rnn_conv_b'])
    h, h_last = rg_lru(bc, h0, lp['lru_wa'], lp['lru_ba'], lp['lru_wx'], lp['lru_bx'], lp['lru_lambda'])
    yb = jax.nn.gelu(b_gate) * h
    yc, v_rows = chunk_mlp(jax.nn.gelu(c_uv), lp['chunk_vnorm_g'], lp['chunk_ws'], lp['chunk_bs'])
    ga, gb, gc = jnp.split(jax.nn.sigmoid(gates), N_BRANCH, axis=-1)
    merged = ga * (ya @ lp['w_pa']) + gb * (yb @ lp['w_pb']) + gc * (yc @ lp['w_pc'])
    x = x + merged @ lp['w_o']
    f, ffn_state = conv_ffn(rmsnorm(x, lp['norm2_g']), ffn_prefix, lp['ffn_wg'], lp['ffn_wu'],
                            lp['ffn_conv_w'], lp['ffn_conv_b'], lp['ffn_wd'])
    x = x + f
    return x, a_ext[:, -POOL_STATE:], b_ext[:, -(CONV_WIDTH - 1):], h_last, ffn_state, v_rows


def setup_inputs(seed: int = 0) -> dict:
    key = jax.random.key(seed)
    ks = jax.random.split(key, 40)
    f32 = jnp.float32

    def nrm(k, shape, scale):
        return jax.random.normal(k, shape, f32) * scale

    u = jax.random.uniform(ks[16], (DEPTH, D_RNN), f32, minval=0.9, maxval=0.999)
    s = u ** (1.0 / LRU_C)
    lam = jnp.log(s) - jnp.log1p(-s)
    return {
        'x_prompt': nrm(ks[0], (BATCH, SEQ, D_MODEL), 1.0),
        'x_sample': nrm(ks[1], (DEC_BATCH, DEC_SEQ, D_MODEL), 1.0),
        'state_pool': nrm(ks[2], (DEPTH, DEC_BATCH, POOL_STATE, D_POOL), 1.0),
        'state_rnn_conv': nrm(ks[3], (DEPTH, DEC_BATCH, CONV_WIDTH - 1, D_RNN), 1.0),
        'state_rnn_h': nrm(ks[4], (DEPTH, DEC_BATCH, D_RNN), 0.5),
        'state_ffn_conv': nrm(ks[5], (DEPTH, DEC_BATCH, FFN_CONV - 1, D_FF), 1.0),
        'norm1_g': 1.0 + nrm(ks[6], (DEPTH, D_MODEL), 0.05),
        'w_in': nrm(ks[7], (DEPTH, D_MODEL, IN_COLS), D_MODEL ** -0.5),
        'pool_w': nrm(ks[8], (DEPTH, POOL_GROUPS, POOL_GW, POOL_GW), POOL_GW ** -0.5),
        'pool_scale': 1.0 + nrm(ks[9], (DEPTH, D_POOL), 0.05),
        'rnn_conv_w': nrm(ks[10], (DEPTH, CONV_WIDTH, D_RNN), CONV_WIDTH ** -0.5),
        'rnn_conv_b': nrm(ks[11], (DEPTH, D_RNN), 0.01),
        'lru_wa': nrm(ks[12], (DEPTH, RNN_HEADS, RNN_HD, RNN_HD), RNN_HD ** -0.5),
        'lru_ba': nrm(ks[13], (DEPTH, D_RNN), 0.01),
        'lru_wx': nrm(ks[14], (DEPTH, RNN_HEADS, RNN_HD, RNN_HD), RNN_HD ** -0.5),
        'lru_bx': nrm(ks[15], (DEPTH, D_RNN), 0.01),
        'lru_lambda': lam,
        'chunk_vnorm_g': 1.0 + nrm(ks[17], (DEPTH, D_CHUNK), 0.05),
        'chunk_ws': nrm(ks[18], (DEPTH, CHUNK_GROUPS, CHUNK, CHUNK), CHUNK ** -0.5),
        'chunk_bs': 1.0 + nrm(ks[19], (DEPTH, CHUNK_GROUPS, CHUNK), 0.01),
        'w_pa': nrm(ks[20], (DEPTH, D_POOL, D_MODEL), D_POOL ** -0.5),
        'w_pb': nrm(ks[21], (DEPTH, D_RNN, D_MODEL), D_RNN ** -0.5),
        'w_pc': nrm(ks[22], (DEPTH, D_CHUNK, D_MODEL), D_CHUNK ** -0.5),
        'w_o': nrm(ks[23], (DEPTH, D_MODEL, D_MODEL), D_MODEL ** -0.5),
        'norm2_g': 1.0 + nrm(ks[24], (DEPTH, D_MODEL), 0.05),
        'ffn_wg': nrm(ks[25], (DEPTH, D_MODEL, D_FF), D_MODEL ** -0.5),
        'ffn_wu': nrm(ks[26], (DEPTH, D_MODEL, D_FF), D_MODEL ** -0.5),
        'ffn_conv_w': nrm(ks[27], (DEPTH, FFN_CONV, D_FF), FFN_CONV ** -0.5),
        'ffn_conv_b': nrm(ks[28], (DEPTH, D_FF), 0.01),
        'ffn_wd': nrm(ks[29], (DEPTH, D_FF, D_MODEL), D_FF ** -0.5),
        'final_norm_g': 1.0 + nrm(ks[30], (D_MODEL,), 0.05),
    }


def reference(x_prompt, x_sample, state_pool, state_rnn_conv, state_rnn_h, state_ffn_conv,
              norm1_g, w_in, pool_w, pool_scale, rnn_conv_w, rnn_conv_b, lru_wa, lru_ba, lru_wx, lru_bx,
              lru_lambda, chunk_vnorm_g, chunk_ws, chunk_bs, w_pa, w_pb, w_pc, w_o, norm2_g,
              ffn_wg, ffn_wu, ffn_conv_w, ffn_conv_b, ffn_wd, final_norm_g):
    bp, tp = x_prompt.shape[0], x_prompt.shape[1]
    ts = x_sample.shape[1]
    pos_p = jnp.arange(tp, dtype=jnp.int32)
    pos_s = PAST_LEN + jnp.arange(ts, dtype=jnp.int32)
    dt = x_prompt.dtype
    xp, xs = x_prompt, x_sample
    pool_p, pool_s, rc_p, rc_s, h_p, h_s, ff_p, ff_s, cv_s = [], [], [], [], [], [], [], [], []
    for l in range(DEPTH):
        lp = {'norm1_g': norm1_g[l], 'w_in': w_in[l], 'pool_w': pool_w[l], 'pool_scale': pool_scale[l],
              'rnn_conv_w': rnn_conv_w[l], 'rnn_conv_b': rnn_conv_b[l], 'lru_wa': lru_wa[l],
              'lru_ba': lru_ba[l], 'lru_wx': lru_wx[l], 'lru_bx': lru_bx[l], 'lru_lambda': lru_lambda[l],
              'chunk_vnorm_g': chunk_vnorm_g[l], 'chunk_ws': chunk_ws[l], 'chunk_bs': chunk_bs[l],
              'w_pa': w_pa[l], 'w_pb': w_pb[l], 'w_pc': w_pc[l], 'w_o': w_o[l], 'norm2_g': norm2_g[l],
              'ffn_wg': ffn_wg[l], 'ffn_wu': ffn_wu[l], 'ffn_conv_w': ffn_conv_w[l],
              'ffn_conv_b': ffn_conv_b[l], 'ffn_wd': ffn_wd[l]}
        xp, a1, b1, c1, d1, _ = decoder_layer(
            xp, pos_p, jnp.zeros((bp, POOL_STATE, D_POOL), dt), jnp.zeros((bp, CONV_WIDTH - 1, D_RNN), dt),
            jnp.zeros((bp, D_RNN), dt), jnp.zeros((bp, FFN_CONV - 1, D_FF), dt), lp)
        xs, a2, b2, c2, d2, v2 = decoder_layer(
            xs, pos_s, state_pool[l], state_rnn_conv[l], state_rnn_h[l], state_ffn_conv[l], lp)
        pool_p.append(a1); pool_s.append(a2)
        rc_p.append(b1); rc_s.append(b2)
        h_p.append(c1); h_s.append(c2)
        ff_p.append(d1); ff_s.append(d2)
        cv_s.append(v2)
    y_prompt = rmsnorm(xp, final_norm_g)
    y_sample = rmsnorm(xs, final_norm_g)
    return (y_prompt, y_sample,
            jnp.stack(pool_p), jnp.stack(pool_s),
            jnp.stack(rc_p), jnp.stack(rc_s),
            jnp.stack(h_p), jnp.stack(h_s),
            jnp.stack(ff_p), jnp.stack(ff_s),
            jnp.stack(cv_s))
```

```python
import numpy as np
from contextlib import ExitStack
import concourse.bass as bass
import concourse.mybir as mybir
from concourse.bass_utils import run_bass_kernel_spmd

F32 = mybir.dt.float32
BF16 = mybir.dt.bfloat16
AF = mybir.ActivationFunctionType
ALU = mybir.AluOpType
AX = mybir.AxisListType


class Dep:
    __slots__ = ("w", "r")

    def __init__(self):
        self.w = []
        self.r = []


class Buf(Dep):
    __slots__ = ("t", "a", "ch")

    def __init__(self, t, a=None, ch=None):
        super().__init__()
        self.t = t
        self.a = t[:] if a is None else a
        self.ch = ch


def V(ap):
    return Buf(None, ap)


class Sched:
    def __init__(self, nc, st):
        self.nc, self.st = nc, st
        self.eng = dict(pe=nc.tensor, act=nc.scalar, dve=nc.vector, pool=nc.gpsimd, sp=nc.sync)
        self.sems, self.cnt = {}, {}
        for k in self.eng:
            self.sems[k] = st.enter_context(nc.semaphore("sem_" + k))
            self.cnt[k] = 0
        self.waited = {k: {} for k in self.eng}
        self.finals = []
        self.nwait = 0
        self.nins = {k: 0 for k in self.eng}
        self.log = {k: [] for k in self.eng}

    def sb(self, name, shape, dtype):
        return Buf(self.st.enter_context(self.nc.sbuf_tensor(name, list(shape), dtype)))

    def ps(self, name):
        return Buf(self.st.enter_context(self.nc.psum_tensor(name, [128, 512], F32)))

    def chan(self, name):
        self.sems[name] = self.st.enter_context(self.nc.semaphore("sem_" + name))
        self.cnt[name] = 0
        return name

    def _wait(self, e, evs):
        need = {}
        for (k, v) in evs:
            if v > need.get(k, 0):
                need[k] = v
        wd = self.waited[e]
        for k, v in need.items():
            if wd.get(k, 0) >= v:
                continue
            if k == e and e == "pe":
                continue
            self.eng[e].wait_ge(self.sems[k], v)
            self.log[e].append(("w", k, v))
            self.nwait += 1
            wd[k] = v

    def _deps(self, reads, writes):
        evs = []
        for d in reads:
            evs += d.w
        for d in writes:
            evs += d.w
            evs += d.r
        return evs

    def _record(self, ev, reads, writes):
        for d in reads:
            d.r.append(ev)
            if len(d.r) > 24:
                d.r = _compact(d.r)
        for d in writes:
            d.w = [ev]
            d.r = []

    def op(self, e, fn, reads=(), writes=(), sig=True):
        self._wait(e, self._deps(reads, writes))
        ins = fn(self.eng[e])
        self.nins[e] += 1
        if sig:
            self.cnt[e] += 1
            ins.then_inc(self.sems[e], 1)
            self.log[e].append(("i", e, 1))
            ev = (e, self.cnt[e])
        else:
            ev = (e, self.cnt[e] + 1)
        self._record(ev, reads, writes)
        return ev

    def sbc(self, name, shape, dtype):
        b = self.sb(name, shape, dtype)
        b.ch = self.chan("c_" + name)
        return b

    def batch_end(self, ch, deps):
        for d in deps:
            d.w = [(ch, self.cnt[ch])]

    def dma(self, q, ch, out, in_, reads=(), writes=(), final=False, join=(), **kw):
        self._wait(q, self._deps(reads, writes))
        ins = self.eng[q].dma_start(out=out, in_=in_, **kw)
        self.nins[q] += 1
        self.cnt[ch] += 16
        ins.then_inc(self.sems[ch], 16)
        self.log[q].append(("i", ch, 16))
        ev = (ch, self.cnt[ch])
        self._record(ev, reads, writes)
        for d in join:
            d.w.append(ev)
        if final:
            self.finals.append(ev)
        return ev

    def finish(self):
        self._wait("sp", self.finals)
        for e in ("act", "dve", "pool", "pe"):
            self._wait(e, self.finals)


def _compact(evs):
    m = {}
    for k, v in evs:
        if v > m.get(k, 0):
            m[k] = v
    return list(m.items())


NL, D, KT, NB = 2, 1024, 8, 512
SEQ, TH = 2048, 1024
NSMP = 16
WIN = (2, 4, 8, 16)
EPS = 1e-6
C_POOL, C_RX, C_RG, C_U, C_V, C_G = 0, 512, 1536, 2560, 3072, 3584
PV_N1G, PV_N2G, PV_PSC, PV_RCW, PV_RCB, PV_LBA, PV_LBX, PV_LAM, PV_FCW, PV_FCB, PV_FNG = (
    0, 16, 32, 40, 104, 120, 136, 152, 168, 312, 360)
PV_ROWS = 384
NSLOT = 8
SLOTW = 2048


class Ring:
    def __init__(self, bufs):
        self.b, self.i = bufs, 0

    def get(self):
        b = self.b[self.i % len(self.b)]
        self.i += 1
        return b


NW = (NB, NB, NSMP)


class Ctx:
    def __init__(self, tok0, blocks):
        self.tok0, self.blocks = tok0, blocks


def build_program():
    nc = bass.Bass("TRN2", target_bir_lowering=False)

    def din(n, s):
        return nc.dram_tensor(n, list(s), F32, kind="ExternalInput").ap()

    def dout(n, s):
        return nc.dram_tensor(n, list(s), F32, kind="ExternalOutput").ap()

    x_prompt = din("x_prompt", [SEQ, D])
    x_sample = din("x_sample", [NSMP, D])
    st_pool = din("state_pool", [NL, NSMP, 15, 512])
    st_rc = din("state_rnn_conv", [NL, NSMP, 3, 1024])
    st_h = din("state_rnn_h", [NL, NSMP, 1024])
    st_ff = din("state_ffn_conv", [NL, NSMP, 2, 3072])
    w_in = din("w_in", [NL, 1024, 6656])
    pool_w = din("pool_w", [NL, 4, 128, 128])
    lru_wa = din("lru_wa", [NL, 8, 128, 128])
    lru_wx = din("lru_wx", [NL, 8, 128, 128])
    chunk_ws = din("chunk_ws", [NL, 4, 128, 128])
    chunk_bs = din("chunk_bs", [NL, 4, 128])
    chunk_vg = din("chunk_vnorm_g", [NL, 512])
    w_pa = din("w_pa", [NL, 512, 1024])
    w_pb = din("w_pb", [NL, 1024, 1024])
    w_pc = din("w_pc", [NL, 512, 1024])
    w_o = din("w_o", [NL, 1024, 1024])
    ffn_wg = din("ffn_wg", [NL, 1024, 3072])
    ffn_wu = din("ffn_wu", [NL, 1024, 3072])
    ffn_wd = din("ffn_wd", [NL, 3072, 1024])
    pvec = din("pvec", [PV_ROWS, 128])
    ident_d = din("ident", [128, 128])
    tril_d = din("tril", [128, 128])
    invcnt_d = din("invcnt", [1, 64])
    ws00_d = din("ws00", [1, 8])
    bs0_d = din("bs0", [1, 8])

    y_prompt = dout("y_prompt", [SEQ, D])
    y_sample = dout("y_sample", [NSMP, D])
    o_pool_p = dout("new_pool_prompt", [NL, 15, 512])
    o_pool_s = dout("new_pool_sample", [NL, NSMP, 15, 512])
    o_rc_p = dout("new_rconv_prompt", [NL, 3, 1024])
    o_rc_s = dout("new_rconv_sample", [NL, NSMP, 3, 1024])
    o_h_p = dout("new_h_prompt", [NL, 1, 1024])
    o_h_s = dout("new_h_sample", [NL, NSMP, 1024])
    o_ff_p = dout("new_ffn_prompt", [NL, 2, 3072])
    o_ff_s = dout("new_ffn_sample", [NL, NSMP, 2, 3072])
    o_cv_s = dout("new_chunk_v_sample", [NL, NSMP, 512])

    with ExitStack() as st:
        S = Sched(nc, st)
        xt = S.sb("xt", [128, 8, TH], F32)
        xnt = S.sb("xnt", [128, 8, TH], BF16)
        hft = S.sb("hft", [128, 24, TH], BF16)
        xst = S.sb("xst", [128, 8, NSMP], F32)
        xnst = S.sb("xnst", [128, 8, NSMP], BF16)
        hfst = S.sb("hfst", [128, 24, NSMP], BF16)
        x = [[V(xt.t[:, k, b * NB:(b + 1) * NB]) for b in range(2)] + [V(xst.t[:, k, :])] for k in range(8)]
        xn = [[V(xnt.t[:, k, b * NB:(b + 1) * NB]) for b in range(2)] + [V(xnst.t[:, k, :])] for k in range(8)]
        hf = [[V(hft.t[:, k, b * NB:(b + 1) * NB]) for b in range(2)] + [V(hfst.t[:, k, :])] for k in range(24)]
        ya, yb, yc, mrg = hf[0:4], hf[4:12], hf[12:16], hf[16:24]

        TW = 528
        T = {}
        for role, n in (("ext", 2), ("acc", 3), ("r", 2), ("i", 2), ("sq", 2), ("gg", 2)):
            T[role] = Ring([S.sb(f"t_{role}{i}", [128, TW], F32) for i in range(n)])
        T["bcb"] = Ring([S.sb(f"t_bcb{i}", [128, NB], BF16) for i in range(2)])
        TSm = {}
        tsm_t = S.sb("tsm", [128, 11, NSMP], F32)
        tsm_b = S.sb("tsmb", [128, 2, NSMP], BF16)
        k_ = 0
        for role, n in (("acc", 3), ("r", 2), ("i", 2), ("sq", 2), ("gg", 2)):
            TSm[role] = Ring([V(tsm_t.t[:, k_ + i, :]) for i in range(n)])
            k_ += n
        TSm["bcb"] = Ring([V(tsm_b.t[:, i, :]) for i in range(2)])
        T["sqn"] = Ring([S.sb(f"t_sqn{i}", [128, NB], BF16) for i in range(2)])
        T["vn"] = Ring([S.sb(f"t_vn{i}", [128, NB], BF16) for i in range(2)])
        ssq_t = S.sb("ssq", [128, 4], F32)
        T["ssq"] = Ring([V(ssq_t.t[:, i:i + 1]) for i in range(4)])
        P = Ring([S.ps(f"ps{i}") for i in range(8)])

        xstage = S.sbc("xstage", [128, D], F32)
        T["rows"] = Ring([S.sbc(f"rows{i}", [128, NB], F32) for i in range(2)])

        wring = S.sb("wring", [128, NSLOT * SLOTW], BF16)
        slots = [Buf(None, wring.t[:, i * SLOTW:(i + 1) * SLOTW], S.chan(f"c_wslot{i}")) for i in range(NSLOT)]
        wctr = [0]

        ident = S.sb("ident_sb", [128, 128], F32)
        tril = S.sb("tril_sb", [128, 128], F32)
        pv = S.sb("pv", [128, PV_ROWS], F32)
        clv = S.sb("clv", [128, 32], F32)
        hbv = S.sb("hbv", [128, 32], F32)
        vgb = S.sb("vgb", [128, NL, 512], F32)
        icn = S.sb("icn", [128, 64], F32)
        wsb = S.sb("wsb", [128, 8], F32)
        bsb = S.sb("bsb", [128, 8], F32)
        bsbt = S.sb("bsbt", [128, NL * 4, 128], F32)
        mhalf = S.sb("mhalf", [128, 1], F32)
        ones_bf = S.sb("ones_bf", [128, 128], BF16)
        poolw = S.sb("poolw", [128, NL * 4, 128], BF16)
        lruwa = S.sb("lruwa", [128, NL * 8, 128], BF16)
        lruwx = S.sb("lruwx", [128, NL * 8, 128], BF16)
        wsT = S.sb("wsT", [128, NL * 4, 128], BF16)
        cpt = S.sb("cpt", [128, 2, NL, 4, 15], F32)
        crt = S.sb("crt", [128, 2, NL, 8, 3], F32)
        cht = S.sb("cht", [128, NL, 8, 1], F32)
        cft = S.sb("cft", [128, 2, NL, 24, 2], F32)
        cp_v = [[[V(cpt.t[:, q, l, j, :]) for q in range(2)] for j in range(4)] for l in range(NL)]
        cr_v = [[[V(crt.t[:, q, l, j, :]) for q in range(2)] for j in range(8)] for l in range(NL)]
        ch_v = [[V(cht.t[:, l, j, :]) for j in range(8)] for l in range(NL)]
        cf_v = [[[V(cft.t[:, q, l, j, :]) for q in range(2)] for j in range(24)] for l in range(NL)]
        exs = S.sb("exs", [128, 4, NSMP, 16], F32)
        ex3 = S.sb("ex3", [128, 8, NSMP, 4], F32)
        h0t = S.sb("h0t", [128, 8, NSMP], F32)
        ex2 = S.sb("ex2", [128, 24, NSMP, 3], F32)
        exs_v = [V(exs.t[:, j]) for j in range(4)]
        ex3_v = [V(ex3.t[:, j]) for j in range(8)]
        h0_v = [V(h0t.t[:, j]) for j in range(8)]
        ex2_v = [V(ex2.t[:, j]) for j in range(24)]

        ch_misc = S.chan("misc")
        ch_d2d = S.chan("d2d")
        ch_cv = S.chan("cvout")

        def ACT(out, in_, func, reads, writes, **kw):
            S.op("act", lambda e: e.activation(out, in_, func, **kw), reads, writes)

        def TT(out, a, b, op, reads, writes, eng="dve"):
            S.op(eng, lambda e: e.tensor_tensor(out, a, b, op), reads, writes)

        def TS(out, a, s1, s2, op0, op1, reads, writes, eng="dve"):
            S.op(eng, lambda e: e.tensor_scalar(out, a, s1, s2, op0, op1), reads, writes)

        def STT(out, a, s, b, op0, op1, reads, writes):
            S.op("dve", lambda e: e.scalar_tensor_tensor(out, a, s, b, op0, op1), reads, writes)

        def CP(out, in_, reads, writes, eng="dve"):
            if eng == "act":
                S.op("act", lambda e: e.activation(out, in_, AF.Copy), reads, writes)
            else:
                S.op(eng, lambda e: e.tensor_copy(out, in_), reads, writes)

        def MM(psb, out, l, r, start, stop, reads, sig):
            S.op("pe", lambda e: e.matmul(out, l, r, start=start, stop=stop), reads, [psb], sig=sig)

        def mm(psb, out, pairs, reads):
            n = len(pairs)
            for i, pr in enumerate(pairs):
                last = i == n - 1
                rd = list(pr[2]) if len(pr) > 2 else []
                if i == 0 or last:
                    rd = rd + list(reads)
                MM(psb, out, pr[0], pr[1], i == 0, last, rd, last)

        def TR(psb, out, in_, idn, reads, sig=True):
            S.op("pe", lambda e: e.transpose(out, in_, idn), list(reads) + [ident], [psb], sig=sig)

        def pvc(i):
            return pv.t[:, i:i + 1]

        def fetch(pieces):
            slot = slots[wctr[0] % NSLOT]
            wctr[0] += 1
            views, off = [], 0
            for pi, p in enumerate(pieces):
                K, ncol = p.shape
                kt = K // 128
                v = slot.a[:, off:off + kt * ncol].rearrange("p (k c) -> p k c", k=kt)
                S.dma("pool", slot.ch, v, p.rearrange("(k p) c -> p k c", p=128),
                      writes=[slot] if pi == 0 else (), join=() if pi == 0 else [slot])
                views.append(v)
                off += kt * ncol
            assert off <= SLOTW
            return slot, views

        def fetch2(piece):
            if wctr[0] % 2:
                wctr[0] += 1
            i0 = wctr[0] % NSLOT
            wctr[0] += 2
            sa, sb_ = slots[i0], slots[i0 + 1]
            v = wring.t[:, i0 * SLOTW:(i0 + 2) * SLOTW].rearrange("p (k c) -> p k c", k=8)
            S.dma("pool", sa.ch, v, piece.rearrange("(k p) c -> p k c", p=128), writes=[sa, sb_])
            return [sa, sb_], v

        def emit_rows(srcs, n, dst, reads, final=True):
            ps = P.get()
            for i, s_ in enumerate(srcs):
                TR(ps, ps.a[0:n, i * 128:(i + 1) * 128], s_, ident.a, reads, sig=(i == len(srcs) - 1))
            rw = T["rows"].get()
            CP(rw.a[0:n, :], ps.a[0:n, :], [ps], [rw])
            S.dma("sp", rw.ch, dst, rw.a[0:n, :], reads=[rw], final=final)

        class StageA:
            deps = [xstage]

            @staticmethod
            def load(src_ap, n):
                S.dma("sp", xstage.ch, xstage.a[0:n, :], src_ap, writes=[xstage])

            @staticmethod
            def ap(kt, n):
                return xstage.a[0:n, kt * 128:(kt + 1) * 128]

        class StageB:
            deps = list(T["rows"].b)

            @staticmethod
            def load(src_ap, n):
                for h_ in range(2):
                    rb = T["rows"].b[h_]
                    S.dma("sp", rb.ch, rb.a[0:n, :], src_ap[:, h_ * 512:(h_ + 1) * 512], writes=[rb])

            @staticmethod
            def ap(kt, n):
                return T["rows"].b[kt // 4].a[0:n, (kt % 4) * 128:(kt % 4 + 1) * 128]

        stages = [StageA, StageB]

        setup = [ident, tril, vgb, icn, wsb, bsb]
        S.dma("sp", ch_misc, ident.a, ident_d, writes=[ident])
        S.dma("sp", ch_misc, tril.a, tril_d, writes=[tril])
        for l in range(NL):
            S.dma("sp", ch_misc, vgb.t[:, l, :], chunk_vg[l].partition_broadcast(128), writes=[vgb] if l == 0 else (),
                  join=() if l == 0 else [vgb])
        S.dma("sp", ch_misc, icn.a, invcnt_d[0].partition_broadcast(128), writes=[icn])
        S.dma("sp", ch_misc, wsb.a, ws00_d[0].partition_broadcast(128), writes=[wsb])
        S.dma("sp", ch_misc, bsb.a, bs0_d[0].partition_broadcast(128), writes=[bsb])
        S.batch_end(ch_misc, setup)
        rw0 = T["rows"].get()
        S.dma("sp", rw0.ch, rw0.a[:, 0:384].rearrange("p (r c) -> p r c", r=3),
              pvec.rearrange("(r p) c -> p r c", p=128), writes=[rw0])
        S.dma("sp", xstage.ch, xstage.a.rearrange("p (g j) -> p g j", g=8),
              chunk_ws.rearrange("l g i j -> i (l g) j"), writes=[xstage])
        ch_miscw = S.chan("miscw")
        for dst_, src_ in ((poolw, pool_w), (lruwa, lru_wa), (lruwx, lru_wx)):
            S.dma("pool", ch_miscw, dst_.a, src_.rearrange("l g i j -> i (l g) j"), writes=[dst_])
        S.batch_end(ch_miscw, [poolw, lruwa, lruwx])
        S.op("dve", lambda e: e.memset(ones_bf.a, 1.0), (), [ones_bf])
        S.op("dve", lambda e: e.memset(mhalf.a, -0.5), (), [mhalf])
        for cb_ in (cpt, crt, cht, cft):
            S.op("dve", lambda e, cb_=cb_: e.memset(cb_.a, 0.0), (), [cb_])
        for l in range(NL):
            for lst, n in ((cp_v, 4), (cr_v, 8), (cf_v, 24)):
                for j in range(n):
                    for q in range(2):
                        lst[l][j][q].w = [("dve", S.cnt["dve"])]
            for j in range(8):
                ch_v[l][j].w = [("dve", S.cnt["dve"])]
        ps = P.get()
        for r_ in range(3):
            TR(ps, ps.a[:, r_ * 128:(r_ + 1) * 128], rw0.a[:, r_ * 128:(r_ + 1) * 128], ident.a, [rw0], sig=(r_ == 2))
        CP(pv.a, ps.a[:, 0:PV_ROWS], [ps], [pv])
        ACT(clv.t[:, 0:16], pv.t[:, PV_LAM:PV_LAM + 16], AF.Exp, [pv], [clv], scale=-1.0)
        ACT(clv.t[:, 0:16], clv.t[:, 0:16], AF.Ln, [clv], [clv], bias=1.0)
        TS(clv.t[:, 16:32], clv.t[:, 0:16], -4.0, None, ALU.mult, ALU.bypass, [clv], [clv])
        TS(clv.t[:, 0:16], clv.t[:, 0:16], -8.0, None, ALU.mult, ALU.bypass, [clv], [clv])
        TS(hbv.a, pv.t[:, PV_LBA:PV_LBA + 32], 0.5, None, ALU.mult, ALU.bypass, [pv], [hbv])
        for lg in range(8):
            blk_ = xstage.a[:, lg * 128:(lg + 1) * 128]
            TT(blk_, blk_, tril.a, ALU.mult, [xstage, tril], [xstage])
        for h_ in range(2):
            ps = P.get()
            for i in range(4):
                lg = h_ * 4 + i
                TR(ps, ps.a[:, i * 128:(i + 1) * 128], xstage.a[:, lg * 128:(lg + 1) * 128], ident.a, [xstage], sig=(i == 3))
            CP(wsT.t[:, h_ * 4:(h_ + 1) * 4, :], ps.a.rearrange("p (a b) -> p a b", a=4), [ps], [wsT])
        S.dma("sp", ch_misc, bsbt.a.rearrange("p a b -> p (a b)"), chunk_bs.rearrange("l g i -> (l g i)").partition_broadcast(128),
              writes=[bsbt])

        def load_x(c):
            if 2 in c.blocks:
                S.dma("sp", xstage.ch, xstage.a[0:NSMP, :], x_sample, writes=[xstage])
                for g in range(2):
                    ps = P.get()
                    for i in range(4):
                        kt = g * 4 + i
                        TR(ps, ps.a[:, i * NSMP:(i + 1) * NSMP], xstage.a[0:NSMP, kt * 128:(kt + 1) * 128],
                           ident.a[0:NSMP, 0:NSMP], [xstage], sig=(i == 3))
                    CP(xst.t[:, g * 4:(g + 1) * 4, :], ps.a[:, 0:4 * NSMP].rearrange("p (a b) -> p a b", a=4),
                       [ps], [x[g * 4 + i][2] for i in range(4)])
            for tt in range(8):
                stg = stages[tt % 2]
                stg.load(x_prompt[c.tok0 + tt * 128:c.tok0 + (tt + 1) * 128, :], 128)
                blk, c0 = tt // 4, (tt % 4) * 128
                for g in range(2):
                    ps = P.get()
                    for i in range(4):
                        kt = g * 4 + i
                        TR(ps, ps.a[:, i * 128:(i + 1) * 128], stg.ap(kt, 128), ident.a, stg.deps, sig=(i == 3))
                    CP(xt.t[:, g * 4:(g + 1) * 4, blk * NB + c0:blk * NB + c0 + 128],
                       ps.a.rearrange("p (a b) -> p a b", a=4), [ps], [x[g * 4 + i][blk] for i in range(4)],
                       eng="act" if g == 0 else "dve")
                if tt % 4 == 3:
                    norm_blk(c, PV_N1G, blk)
            if 2 in c.blocks:
                norm_blk(c, PV_N1G, 2)

        def rstd_blk(c, blk):
            N = NW[blk]
            ps = P.get()
            for kt in range(8):
                sq = T["sqn"].get()
                ACT(sq.a[:, :N], x[kt][blk].a[:, :N], AF.Square, [x[kt][blk]], [sq])
                MM(ps, ps.a[:, :N], ones_bf.a, sq.a[:, :N], kt == 0, kt == 7, [sq, ones_bf], True)
            rs = T["r"].get()
            ACT(rs.a[:, :N], ps.a[:, :N], AF.Sqrt, [ps], [rs], bias=EPS, scale=1.0 / D)
            S.op("dve", lambda e: e.reciprocal(rs.a[:, :N], rs.a[:, :N]), [rs], [rs])
            return rs

        def norm_blk(c, g0, blk):
            N = NW[blk]
            rs = rstd_blk(c, blk)
            for kt in range(8):
                STT(xn[kt][blk].a[:, :N], x[kt][blk].a[:, :N], pvc(g0 + kt), rs.a[:, :N], ALU.mult, ALU.mult,
                    [x[kt][blk], rs, pv], [xn[kt][blk]])

        def norm_to_xn(c, g0):
            for blk in c.blocks:
                norm_blk(c, g0, blk)

        def residual_stage(c, ncg, make_group, after_blk, tail=2):
            for cg in range(ncg - tail):
                grp = make_group(cg)
                for m2 in range(2):
                    for blk in c.blocks:
                        grp(m2, blk)
            grps = [make_group(cg) for cg in range(ncg - tail, ncg)]
            for bi, blk in enumerate(c.blocks):
                first = True
                for grp in grps:
                    for m2 in range(2):
                        grp(m2, blk)
                        if bi > 0 and first:
                            after_blk(c.blocks[bi - 1])
                        first = False
            after_blk(c.blocks[-1])

        def xn_pairs(wv, cs, blk, N):
            return [(wv[:, kt, cs], xn[kt][blk].a[:, :N], [xn[kt][blk]]) for kt in range(8)]

        def xn_reads(blk):
            return []

        def run_pipelined(items, phase_fns, ascending=False):
            n, npz = len(items), len(phase_fns)
            states = [dict() for _ in items]
            import types
            for step in range(n + npz - 1):
                gens = []
                for ph in (range(npz) if ascending else range(npz - 1, -1, -1)):
                    it = step - ph
                    if 0 <= it < n:
                        g = phase_fns[ph](items[it], states[it])
                        if isinstance(g, types.GeneratorType):
                            gens.append(g)
                while gens:
                    for g in list(gens):
                        try:
                            next(g)
                        except StopIteration:
                            gens.remove(g)

        def stage_pool(c, l):
            items = [(j, blk) for jp in range(0, 4, 2) for blk in c.blocks for j in (jp, jp + 1)]
            sl = {}

            def wts(j):
                g2 = j // 2
                if g2 not in sl:
                    sl[g2] = fetch([w_in[l][:, C_POOL + g2 * 256:C_POOL + (g2 + 1) * 256]])
                return sl[g2][0], sl[g2][1][0]

            def p0(it, s):
                j, blk = it
                N = NW[blk]
                slot, wv = wts(j)
                ps = P.get()
                mm(ps, ps.a[:, :N], xn_pairs(wv, slice((j % 2) * 128, (j % 2 + 1) * 128), blk, N), [slot])
                if blk != 2:
                    ext = T["ext"].get()
                    cvi, cvo = cp_v[l][j][blk], cp_v[l][j][1 - blk]
                    CP(ext.a[:, 0:15], cvi.a, [cvi], [ext])
                    CP(ext.a[:, 15:527], ps.a, [ps], [ext], eng="act")
                    CP(cvo.a, ps.a[:, 512 - 15:512], [ps], [cvo], eng="act")
                    s["ext"] = ext
                else:
                    CP(exs.t[:, j, :, 15], ps.a[:, :N], [ps], [exs_v[j]], eng="act")

            def p1(it, s):
                j, blk = it
                N, w = NW[blk], WIN[j]
                d = T["bcb"].get()
                s["d"] = d
                if blk != 2:
                    ext = s["ext"]
                    top = j + 1
                    los = {top: 15}
                    for k in range(top, 1, -1):
                        los[k - 1] = los[k] - 2 ** (k - 1)
                    bufs = [T["acc"].get(), T["r"].get()]
                    prev = ext
                    for k in range(1, top + 1):
                        lo, sh = los[k], 2 ** (k - 1)
                        cur = bufs[(k - 1) % 2]
                        TT(cur.a[:, lo:527], prev.a[:, lo:527], prev.a[:, lo - sh:527 - sh], ALU.add, [prev], [cur])
                        prev = cur
                    STT(d.a, prev.a[:, 15:527], 1.0 / w, ext.a[:, 15:527], ALU.mult, ALU.subtract, [prev, ext], [d])
                    if c.tok0 == 0 and blk == 0:
                        tm = T["sq"].get()
                        TT(tm.a[:, 0:16], prev.a[:, 15:31], icn.t[:, j * 16:(j + 1) * 16], ALU.mult, [prev, icn], [tm])
                        TT(d.a[:, 0:16], tm.a[:, 0:16], ext.a[:, 15:31], ALU.subtract, [tm, ext], [d])
                else:
                    ev = exs_v[j]
                    sr = T["acc"].get()
                    S.op("dve", lambda e: e.tensor_reduce(sr.a[:, :N], exs.t[:, j, :, 16 - w:16], AX.X, ALU.add), [ev], [sr])
                    STT(d.a[:, :N], sr.a[:, :N], 1.0 / w, exs.t[:, j, :, 15], ALU.mult, ALU.subtract, [sr, ev], [d])

            def p2(it, s):
                j, blk = it
                N, d = NW[blk], s["d"]
                ps2 = P.get()
                MM(ps2, ps2.a[:, :N], poolw.t[:, l * 4 + j, :], d.a[:, :N], True, True, [poolw, d], True)
                ACT(ya[j][blk].a[:, :N], ps2.a[:, :N], AF.Identity, [ps2, pv], [ya[j][blk]], scale=pvc(PV_PSC + l * 4 + j))

            run_pipelined(items, [p0, p1, p2])
            if 2 in c.blocks:
                emit_rows([exs.t[:, j, :, 15] for j in range(4)], NSMP, o_pool_s[l, :, 14, :], exs_v)

        def stage_rnn(c, l):
            items = [(j, blk) for jp in range(0, 8, 2) for blk in (0, 1) for j in (jp, jp + 1)]
            has_s = 2 in c.blocks
            sl = {}

            def wts(j):
                jg = j // 2
                if jg not in sl:
                    sx, (wxv,) = fetch([w_in[l][:, C_RX + jg * 256:C_RX + (jg + 1) * 256]])
                    sg, (wgv,) = fetch([w_in[l][:, C_RG + jg * 256:C_RG + (jg + 1) * 256]])
                    sl[jg] = (sx, wxv, sg, wgv)
                return sl[jg]

            def tmp(role, blk):
                return (TSm if blk == 2 else T)[role].get()

            def subs(it, s):
                j, blk = it
                out = [(j, blk, s)]
                if blk == 1 and has_s:
                    out.append((j, 2, s.setdefault("smp", {})))
                return out

            def p0(it, s):
                late = []
                for j, blk, st_ in subs(it, s):
                    N = NW[blk]
                    sx, wxv, sg, wgv = wts(j)
                    cs = slice((j % 2) * 128, (j % 2 + 1) * 128)
                    ps = P.get()
                    mm(ps, ps.a[:, :N], xn_pairs(wxv, cs, blk, N), [sx])
                    if blk == 2:
                        CP(ex3.t[:, j, :, 3], ps.a[:, :N], [ps], [ex3_v[j]], eng="act")
                    else:
                        late.append((j, blk, st_, ps))
                for _ in range(10):
                    yield
                for j, blk, st_, ps in late:
                    ext = T["ext"].get()
                    cvi, cvo = cr_v[l][j][blk], cr_v[l][j][1 - blk]
                    CP(ext.a[:, 0:3], cvi.a, [cvi], [ext])
                    CP(ext.a[:, 3:515], ps.a, [ps], [ext])
                    CP(cvo.a, ps.a[:, 512 - 3:512], [ps], [cvo])
                    st_["ext"] = ext

            def p1(it, s):
                first = True
                for j, blk, st_ in subs(it, s):
                    N = NW[blk]
                    wk = [pvc(PV_RCW + (l * 4 + k) * 8 + j) for k in range(4)]
                    cb = pvc(PV_RCB + l * 8 + j)
                    acc, bcb = tmp("acc", blk), tmp("bcb", blk)
                    st_.update(acc=acc, bcb=bcb)
                    if first:
                        yield
                        first = False
                    if blk != 2:
                        ext = st_["ext"]
                        taps = [ext.a[:, k:k + 512] for k in range(4)]
                        rd = [ext]
                    else:
                        taps = [ex3.t[:, j, :, k] for k in range(4)]
                        rd = [ex3_v[j]]
                    TS(acc.a[:, :N], taps[3], wk[3], cb, ALU.mult, ALU.add, rd + [pv], [acc])
                    yield
                    for k in range(3):
                        STT(acc.a[:, :N], taps[k], wk[k], acc.a[:, :N], ALU.mult, ALU.add, rd + [acc, pv], [acc])
                        yield
                    CP(bcb.a[:, :N], acc.a[:, :N], [acc], [bcb])

            def p2(it, s):
                for j, blk, st_ in subs(it, s):
                    N, bcb = NW[blk], st_["bcb"]
                    sx, wxv, sg, wgv = wts(j)
                    cs = slice((j % 2) * 128, (j % 2 + 1) * 128)
                    lj = l * 8 + j
                    if blk != 2:
                        psg, psr, psi = P.get(), P.get(), P.get()
                        og, orr, oi = psg.a[:, :N], psr.a[:, :N], psi.a[:, :N]
                    else:
                        psg = psr = psi = P.get()
                        og, orr, oi = psg.a[:, 0:N], psg.a[:, N:2 * N], psg.a[:, 2 * N:3 * N]
                    mm(psg, og, xn_pairs(wgv, cs, blk, N), [sg])
                    MM(psr, orr, lruwa.t[:, lj, :], bcb.a[:, :N], True, True, [lruwa, bcb], True)
                    MM(psi, oi, lruwx.t[:, lj, :], bcb.a[:, :N], True, True, [lruwx, bcb], True)
                    st_.update(bg=psg, br=psr, bi=psi, og=og, orr=orr, oi=oi)

            def p3(it, s):
                ss = subs(it, s)
                for j, blk, st_ in ss:
                    N, lj = NW[blk], l * 8 + j
                    r, ii, sq, gg = tmp("r", blk), tmp("i", blk), tmp("sq", blk), tmp("gg", blk)
                    st_.update(r=r, ii=ii, sq=sq, gg=gg)
                    ACT(gg.a[:, :N], st_["og"], AF.Gelu_apprx_tanh, [st_["bg"]], [gg])
                    ACT(r.a[:, :N], st_["orr"], AF.Tanh, [st_["br"], hbv], [r], bias=hbv.t[:, lj:lj + 1], scale=0.5)
                    ACT(ii.a[:, :N], st_["oi"], AF.Tanh, [st_["bi"], hbv], [ii], bias=hbv.t[:, 16 + lj:17 + lj], scale=0.5)
                for j, blk, st_ in ss:
                    N, lj, r, sq = NW[blk], l * 8 + j, st_["r"], st_["sq"]
                    ACT(sq.a[:, :N], r.a[:, :N], AF.Exp, [r, clv], [sq], scale=clv.t[:, lj:lj + 1], bias=clv.t[:, lj:lj + 1])
                    ACT(r.a[:, :N], r.a[:, :N], AF.Exp, [r, clv], [r], scale=clv.t[:, 16 + lj:17 + lj], bias=clv.t[:, 16 + lj:17 + lj])
                    ACT(sq.a[:, :N], sq.a[:, :N], AF.Ln, [sq], [sq], scale=-1.0, bias=1.0)
                    ACT(sq.a[:, :N], sq.a[:, :N], AF.Exp, [sq], [sq], scale=0.5)

            def p4(it, s):
                for j, blk, st_ in subs(it, s):
                    N, acc, r, ii, sq, gg = NW[blk], st_["acc"], st_["r"], st_["ii"], st_["sq"], st_["gg"]
                    STT(ii.a[:, :N], ii.a[:, :N], 1.0, acc.a[:, :N], ALU.add, ALU.mult, [ii, acc], [ii])
                    yield
                    STT(ii.a[:, :N], ii.a[:, :N], 0.5, sq.a[:, :N], ALU.mult, ALU.mult, [ii, sq], [ii])
                    yield
                    h = sq
                    if blk != 2:
                        hv = ch_v[l][j]
                        S.op("dve", lambda e, h=h, r=r, ii=ii, hv=hv, N=N: e.tensor_tensor_scan(h.a[:, :N], r.a[:, :N], ii.a[:, :N], hv.a,
                                                                                              ALU.mult, ALU.add), [r, ii, hv], [h])
                        yield
                        CP(hv.a, h.a[:, N - 1:N], [h], [hv])
                    else:
                        TT(h.a[:, :N], r.a[:, :N], h0t.t[:, j, :], ALU.mult, [r, h0_v[j]], [h])
                        TT(h.a[:, :N], h.a[:, :N], ii.a[:, :N], ALU.add, [h, ii], [h])
                        CP(h0t.t[:, j, :], h.a[:, :N], [h], [h0_v[j]])
                    yield
                    TT(yb[j][blk].a[:, :N], gg.a[:, :N], h.a[:, :N], ALU.mult, [gg, h], [yb[j][blk]])

            run_pipelined(items, [p0, p1, p2, p3, p4])
            if has_s:
                for g in range(2):
                    emit_rows([ex3.t[:, g * 4 + i, :, 3] for i in range(4)], NSMP, o_rc_s[l, :, 2, g * 512:(g + 1) * 512], ex3_v)
                    emit_rows([h0t.t[:, g * 4 + i, :] for i in range(4)], NSMP, o_h_s[l, :, g * 512:(g + 1) * 512], h0_v)

        def rstd_pow(ssq, n):
            S.op("pool", lambda e: e.tensor_scalar(ssq.a[0:n, :], ssq.a[0:n, :], 1.0 / 512, EPS, ALU.mult, ALU.add), [ssq], [ssq])
            S.op("pool", lambda e: e.tensor_tensor(ssq.a[0:n, :], ssq.a[0:n, :], mhalf.a[0:n, :], ALU.pow), [ssq, mhalf], [ssq])

        def stage_chunk(c, l):
            svd, wvv = fetch2(w_in[l][:, C_V:C_V + 512])
            sus = [fetch([w_in[l][:, C_U + h_ * 256:C_U + (h_ + 1) * 256]]) for h_ in range(2)]

            def u_pairs(g, blk, N):
                return xn_pairs(sus[g // 2][1][0], slice((g % 2) * 128, (g % 2 + 1) * 128), blk, N), [sus[g // 2][0]]
            if 2 in c.blocks:
                N = NSMP
                ps = P.get()
                mm(ps, ps.a[0:N, :], [(xn[kt][2].a[:, 0:N], wvv[:, kt, :], [xn[kt][2]]) for kt in range(8)], svd)
                vg, junk, ssq, vnf = T["r"].get(), T["sq"].get(), T["ssq"].get(), T["acc"].get()
                ACT(vg.a[0:N, :512], ps.a[0:N, :], AF.Gelu_apprx_tanh, [ps], [vg])
                ACT(junk.a[0:N, :512], vg.a[0:N, :512], AF.Square, [vg], [junk, ssq], accum_out=ssq.a[0:N, :])
                rstd_pow(ssq, N)
                STT(vnf.a[0:N, :512], vg.a[0:N, :512], ssq.a[0:N, :], vgb.t[0:N, l, :], ALU.mult, ALU.mult, [vg, ssq, vgb], [vnf])
                S.dma("sp", ch_cv, o_cv_s[l], vnf.a[0:N, :512], reads=[vnf], final=True)
                for g in range(4):
                    pst = P.get()
                    TR(pst, pst.a[:, 0:N], vnf.a[0:N, g * 128:(g + 1) * 128], ident.a[0:N, 0:N], [vnf])
                    mixs = T["i"].get()
                    lg = l * 4 + g
                    ACT(mixs.a[:, :N], pst.a[:, :N], AF.Identity, [pst, wsb, bsb], [mixs], scale=wsb.t[:, lg:lg + 1], bias=bsb.t[:, lg:lg + 1])
                    psu = P.get()
                    prs, rds = u_pairs(g, 2, N)
                    mm(psu, psu.a[:, :N], prs, rds)
                    ug = T["gg"].get()
                    ACT(ug.a[:, :N], psu.a[:, :N], AF.Gelu_apprx_tanh, [psu], [ug])
                    TT(yc[g][2].a[:, :N], ug.a[:, :N], mixs.a[:, :N], ALU.mult, [ug, mixs], [yc[g][2]])
            mixb = P.b[0:4]
            PG = Ring(P.b[4:8])
            for blk in range(2):
                items = [("v", c4) for c4 in range(4)] + [("u", g) for g in range(4)]

                def q0(it, s, blk=blk):
                    kind, i = it
                    ps = PG.get()
                    if kind == "v":
                        tc = slice(i * 128, (i + 1) * 128)
                        mm(ps, ps.a, [(xn[kt][blk].a[:, tc], wvv[:, kt, :], [xn[kt][blk]]) for kt in range(8)], svd)
                        vg, junk, ssq = T["r"].get(), T["sq"].get(), T["ssq"].get()
                        s.update(vg=vg, ssq=ssq)
                        ACT(vg.a[:, :512], ps.a, AF.Gelu_apprx_tanh, [ps], [vg])
                        ACT(junk.a[:, :512], vg.a[:, :512], AF.Square, [vg], [junk, ssq], accum_out=ssq.a)
                    else:
                        prs, rds = u_pairs(i, blk, 512)
                        mm(ps, ps.a, prs, rds)
                        ug = T["gg"].get()
                        s["ug"] = ug
                        ACT(ug.a[:, :512], ps.a, AF.Gelu_apprx_tanh, [ps], [ug])

                def q1(it, s, blk=blk):
                    kind, i = it
                    if kind == "v":
                        tc = slice(i * 128, (i + 1) * 128)
                        vg, ssq, vn = s["vg"], s["ssq"], T["vn"].get()
                        rstd_pow(ssq, 128)
                        STT(vn.a, vg.a[:, :512], ssq.a, vgb.t[:, l, :], ALU.mult, ALU.mult, [vg, ssq, vgb], [vn])
                        for g in range(4):
                            lg = l * 4 + g
                            MM(mixb[g], mixb[g].a[:, tc], vn.a[:, g * 128:(g + 1) * 128], wsT.t[:, lg, :], True, True, [vn, wsT], g == 3)
                    else:
                        ug, tb = s["ug"], T["i"].get()
                        lg = l * 4 + i
                        TT(tb.a[:, :512].rearrange("p (a b) -> p a b", a=4), mixb[i].a.rearrange("p (a b) -> p a b", a=4),
                           bsbt.t[:, lg, :].unsqueeze(1).to_broadcast([128, 4, 128]), ALU.add, [mixb[i], bsbt], [tb])
                        TT(yc[i][blk].a, ug.a[:, :512], tb.a[:, :512], ALU.mult, [ug, tb], [yc[i][blk]])

                run_pipelined(items, [q0, q1], ascending=True)


        def stage_merge(c, l):
            for mg in range(4):
                c0 = mg * 256
                sA, (gAv,) = fetch([w_in[l][:, C_G + c0:C_G + c0 + 256]])
                sP, (pav, pcv) = fetch([w_pa[l][:, c0:c0 + 256], w_pc[l][:, c0:c0 + 256]])
                sB, (gBv,) = fetch([w_in[l][:, C_G + 1024 + c0:C_G + 1024 + c0 + 256]])
                sPb, (pbv,) = fetch([w_pb[l][:, c0:c0 + 256]])
                sC, (gCv,) = fetch([w_in[l][:, C_G + 2048 + c0:C_G + 2048 + c0 + 256]])
                plan = [(gAv, sA, pav, sP, ya, 4), (gBv, sB, pbv, sPb, yb, 8), (gCv, sC, pcv, sP, yc, 4)]
                for m2 in range(2):
                    m = mg * 2 + m2
                    cs = slice(m2 * 128, (m2 + 1) * 128)
                    for blk in c.blocks:
                        N = NW[blk]
                        acc, tmp = T["acc"].get(), T["i"].get()
                        for bi, (gv, gs, pw, pslot, ys, nk) in enumerate(plan):
                            psg = P.get()
                            mm(psg, psg.a[:, :N], xn_pairs(gv, cs, blk, N), [gs] + xn_reads(blk))
                            gt = T["gg"].get()
                            ACT(gt.a[:, :N], psg.a[:, :N], AF.Sigmoid, [psg], [gt])
                            psp = P.get()
                            mm(psp, psp.a[:, :N], [(pw[:, kt, cs], ys[kt][blk].a[:, :N], [ys[kt][blk]]) for kt in range(nk)], [pslot])
                            if bi == 0:
                                TT(acc.a[:, :N], gt.a[:, :N], psp.a[:, :N], ALU.mult, [gt, psp], [acc])
                            else:
                                TT(tmp.a[:, :N], gt.a[:, :N], psp.a[:, :N], ALU.mult, [gt, psp], [tmp])
                                if bi == 1:
                                    TT(acc.a[:, :N], acc.a[:, :N], tmp.a[:, :N], ALU.add, [acc, tmp], [acc])
                                else:
                                    TT(mrg[m][blk].a[:, :N], acc.a[:, :N], tmp.a[:, :N], ALU.add, [acc, tmp], [mrg[m][blk]])

        def stage_wo(c, l, after_blk):
            def make_group(cg):
                so, (wov,) = fetch([w_o[l][:, cg * 256:(cg + 1) * 256]])

                def grp(m4, blk):
                    m, N = cg * 2 + m4, NW[blk]
                    ps = P.get()
                    mm(ps, ps.a[:, :N],
                       [(wov[:, kt, m4 * 128:(m4 + 1) * 128], mrg[kt][blk].a[:, :N], [mrg[kt][blk]]) for kt in range(8)], [so])
                    TT(x[m][blk].a[:, :N], x[m][blk].a[:, :N], ps.a[:, :N], ALU.add, [x[m][blk], ps], [x[m][blk]])
                return grp
            residual_stage(c, 4, make_group, after_blk)

        def stage_ffn_up(c, l):
            items = [(t, blk) for tp in range(0, 24, 2) for blk in c.blocks for t in (tp, tp + 1)]
            sl = {}

            def wts(t):
                cg = t // 2
                if cg not in sl:
                    sg, (wgv,) = fetch([ffn_wg[l][:, cg * 256:(cg + 1) * 256]])
                    su, (wuv,) = fetch([ffn_wu[l][:, cg * 256:(cg + 1) * 256]])
                    sl[cg] = (sg, wgv, su, wuv)
                return sl[cg]

            def p0(it, s):
                t, blk = it
                N = NW[blk]
                sg, wgv, su, wuv = wts(t)
                cs = slice((t % 2) * 128, (t % 2 + 1) * 128)
                ps = P.get()
                mm(ps, ps.a[:, :N], xn_pairs(wgv, cs, blk, N), [sg] + xn_reads(blk))
                if blk != 2:
                    ext = T["ext"].get()
                    cvi, cvo = cf_v[l][t][blk], cf_v[l][t][1 - blk]
                    CP(ext.a[:, 0:2], cvi.a, [cvi], [ext])
                    CP(ext.a[:, 2:514], ps.a, [ps], [ext], eng="act")
                    CP(cvo.a, ps.a[:, 512 - 2:512], [ps], [cvo], eng="act")
                    s["ext"] = ext
                else:
                    CP(ex2.t[:, t, :, 2], ps.a[:, :N], [ps], [ex2_v[t]], eng="act")

            def p1(it, s):
                t, blk = it
                N = NW[blk]
                wk = [pvc(PV_FCW + (l * 3 + k) * 24 + t) for k in range(3)]
                cb = pvc(PV_FCB + l * 24 + t)
                acc = T["acc"].get()
                s["acc"] = acc
                if blk != 2:
                    ext = s["ext"]
                    TS(acc.a[:, :N], ext.a[:, 2:514], wk[2], cb, ALU.mult, ALU.add, [ext, pv], [acc])
                    for k in range(2):
                        STT(acc.a[:, :N], ext.a[:, k:k + 512], wk[k], acc.a[:, :N], ALU.mult, ALU.add, [ext, acc, pv], [acc])
                else:
                    ev = ex2_v[t]
                    TS(acc.a[:, :N], ex2.t[:, t, :, 2], wk[2], cb, ALU.mult, ALU.add, [ev, pv], [acc])
                    for k in range(2):
                        STT(acc.a[:, :N], ex2.t[:, t, :, k], wk[k], acc.a[:, :N], ALU.mult, ALU.add, [ev, acc, pv], [acc])

            def p2(it, s):
                t, blk = it
                N, acc = NW[blk], s["acc"]
                sg, wgv, su, wuv = wts(t)
                cs = slice((t % 2) * 128, (t % 2 + 1) * 128)
                ACT(acc.a[:, :N], acc.a[:, :N], AF.Gelu_apprx_tanh, [acc], [acc])
                psu = P.get()
                mm(psu, psu.a[:, :N], xn_pairs(wuv, cs, blk, N), [su] + xn_reads(blk))
                s["psu"] = psu

            def p3(it, s):
                t, blk = it
                N, acc, psu = NW[blk], s["acc"], s["psu"]
                TT(hf[t][blk].a[:, :N], acc.a[:, :N], psu.a[:, :N], ALU.mult, [acc, psu], [hf[t][blk]])
                if blk == c.blocks[-1] and t % 4 == 3 and 2 in c.blocks:
                    cg = t // 4
                    emit_rows([ex2.t[:, cg * 4 + i, :, 2] for i in range(4)], NSMP, o_ff_s[l, :, 1, cg * 512:(cg + 1) * 512],
                              ex2_v[cg * 4:cg * 4 + 4])

            run_pipelined(items, [p0, p1, p2, p3])


        def stage_ffn_down(c, l, after_blk):
            def make_group(cg):
                fs = [fetch([ffn_wd[l][kc * 1024:(kc + 1) * 1024, cg * 256:(cg + 1) * 256]]) for kc in range(3)]

                def grp(m4, blk):
                    m, N = cg * 2 + m4, NW[blk]
                    ps = P.get()
                    mm(ps, ps.a[:, :N],
                       [(fs[kt // 8][1][0][:, kt % 8, m4 * 128:(m4 + 1) * 128], hf[kt][blk].a[:, :N], [hf[kt][blk]]) for kt in range(24)],
                       [f[0] for f in fs])
                    TT(x[m][blk].a[:, :N], x[m][blk].a[:, :N], ps.a[:, :N], ALU.add, [x[m][blk], ps], [x[m][blk]])
                return grp
            residual_stage(c, 4, make_group, after_blk)

        def load_sample_state(l):
            for hs in range(2):
                rw = T["rows"].get()
                S.dma("sp", rw.ch, rw.a[0:120, :], st_pool[l, hs * 8:(hs + 1) * 8].rearrange("s r c -> (s r) c"), writes=[rw])
                ps = P.get()
                for j in range(4):
                    TR(ps, ps.a[:, j * 120:(j + 1) * 120], rw.a[0:120, j * 128:(j + 1) * 128], ident.a[0:120, 0:120], [rw], sig=(j == 3))
                for j in range(4):
                    CP(exs.t[:, j, hs * 8:(hs + 1) * 8, 0:15], ps.a[:, j * 120:(j + 1) * 120].rearrange("p (s r) -> p s r", r=15),
                       [ps], [exs_v[j]])
            stg = StageA
            stg.load(st_rc[l].rearrange("s r c -> (s r) c"), 48)
            for g in range(2):
                ps = P.get()
                for i in range(4):
                    j = g * 4 + i
                    TR(ps, ps.a[:, i * 48:(i + 1) * 48], stg.ap(j, 48), ident.a[0:48, 0:48], stg.deps, sig=(i == 3))
                for i in range(4):
                    j = g * 4 + i
                    CP(ex3.t[:, j, :, 0:3], ps.a[:, i * 48:(i + 1) * 48].rearrange("p (s r) -> p s r", r=3), [ps], [ex3_v[j]])
            stg = StageB
            stg.load(st_h[l], NSMP)
            for g in range(2):
                ps = P.get()
                for i in range(4):
                    j = g * 4 + i
                    TR(ps, ps.a[:, i * NSMP:(i + 1) * NSMP], stg.ap(j, NSMP), ident.a[0:NSMP, 0:NSMP], stg.deps, sig=(i == 3))
                CP(h0t.t[:, g * 4:(g + 1) * 4, :], ps.a[:, 0:4 * NSMP].rearrange("p (a b) -> p a b", a=4), [ps],
                   [h0_v[g * 4 + i] for i in range(4)])
            for q in range(3):
                stg = stages[q % 2]
                stg.load(st_ff[l][:, :, q * 1024:(q + 1) * 1024].rearrange("s r c -> (s r) c"), 32)
                for g in range(2):
                    ps = P.get()
                    for i in range(4):
                        t = q * 8 + g * 4 + i
                        TR(ps, ps.a[:, i * 32:(i + 1) * 32], stg.ap(g * 4 + i, 32), ident.a[0:32, 0:32], stg.deps, sig=(i == 3))
                    for i in range(4):
                        t = q * 8 + g * 4 + i
                        CP(ex2.t[:, t, :, 0:2], ps.a[:, i * 32:(i + 1) * 32].rearrange("p (s r) -> p s r", r=2), [ps], [ex2_v[t]])
            S.dma("sp", ch_d2d, o_pool_s[l, :, 0:14, :], st_pool[l, :, 1:15, :], final=True)
            S.dma("sp", ch_d2d, o_rc_s[l, :, 0:2, :], st_rc[l, :, 1:3, :], final=True)
            S.dma("sp", ch_d2d, o_ff_s[l, :, 0:1, :], st_ff[l, :, 1:2, :], final=True)

        def emit_prompt_state(l):
            emit_rows([cp_v[l][j][0].a for j in range(4)], 15, o_pool_p[l], [cp_v[l][j][0] for j in range(4)])
            for g in range(2):
                emit_rows([cr_v[l][g * 4 + i][0].a for i in range(4)], 3, o_rc_p[l, :, g * 512:(g + 1) * 512],
                          [cr_v[l][g * 4 + i][0] for i in range(4)])
                emit_rows([ch_v[l][g * 4 + i].a for i in range(4)], 1, o_h_p[l, :, g * 512:(g + 1) * 512], ch_v[l])
            for g in range(6):
                emit_rows([cf_v[l][g * 4 + i][0].a for i in range(4)], 2, o_ff_p[l, :, g * 512:(g + 1) * 512],
                          [cf_v[l][g * 4 + i][0] for i in range(4)])

        def final_store_blk(c, blk):
            if True:
                N = NW[blk]
                rs = rstd_blk(c, blk)
                for g in range(2):
                    yts = [T["acc"].get(), T["i"].get(), T["gg"].get(), T["sq"].get()]
                    for i in range(4):
                        kt = g * 4 + i
                        STT(yts[i].a[:, :N], x[kt][blk].a[:, :N], pvc(PV_FNG + kt), rs.a[:, :N], ALU.mult, ALU.mult,
                            [x[kt][blk], rs, pv], [yts[i]])
                    if blk == 2:
                        ps = P.get()
                        for i in range(4):
                            TR(ps, ps.a[0:N, i * 128:(i + 1) * 128], yts[i].a[:, 0:N], ident.a, [yts[i]], sig=(i == 3))
                        rw = T["rows"].get()
                        CP(rw.a[0:N, :], ps.a[0:N, :], [ps], [rw])
                        S.dma("sp", rw.ch, y_sample[:, g * 512:(g + 1) * 512], rw.a[0:N, :], reads=[rw], final=True)
                    else:
                        for tt in range(4):
                            ps = P.get()
                            for i in range(4):
                                TR(ps, ps.a[:, i * 128:(i + 1) * 128], yts[i].a[:, tt * 128:(tt + 1) * 128], ident.a, [yts[i]], sig=(i == 3))
                            rw = T["rows"].get()
                            CP(rw.a, ps.a, [ps], [rw], eng="act" if tt % 2 == 0 else "dve")
                            r0 = c.tok0 + blk * NB + tt * 128
                            S.dma("sp", rw.ch, y_prompt[r0:r0 + 128, g * 512:(g + 1) * 512], rw.a, reads=[rw], final=True)

        marks = []

        def mark(name):
            marks.append((name, dict(S.nins)))

        def layer(c, l):
            mark(f"L{l}.smpstate")
            if 2 in c.blocks:
                load_sample_state(l)
            mark(f"L{l}.norm1")
            mark(f"L{l}.pool")
            stage_pool(c, l)
            mark(f"L{l}.rnn")
            stage_rnn(c, l)
            mark(f"L{l}.chunk")
            stage_chunk(c, l)
            mark(f"L{l}.merge")
            stage_merge(c, l)
            mark(f"L{l}.wo")
            stage_wo(c, l, lambda blk: norm_blk(c, PV_N2G + l * 8, blk))
            mark(f"L{l}.ffn_up")
            stage_ffn_up(c, l)
            mark(f"L{l}.ffn_down")
            if l + 1 < NL:
                stage_ffn_down(c, l, lambda blk: norm_blk(c, PV_N1G + (l + 1) * 8, blk))
            else:
                stage_ffn_down(c, l, lambda blk: final_store_blk(c, blk))

        for c in (Ctx(0, (0, 1)), Ctx(TH, (0, 1, 2))):
            mark("load_x")
            load_x(c)
            for l in range(NL):
                layer(c, l)
                if c.tok0 == TH:
                    emit_prompt_state(l)
        S.finish()
        build_program.stats = dict(nins=dict(S.nins), nwait=S.nwait)
        build_program.log = S.log
        mark("end")
        build_program.marks = marks
    return nc


_NC = None


def _pack_pvec(p):
    rows = [p["norm1_g"].reshape(16, 128), p["norm2_g"].reshape(16, 128), p["pool_scale"].reshape(8, 128),
            p["rnn_conv_w"].reshape(64, 128), p["rnn_conv_b"].reshape(16, 128), p["lru_ba"].reshape(16, 128),
            p["lru_bx"].reshape(16, 128), p["lru_lambda"].reshape(16, 128), p["ffn_conv_w"].reshape(144, 128),
            p["ffn_conv_b"].reshape(48, 128), p["final_norm_g"].reshape(8, 128)]
    pv = np.concatenate(rows, axis=0)
    out = np.zeros((PV_ROWS, 128), np.float32)
    out[:pv.shape[0]] = pv
    return out


def kernel(**inputs):
    global _NC
    p = {k: np.ascontiguousarray(np.asarray(v, dtype=np.float32)) for k, v in inputs.items()}
    if _NC is None:
        _NC = build_program()
    nc = _NC
    ncore = 8
    shared = {k: p[k] for k in ("w_in", "pool_w", "lru_wa", "lru_wx", "chunk_ws", "chunk_bs", "chunk_vnorm_g", "w_pa", "w_pb",
                                "w_pc", "w_o", "ffn_wg", "ffn_wu", "ffn_wd")}
    shared["pvec"] = _pack_pvec(p)
    shared["ident"] = np.eye(128, dtype=np.float32)
    shared["tril"] = np.tril(np.ones((128, 128), np.float32))
    ic = np.zeros((4, 16), np.float32)
    for j, w in enumerate(WIN):
        ic[j] = 1.0 / np.minimum(np.arange(16) + 1, w)
    shared["invcnt"] = ic.reshape(1, 64)
    shared["ws00"] = np.ascontiguousarray(p["chunk_ws"][:, :, 0, 0]).reshape(1, 8)
    shared["bs0"] = np.ascontiguousarray(p["chunk_bs"][:, :, 0]).reshape(1, 8)
    in_maps = []
    for i in range(ncore):
        m = dict(shared)
        m["x_prompt"] = np.ascontiguousarray(p["x_prompt"][i])
        sl = slice(i * NSMP, (i + 1) * NSMP)
        m["x_sample"] = np.ascontiguousarray(p["x_sample"][sl, 0, :])
        m["state_pool"] = np.ascontiguousarray(p["state_pool"][:, sl])
        m["state_rnn_conv"] = np.ascontiguousarray(p["state_rnn_conv"][:, sl])
        m["state_rnn_h"] = np.ascontiguousarray(p["state_rnn_h"][:, sl])
        m["state_ffn_conv"] = np.ascontiguousarray(p["state_ffn_conv"][:, sl])
        in_maps.append(m)
    res = run_bass_kernel_spmd(nc, in_maps, core_ids=list(range(ncore)))
    R = res.results
    f32 = np.float32
    y_prompt = np.stack([R[i]["y_prompt"] for i in range(ncore)], 0).astype(f32)
    y_sample = np.concatenate([R[i]["y_sample"] for i in range(ncore)], 0)[:, None, :].astype(f32)
    pool_p = np.stack([R[i]["new_pool_prompt"] for i in range(ncore)], 1).astype(f32)
    pool_s = np.concatenate([R[i]["new_pool_sample"] for i in range(ncore)], 1).astype(f32)
    rc_p = np.stack([R[i]["new_rconv_prompt"] for i in range(ncore)], 1).astype(f32)
    rc_s = np.concatenate([R[i]["new_rconv_sample"] for i in range(ncore)], 1).astype(f32)
    h_p = np.stack([R[i]["new_h_prompt"][:, 0, :] for i in range(ncore)], 1).astype(f32)
    h_s = np.concatenate([R[i]["new_h_sample"] for i in range(ncore)], 1).astype(f32)
    ff_p = np.stack([R[i]["new_ffn_prompt"] for i in range(ncore)], 1).astype(f32)
    ff_s = np.concatenate([R[i]["new_ffn_sample"] for i in range(ncore)], 1).astype(f32)
    cv_s = np.concatenate([R[i]["new_chunk_v_sample"] for i in range(ncore)], 1)[:, :, None, :].astype(f32)
    return (y_prompt, y_sample, pool_p, pool_s, rc_p, rc_s, h_p, h_s, ff_p, ff_s, cv_s)
```

```python
import numpy as np
from contextlib import ExitStack
import concourse.bass as bass
import concourse.mybir as mybir
from concourse.bass_utils import run_bass_kernel_spmd

F32 = mybir.dt.float32
BF16 = mybir.dt.bfloat16
AF = mybir.ActivationFunctionType
ALU = mybir.AluOpType
AX = mybir.AxisListType


class Dep:
    __slots__ = ("w", "r")

    def __init__(self):
        self.w = []
        self.r = []


class Buf(Dep):
    __slots__ = ("t", "a", "ch")

    def __init__(self, t, a=None, ch=None):
        super().__init__()
        self.t = t
        self.a = t[:] if a is None else a
        self.ch = ch


def V(ap):
    return Buf(None, ap)


class Sched:
    def __init__(self, nc, st):
        self.nc, self.st = nc, st
        self.eng = dict(pe=nc.tensor, act=nc.scalar, dve=nc.vector, pool=nc.gpsimd, sp=nc.sync)
        self.sems, self.cnt = {}, {}
        for k in self.eng:
            self.sems[k] = st.enter_context(nc.semaphore("sem_" + k))
            self.cnt[k] = 0
        self.waited = {k: {} for k in self.eng}
        self.finals = []
        self.nwait = 0
        self.nins = {k: 0 for k in self.eng}
        self.log = {k: [] for k in self.eng}

    def sb(self, name, shape, dtype):
        return Buf(self.st.enter_context(self.nc.sbuf_tensor(name, list(shape), dtype)))

    def ps(self, name):
        return Buf(self.st.enter_context(self.nc.psum_tensor(name, [128, 512], F32)))

    def chan(self, name):
        self.sems[name] = self.st.enter_context(self.nc.semaphore("sem_" + name))
        self.cnt[name] = 0
        return name

    def _wait(self, e, evs):
        need = {}
        for (k, v) in evs:
            if v > need.get(k, 0):
                need[k] = v
        wd = self.waited[e]
        for k, v in need.items():
            if wd.get(k, 0) >= v:
                continue
            if k == e and e == "pe":
                continue
            self.eng[e].wait_ge(self.sems[k], v)
            self.log[e].append(("w", k, v))
            self.nwait += 1
            wd[k] = v

    def _deps(self, reads, writes):
        evs = []
        for d in reads:
            evs += d.w
        for d in writes:
            evs += d.w
            evs += d.r
        return evs

    def _record(self, ev, reads, writes):
        for d in reads:
            d.r.append(ev)
            if len(d.r) > 24:
                d.r = _compact(d.r)
        for d in writes:
            d.w = [ev]
            d.r = []

    def op(self, e, fn, reads=(), writes=(), sig=True):
        self._wait(e, self._deps(reads, writes))
        ins = fn(self.eng[e])
        self.nins[e] += 1
        if sig:
            self.cnt[e] += 1
            ins.then_inc(self.sems[e], 1)
            self.log[e].append(("i", e, 1))
            ev = (e, self.cnt[e])
        else:
            ev = (e, self.cnt[e] + 1)
        self._record(ev, reads, writes)
        return ev

    def sbc(self, name, shape, dtype):
        b = self.sb(name, shape, dtype)
        b.ch = self.chan("c_" + name)
        return b

    def batch_end(self, ch, deps):
        for d in deps:
            d.w = [(ch, self.cnt[ch])]

    def dma(self, q, ch, out, in_, reads=(), writes=(), final=False, join=(), **kw):
        self._wait(q, self._deps(reads, writes))
        ins = self.eng[q].dma_start(out=out, in_=in_, **kw)
        self.nins[q] += 1
        self.cnt[ch] += 16
        ins.then_inc(self.sems[ch], 16)
        self.log[q].append(("i", ch, 16))
        ev = (ch, self.cnt[ch])
        self._record(ev, reads, writes)
        for d in join:
            d.w.append(ev)
        if final:
            self.finals.append(ev)
        return ev

    def finish(self):
        self._wait("sp", self.finals)
        for e in ("act", "dve", "pool", "pe"):
            self._wait(e, self.finals)


def _compact(evs):
    m = {}
    for k, v in evs:
        if v > m.get(k, 0):
            m[k] = v
    return list(m.items())


NL, D, KT, NB = 2, 1024, 8, 512
SEQ, TH = 2048, 1024
NSMP = 16
WIN = (2, 4, 8, 16)
EPS = 1e-6
C_POOL, C_RX, C_RG, C_U, C_V, C_G = 0, 512, 1536, 2560, 3072, 3584
PV_N1G, PV_N2G, PV_PSC, PV_RCW, PV_RCB, PV_LBA, PV_LBX, PV_LAM, PV_FCW, PV_FCB, PV_FNG = (
    0, 16, 32, 40, 104, 120, 136, 152, 168, 312, 360)
PV_ROWS = 384
NSLOT = 8
SLOTW = 2048


class Ring:
    def __init__(self, bufs):
        self.b, self.i = bufs, 0

    def get(self):
        b = self.b[self.i % len(self.b)]
        self.i += 1
        return b


NW = (NB, NB, NSMP)


class Ctx:
    def __init__(self, tok0, blocks):
        self.tok0, self.blocks = tok0, blocks


def build_program():
    nc = bass.Bass("TRN2", target_bir_lowering=False)

    def din(n, s):
        return nc.dram_tensor(n, list(s), F32, kind="ExternalInput").ap()

    def dout(n, s):
        return nc.dram_tensor(n, list(s), F32, kind="ExternalOutput").ap()

    x_prompt = din("x_prompt", [SEQ, D])
    x_sample = din("x_sample", [NSMP, D])
    st_pool = din("state_pool", [NL, NSMP, 15, 512])
    st_rc = din("state_rnn_conv", [NL, NSMP, 3, 1024])
    st_h = din("state_rnn_h", [NL, NSMP, 1024])
    st_ff = din("state_ffn_conv", [NL, NSMP, 2, 3072])
    w_in = din("w_in", [NL, 1024, 6656])
    pool_w = din("pool_w", [NL, 4, 128, 128])
    lru_wa = din("lru_wa", [NL, 8, 128, 128])
    lru_wx = din("lru_wx", [NL, 8, 128, 128])
    chunk_ws = din("chunk_ws", [NL, 4, 128, 128])
    chunk_bs = din("chunk_bs", [NL, 4, 128])
    chunk_vg = din("chunk_vnorm_g", [NL, 512])
    w_pa = din("w_pa", [NL, 512, 1024])
    w_pb = din("w_pb", [NL, 1024, 1024])
    w_pc = din("w_pc", [NL, 512, 1024])
    w_o = din("w_o", [NL, 1024, 1024])
    ffn_wg = din("ffn_wg", [NL, 1024, 3072])
    ffn_wu = din("ffn_wu", [NL, 1024, 3072])
    ffn_wd = din("ffn_wd", [NL, 3072, 1024])
    pvec = din("pvec", [PV_ROWS, 128])
    ident_d = din("ident", [128, 128])
    tril_d = din("tril", [128, 128])
    invcnt_d = din("invcnt", [1, 64])
    ws00_d = din("ws00", [1, 8])
    bs0_d = din("bs0", [1, 8])

    y_prompt = dout("y_prompt", [SEQ, D])
    y_sample = dout("y_sample", [NSMP, D])
    o_pool_p = dout("new_pool_prompt", [NL, 15, 512])
    o_pool_s = dout("new_pool_sample", [NL, NSMP, 15, 512])
    o_rc_p = dout("new_rconv_prompt", [NL, 3, 1024])
    o_rc_s = dout("new_rconv_sample", [NL, NSMP, 3, 1024])
    o_h_p = dout("new_h_prompt", [NL, 1, 1024])
    o_h_s = dout("new_h_sample", [NL, NSMP, 1024])
    o_ff_p = dout("new_ffn_prompt", [NL, 2, 3072])
    o_ff_s = dout("new_ffn_sample", [NL, NSMP, 2, 3072])
    o_cv_s = dout("new_chunk_v_sample", [NL, NSMP, 512])

    with ExitStack() as st:
        S = Sched(nc, st)
        xt = S.sb("xt", [128, 8, TH], F32)
        xnt = S.sb("xnt", [128, 8, TH], BF16)
        hft = S.sb("hft", [128, 24, TH], BF16)
        xst = S.sb("xst", [128, 8, NSMP], F32)
        xnst = S.sb("xnst", [128, 8, NSMP], BF16)
        hfst = S.sb("hfst", [128, 24, NSMP], BF16)
        x = [[V(xt.t[:, k, b * NB:(b + 1) * NB]) for b in range(2)] + [V(xst.t[:, k, :])] for k in range(8)]
        xn = [[V(xnt.t[:, k, b * NB:(b + 1) * NB]) for b in range(2)] + [V(xnst.t[:, k, :])] for k in range(8)]
        hf = [[V(hft.t[:, k, b * NB:(b + 1) * NB]) for b in range(2)] + [V(hfst.t[:, k, :])] for k in range(24)]
        ya, yb, yc, mrg = hf[0:4], hf[4:12], hf[12:16], hf[16:24]

        TW = 528
        T = {}
        for role, n in (("ext", 2), ("acc", 3), ("r", 2), ("i", 2), ("sq", 2), ("gg", 2)):
            T[role] = Ring([S.sb(f"t_{role}{i}", [128, TW], F32) for i in range(n)])
        T["bcb"] = Ring([S.sb(f"t_bcb{i}", [128, NB], BF16) for i in range(2)])
        TSm = {}
        tsm_t = S.sb("tsm", [128, 11, NSMP], F32)
        tsm_b = S.sb("tsmb", [128, 2, NSMP], BF16)
        k_ = 0
        for role, n in (("acc", 3), ("r", 2), ("i", 2), ("sq", 2), ("gg", 2)):
            TSm[role] = Ring([V(tsm_t.t[:, k_ + i, :]) for i in range(n)])
            k_ += n
        TSm["bcb"] = Ring([V(tsm_b.t[:, i, :]) for i in range(2)])
        T["sqn"] = Ring([S.sb(f"t_sqn{i}", [128, NB], BF16) for i in range(2)])
        T["vn"] = Ring([S.sb(f"t_vn{i}", [128, NB], BF16) for i in range(2)])
        ssq_t = S.sb("ssq", [128, 4], F32)
        T["ssq"] = Ring([V(ssq_t.t[:, i:i + 1]) for i in range(4)])
        P = Ring([S.ps(f"ps{i}") for i in range(8)])

        xstage = S.sbc("xstage", [128, D], F32)
        T["rows"] = Ring([S.sbc(f"rows{i}", [128, NB], F32) for i in range(2)])

        wring = S.sb("wring", [128, NSLOT * SLOTW], BF16)
        slots = [Buf(None, wring.t[:, i * SLOTW:(i + 1) * SLOTW], S.chan(f"c_wslot{i}")) for i in range(NSLOT)]
        wctr = [0]

        ident = S.sb("ident_sb", [128, 128], F32)
        tril = S.sb("tril_sb", [128, 128], F32)
        pv = S.sb("pv", [128, PV_ROWS], F32)
        clv = S.sb("clv", [128, 32], F32)
        hbv = S.sb("hbv", [128, 32], F32)
        vgb = S.sb("vgb", [128, NL, 512], F32)
        icn = S.sb("icn", [128, 64], F32)
        wsb = S.sb("wsb", [128, 8], F32)
        bsb = S.sb("bsb", [128, 8], F32)
        bsbt = S.sb("bsbt", [128, NL * 4, 128], F32)
        mhalf = S.sb("mhalf", [128, 1], F32)
        ones_bf = S.sb("ones_bf", [128, 128], BF16)
        poolw = S.sb("poolw", [128, NL * 4, 128], BF16)
        lruwa = S.sb("lruwa", [128, NL * 8, 128], BF16)
        lruwx = S.sb("lruwx", [128, NL * 8, 128], BF16)
        wsT = S.sb("wsT", [128, NL * 4, 128], BF16)
        cpt = S.sb("cpt", [128, 2, NL, 4, 15], F32)
        crt = S.sb("crt", [128, 2, NL, 8, 3], F32)
        cht = S.sb("cht", [128, NL, 8, 1], F32)
        cft = S.sb("cft", [128, 2, NL, 24, 2], F32)
        cp_v = [[[V(cpt.t[:, q, l, j, :]) for q in range(2)] for j in range(4)] for l in range(NL)]
        cr_v = [[[V(crt.t[:, q, l, j, :]) for q in range(2)] for j in range(8)] for l in range(NL)]
        ch_v = [[V(cht.t[:, l, j, :]) for j in range(8)] for l in range(NL)]
        cf_v = [[[V(cft.t[:, q, l, j, :]) for q in range(2)] for j in range(24)] for l in range(NL)]
        exs = S.sb("exs", [128, 4, NSMP, 16], F32)
        ex3 = S.sb("ex3", [128, 8, NSMP, 4], F32)
        h0t = S.sb("h0t", [128, 8, NSMP], F32)
        ex2 = S.sb("ex2", [128, 24, NSMP, 3], F32)
        exs_v = [V(exs.t[:, j]) for j in range(4)]
        ex3_v = [V(ex3.t[:, j]) for j in range(8)]
        h0_v = [V(h0t.t[:, j]) for j in range(8)]
        ex2_v = [V(ex2.t[:, j]) for j in range(24)]

        ch_misc = S.chan("misc")
        ch_d2d = S.chan("d2d")
        ch_cv = S.chan("cvout")

        def ACT(out, in_, func, reads, writes, **kw):
            S.op("act", lambda e: e.activation(out, in_, func, **kw), reads, writes)

        def TT(out, a, b, op, reads, writes, eng="dve"):
            S.op(eng, lambda e: e.tensor_tensor(out, a, b, op), reads, writes)

        def TS(out, a, s1, s2, op0, op1, reads, writes, eng="dve"):
            S.op(eng, lambda e: e.tensor_scalar(out, a, s1, s2, op0, op1), reads, writes)

        def STT(out, a, s, b, op0, op1, reads, writes):
            S.op("dve", lambda e: e.scalar_tensor_tensor(out, a, s, b, op0, op1), reads, writes)

        def CP(out, in_, reads, writes, eng="dve"):
            if eng == "act":
                S.op("act", lambda e: e.activation(out, in_, AF.Copy), reads, writes)
            else:
                S.op(eng, lambda e: e.tensor_copy(out, in_), reads, writes)

        def MM(psb, out, l, r, start, stop, reads, sig):
            S.op("pe", lambda e: e.matmul(out, l, r, start=start, stop=stop), reads, [psb], sig=sig)

        def mm(psb, out, pairs, reads):
            n = len(pairs)
            for i, pr in enumerate(pairs):
                last = i == n - 1
                rd = list(pr[2]) if len(pr) > 2 else []
                if i == 0 or last:
                    rd = rd + list(reads)
                MM(psb, out, pr[0], pr[1], i == 0, last, rd, last)

        def TR(psb, out, in_, idn, reads, sig=True):
            S.op("pe", lambda e: e.transpose(out, in_, idn), list(reads) + [ident], [psb], sig=sig)

        def pvc(i):
            return pv.t[:, i:i + 1]

        def fetch(pieces):
            slot = slots[wctr[0] % NSLOT]
            wctr[0] += 1
            views, off = [], 0
            for pi, p in enumerate(pieces):
                K, ncol = p.shape
                kt = K // 128
                v = slot.a[:, off:off + kt * ncol].rearrange("p (k c) -> p k c", k=kt)
                S.dma("pool", slot.ch, v, p.rearrange("(k p) c -> p k c", p=128),
                      writes=[slot] if pi == 0 else (), join=() if pi == 0 else [slot])
                views.append(v)
                off += kt * ncol
            assert off <= SLOTW
            return slot, views

        def fetch2(piece):
            if wctr[0] % 2:
                wctr[0] += 1
            i0 = wctr[0] % NSLOT
            wctr[0] += 2
            sa, sb_ = slots[i0], slots[i0 + 1]
            v = wring.t[:, i0 * SLOTW:(i0 + 2) * SLOTW].rearrange("p (k c) -> p k c", k=8)
            S.dma("pool", sa.ch, v, piece.rearrange("(k p) c -> p k c", p=128), writes=[sa, sb_])
            return [sa, sb_], v

        def emit_rows(srcs, n, dst, reads, final=True):
            ps = P.get()
            for i, s_ in enumerate(srcs):
                TR(ps, ps.a[0:n, i * 128:(i + 1) * 128], s_, ident.a, reads, sig=(i == len(srcs) - 1))
            rw = T["rows"].get()
            CP(rw.a[0:n, :], ps.a[0:n, :], [ps], [rw])
            S.dma("sp", rw.ch, dst, rw.a[0:n, :], reads=[rw], final=final)

        class StageA:
            deps = [xstage]

            @staticmethod
            def load(src_ap, n):
                S.dma("sp", xstage.ch, xstage.a[0:n, :], src_ap, writes=[xstage])

            @staticmethod
            def ap(kt, n):
                return xstage.a[0:n, kt * 128:(kt + 1) * 128]

        class StageB:
            deps = list(T["rows"].b)

            @staticmethod
            def load(src_ap, n):
                for h_ in range(2):
                    rb = T["rows"].b[h_]
                    S.dma("sp", rb.ch, rb.a[0:n, :], src_ap[:, h_ * 512:(h_ + 1) * 512], writes=[rb])

            @staticmethod
            def ap(kt, n):
                return T["rows"].b[kt // 4].a[0:n, (kt % 4) * 128:(kt % 4 + 1) * 128]

        stages = [StageA, StageB]

        setup = [ident, tril, vgb, icn, wsb, bsb]
        S.dma("sp", ch_misc, ident.a, ident_d, writes=[ident])
        S.dma("sp", ch_misc, tril.a, tril_d, writes=[tril])
        for l in range(NL):
            S.dma("sp", ch_misc, vgb.t[:, l, :], chunk_vg[l].partition_broadcast(128), writes=[vgb] if l == 0 else (),
                  join=() if l == 0 else [vgb])
        S.dma("sp", ch_misc, icn.a, invcnt_d[0].partition_broadcast(128), writes=[icn])
        S.dma("sp", ch_misc, wsb.a, ws00_d[0].partition_broadcast(128), writes=[wsb])
        S.dma("sp", ch_misc, bsb.a, bs0_d[0].partition_broadcast(128), writes=[bsb])
        S.batch_end(ch_misc, setup)
        rw0 = T["rows"].get()
        S.dma("sp", rw0.ch, rw0.a[:, 0:384].rearrange("p (r c) -> p r c", r=3),
              pvec.rearrange("(r p) c -> p r c", p=128), writes=[rw0])
        S.dma("sp", xstage.ch, xstage.a.rearrange("p (g j) -> p g j", g=8),
              chunk_ws.rearrange("l g i j -> i (l g) j"), writes=[xstage])
        ch_miscw = S.chan("miscw")
        for dst_, src_ in ((poolw, pool_w), (lruwa, lru_wa), (lruwx, lru_wx)):
            S.dma("pool", ch_miscw, dst_.a, src_.rearrange("l g i j -> i (l g) j"), writes=[dst_])
        S.batch_end(ch_miscw, [poolw, lruwa, lruwx])
        S.op("dve", lambda e: e.memset(ones_bf.a, 1.0), (), [ones_bf])
        S.op("dve", lambda e: e.memset(mhalf.a, -0.5), (), [mhalf])
        for cb_ in (cpt, crt, cht, cft):
            S.op("dve", lambda e, cb_=cb_: e.memset(cb_.a, 0.0), (), [cb_])
        for l in range(NL):
            for lst, n in ((cp_v, 4), (cr_v, 8), (cf_v, 24)):
                for j in range(n):
                    for q in range(2):
                        lst[l][j][q].w = [("dve", S.cnt["dve"])]
            for j in range(8):
                ch_v[l][j].w = [("dve", S.cnt["dve"])]
        ps = P.get()
        for r_ in range(3):
            TR(ps, ps.a[:, r_ * 128:(r_ + 1) * 128], rw0.a[:, r_ * 128:(r_ + 1) * 128], ident.a, [rw0], sig=(r_ == 2))
        CP(pv.a, ps.a[:, 0:PV_ROWS], [ps], [pv])
        ACT(clv.t[:, 0:16], pv.t[:, PV_LAM:PV_LAM + 16], AF.Exp, [pv], [clv], scale=-1.0)
        ACT(clv.t[:, 0:16], clv.t[:, 0:16], AF.Ln, [clv], [clv], bias=1.0)
        TS(clv.t[:, 16:32], clv.t[:, 0:16], -4.0, None, ALU.mult, ALU.bypass, [clv], [clv])
        TS(clv.t[:, 0:16], clv.t[:, 0:16], -8.0, None, ALU.mult, ALU.bypass, [clv], [clv])
        TS(hbv.a, pv.t[:, PV_LBA:PV_LBA + 32], 0.5, None, ALU.mult, ALU.bypass, [pv], [hbv])
        for lg in range(8):
            blk_ = xstage.a[:, lg * 128:(lg + 1) * 128]
            TT(blk_, blk_, tril.a, ALU.mult, [xstage, tril], [xstage])
        for h_ in range(2):
            ps = P.get()
            for i in range(4):
                lg = h_ * 4 + i
                TR(ps, ps.a[:, i * 128:(i + 1) * 128], xstage.a[:, lg * 128:(lg + 1) * 128], ident.a, [xstage], sig=(i == 3))
            CP(wsT.t[:, h_ * 4:(h_ + 1) * 4, :], ps.a.rearrange("p (a b) -> p a b", a=4), [ps], [wsT])
        ch_misc2 = S.chan("misc2")
        S.dma("sp", ch_misc2, bsbt.a.rearrange("p a b -> p (a b)"), chunk_bs.rearrange("l g i -> (l g i)").partition_broadcast(128),
              writes=[bsbt])

        def load_x(c):
            if 2 in c.blocks:
                S.dma("sp", xstage.ch, xstage.a[0:NSMP, :], x_sample, writes=[xstage])
                for g in range(2):
                    ps = P.get()
                    for i in range(4):
                        kt = g * 4 + i
                        TR(ps, ps.a[:, i * NSMP:(i + 1) * NSMP], xstage.a[0:NSMP, kt * 128:(kt + 1) * 128],
                           ident.a[0:NSMP, 0:NSMP], [xstage], sig=(i == 3))
                    CP(xst.t[:, g * 4:(g + 1) * 4, :], ps.a[:, 0:4 * NSMP].rearrange("p (a b) -> p a b", a=4),
                       [ps], [x[g * 4 + i][2] for i in range(4)])
            for tt in range(8):
                stg = stages[tt % 2]
                stg.load(x_prompt[c.tok0 + tt * 128:c.tok0 + (tt + 1) * 128, :], 128)
                blk, c0 = tt // 4, (tt % 4) * 128
                for g in range(2):
                    ps = P.get()
                    for i in range(4):
                        kt = g * 4 + i
                        TR(ps, ps.a[:, i * 128:(i + 1) * 128], stg.ap(kt, 128), ident.a, stg.deps, sig=(i == 3))
                    CP(xt.t[:, g * 4:(g + 1) * 4, blk * NB + c0:blk * NB + c0 + 128],
                       ps.a.rearrange("p (a b) -> p a b", a=4), [ps], [x[g * 4 + i][blk] for i in range(4)],
                       eng="act" if g == 0 else "dve")
                if tt % 4 == 3:
                    norm_blk(c, PV_N1G, blk)
            if 2 in c.blocks:
                norm_blk(c, PV_N1G, 2)

        def rstd_blk(c, blk):
            N = NW[blk]
            ps = P.get()
            for kt in range(8):
                sq = T["sqn"].get()
                ACT(sq.a[:, :N], x[kt][blk].a[:, :N], AF.Square, [x[kt][blk]], [sq])
                MM(ps, ps.a[:, :N], ones_bf.a, sq.a[:, :N], kt == 0, kt == 7, [sq, ones_bf], True)
            rs = T["r"].get()
            ACT(rs.a[:, :N], ps.a[:, :N], AF.Sqrt, [ps], [rs], bias=EPS, scale=1.0 / D)
            S.op("dve", lambda e: e.reciprocal(rs.a[:, :N], rs.a[:, :N]), [rs], [rs])
            return rs

        def norm_blk(c, g0, blk):
            N = NW[blk]
            rs = rstd_blk(c, blk)
            for kt in range(8):
                STT(xn[kt][blk].a[:, :N], x[kt][blk].a[:, :N], pvc(g0 + kt), rs.a[:, :N], ALU.mult, ALU.mult,
                    [x[kt][blk], rs, pv], [xn[kt][blk]])

        def norm_to_xn(c, g0):
            for blk in c.blocks:
                norm_blk(c, g0, blk)

        def residual_stage(c, ncg, make_group, after_blk, tail=2):
            for cg in range(ncg - tail):
                grp = make_group(cg)
                for m2 in range(2):
                    for blk in c.blocks:
                        grp(m2, blk)
            grps = [make_group(cg) for cg in range(ncg - tail, ncg)]
            for bi, blk in enumerate(c.blocks):
                first = True
                for grp in grps:
                    for m2 in range(2):
                        grp(m2, blk)
                        if bi > 0 and first:
                            after_blk(c.blocks[bi - 1])
                        first = False
            after_blk(c.blocks[-1])

        def xn_pairs(wv, cs, blk, N):
            return [(wv[:, kt, cs], xn[kt][blk].a[:, :N], [xn[kt][blk]]) for kt in range(8)]

        def xn_reads(blk):
            return []

        def run_pipelined(items, phase_fns, ascending=False):
            n, npz = len(items), len(phase_fns)
            states = [dict() for _ in items]
            import types
            for step in range(n + npz - 1):
                gens = []
                for ph in (range(npz) if ascending else range(npz - 1, -1, -1)):
                    it = step - ph
                    if 0 <= it < n:
                        g = phase_fns[ph](items[it], states[it])
                        if isinstance(g, types.GeneratorType):
                            gens.append(g)
                while gens:
                    for g in list(gens):
                        try:
                            next(g)
                        except StopIteration:
                            gens.remove(g)

        def stage_pool(c, l):
            items = [(j, blk) for jp in range(0, 4, 2) for blk in c.blocks for j in (jp, jp + 1)]
            sl = {}

            def wts(j):
                g2 = j // 2
                if g2 not in sl:
                    sl[g2] = fetch([w_in[l][:, C_POOL + g2 * 256:C_POOL + (g2 + 1) * 256]])
                return sl[g2][0], sl[g2][1][0]

            def p0(it, s):
                j, blk = it
                N = NW[blk]
                slot, wv = wts(j)
                ps = P.get()
                mm(ps, ps.a[:, :N], xn_pairs(wv, slice((j % 2) * 128, (j % 2 + 1) * 128), blk, N), [slot])
                if blk != 2:
                    ext = T["ext"].get()
                    cvi, cvo = cp_v[l][j][blk], cp_v[l][j][1 - blk]
                    CP(ext.a[:, 0:15], cvi.a, [cvi], [ext])
                    CP(ext.a[:, 15:527], ps.a, [ps], [ext], eng="act")
                    CP(cvo.a, ps.a[:, 512 - 15:512], [ps], [cvo], eng="act")
                    s["ext"] = ext
                else:
                    CP(exs.t[:, j, :, 15], ps.a[:, :N], [ps], [exs_v[j]], eng="act")

            def p1(it, s):
                j, blk = it
                N, w = NW[blk], WIN[j]
                d = T["bcb"].get()
                s["d"] = d
                if blk != 2:
                    ext = s["ext"]
                    top = j + 1
                    los = {top: 15}
                    for k in range(top, 1, -1):
                        los[k - 1] = los[k] - 2 ** (k - 1)
                    bufs = [T["acc"].get(), T["r"].get()]
                    prev = ext
                    for k in range(1, top + 1):
                        lo, sh = los[k], 2 ** (k - 1)
                        cur = bufs[(k - 1) % 2]
                        TT(cur.a[:, lo:527], prev.a[:, lo:527], prev.a[:, lo - sh:527 - sh], ALU.add, [prev], [cur])
                        prev = cur
                    STT(d.a, prev.a[:, 15:527], 1.0 / w, ext.a[:, 15:527], ALU.mult, ALU.subtract, [prev, ext], [d])
                    if c.tok0 == 0 and blk == 0:
                        tm = T["sq"].get()
                        TT(tm.a[:, 0:16], prev.a[:, 15:31], icn.t[:, j * 16:(j + 1) * 16], ALU.mult, [prev, icn], [tm])
                        TT(d.a[:, 0:16], tm.a[:, 0:16], ext.a[:, 15:31], ALU.subtract, [tm, ext], [d])
                else:
                    ev = exs_v[j]
                    sr = T["acc"].get()
                    S.op("dve", lambda e: e.tensor_reduce(sr.a[:, :N], exs.t[:, j, :, 16 - w:16], AX.X, ALU.add), [ev], [sr])
                    STT(d.a[:, :N], sr.a[:, :N], 1.0 / w, exs.t[:, j, :, 15], ALU.mult, ALU.subtract, [sr, ev], [d])

            def p2(it, s):
                j, blk = it
                N, d = NW[blk], s["d"]
                ps2 = P.get()
                MM(ps2, ps2.a[:, :N], poolw.t[:, l * 4 + j, :], d.a[:, :N], True, True, [poolw, d], True)
                ACT(ya[j][blk].a[:, :N], ps2.a[:, :N], AF.Identity, [ps2, pv], [ya[j][blk]], scale=pvc(PV_PSC + l * 4 + j))

            run_pipelined(items, [p0, p1, p2])
            if 2 in c.blocks:
                emit_rows([exs.t[:, j, :, 15] for j in range(4)], NSMP, o_pool_s[l, :, 14, :], exs_v)

        def stage_rnn(c, l):
            items = [(j, blk) for jp in range(0, 8, 2) for blk in (0, 1) for j in (jp, jp + 1)]
            has_s = 2 in c.blocks
            sl = {}

            def wts(j):
                jg = j // 2
                if jg not in sl:
                    sx, (wxv,) = fetch([w_in[l][:, C_RX + jg * 256:C_RX + (jg + 1) * 256]])
                    sg, (wgv,) = fetch([w_in[l][:, C_RG + jg * 256:C_RG + (jg + 1) * 256]])
                    sl[jg] = (sx, wxv, sg, wgv)
                return sl[jg]

            def tmp(role, blk):
                return (TSm if blk == 2 else T)[role].get()

            def subs(it, s):
                j, blk = it
                out = [(j, blk, s)]
                if blk == 1 and has_s:
                    out.append((j, 2, s.setdefault("smp", {})))
                return out

            def p0(it, s):
                late = []
                for j, blk, st_ in subs(it, s):
                    N = NW[blk]
                    sx, wxv, sg, wgv = wts(j)
                    cs = slice((j % 2) * 128, (j % 2 + 1) * 128)
                    ps = P.get()
                    mm(ps, ps.a[:, :N], xn_pairs(wxv, cs, blk, N), [sx])
                    if blk == 2:
                        CP(ex3.t[:, j, :, 3], ps.a[:, :N], [ps], [ex3_v[j]], eng="act")
                    else:
                        late.append((j, blk, st_, ps))
                for _ in range(10):
                    yield
                for j, blk, st_, ps in late:
                    ext = T["ext"].get()
                    cvi, cvo = cr_v[l][j][blk], cr_v[l][j][1 - blk]
                    CP(ext.a[:, 0:3], cvi.a, [cvi], [ext])
                    CP(ext.a[:, 3:515], ps.a, [ps], [ext])
                    CP(cvo.a, ps.a[:, 512 - 3:512], [ps], [cvo])
                    st_["ext"] = ext

            def p1(it, s):
                first = True
                for j, blk, st_ in subs(it, s):
                    N = NW[blk]
                    wk = [pvc(PV_RCW + (l * 4 + k) * 8 + j) for k in range(4)]
                    cb = pvc(PV_RCB + l * 8 + j)
                    acc, bcb = tmp("acc", blk), tmp("bcb", blk)
                    st_.update(acc=acc, bcb=bcb)
                    if first:
                        yield
                        first = False
                    if blk != 2:
                        ext = st_["ext"]
                        taps = [ext.a[:, k:k + 512] for k in range(4)]
                        rd = [ext]
                    else:
                        taps = [ex3.t[:, j, :, k] for k in range(4)]
                        rd = [ex3_v[j]]
                    TS(acc.a[:, :N], taps[3], wk[3], cb, ALU.mult, ALU.add, rd + [pv], [acc])
                    yield
                    for k in range(3):
                        STT(acc.a[:, :N], taps[k], wk[k], acc.a[:, :N], ALU.mult, ALU.add, rd + [acc, pv], [acc])
                        yield
                    CP(bcb.a[:, :N], acc.a[:, :N], [acc], [bcb])

            def p2(it, s):
                for j, blk, st_ in subs(it, s):
                    N, bcb = NW[blk], st_["bcb"]
                    sx, wxv, sg, wgv = wts(j)
                    cs = slice((j % 2) * 128, (j % 2 + 1) * 128)
                    lj = l * 8 + j
                    if blk != 2:
                        psg, psr, psi = P.get(), P.get(), P.get()
                        og, orr, oi = psg.a[:, :N], psr.a[:, :N], psi.a[:, :N]
                    else:
                        psg = psr = psi = P.get()
                        og, orr, oi = psg.a[:, 0:N], psg.a[:, N:2 * N], psg.a[:, 2 * N:3 * N]
                    mm(psg, og, xn_pairs(wgv, cs, blk, N), [sg])
                    MM(psr, orr, lruwa.t[:, lj, :], bcb.a[:, :N], True, True, [lruwa, bcb], True)
                    MM(psi, oi, lruwx.t[:, lj, :], bcb.a[:, :N], True, True, [lruwx, bcb], True)
                    st_.update(bg=psg, br=psr, bi=psi, og=og, orr=orr, oi=oi)

            def p3(it, s):
                ss = subs(it, s)
                for j, blk, st_ in ss:
                    N, lj = NW[blk], l * 8 + j
                    r, ii, sq, gg = tmp("r", blk), tmp("i", blk), tmp("sq", blk), tmp("gg", blk)
                    st_.update(r=r, ii=ii, sq=sq, gg=gg)
                    ACT(gg.a[:, :N], st_["og"], AF.Gelu_apprx_tanh, [st_["bg"]], [gg])
                    ACT(r.a[:, :N], st_["orr"], AF.Tanh, [st_["br"], hbv], [r], bias=hbv.t[:, lj:lj + 1], scale=0.5)
                    ACT(ii.a[:, :N], st_["oi"], AF.Tanh, [st_["bi"], hbv], [ii], bias=hbv.t[:, 16 + lj:17 + lj], scale=0.5)
                for j, blk, st_ in ss:
                    N, lj, r, sq = NW[blk], l * 8 + j, st_["r"], st_["sq"]
                    ACT(sq.a[:, :N], r.a[:, :N], AF.Exp, [r, clv], [sq], scale=clv.t[:, lj:lj + 1], bias=clv.t[:, lj:lj + 1])
                    ACT(r.a[:, :N], r.a[:, :N], AF.Exp, [r, clv], [r], scale=clv.t[:, 16 + lj:17 + lj], bias=clv.t[:, 16 + lj:17 + lj])
                    ACT(sq.a[:, :N], sq.a[:, :N], AF.Ln, [sq], [sq], scale=-1.0, bias=1.0)
                    ACT(sq.a[:, :N], sq.a[:, :N], AF.Exp, [sq], [sq], scale=0.5)

            def p4(it, s):
                for j, blk, st_ in subs(it, s):
                    N, acc, r, ii, sq, gg = NW[blk], st_["acc"], st_["r"], st_["ii"], st_["sq"], st_["gg"]
                    STT(ii.a[:, :N], ii.a[:, :N], 1.0, acc.a[:, :N], ALU.add, ALU.mult, [ii, acc], [ii])
                    yield
                    STT(ii.a[:, :N], ii.a[:, :N], 0.5, sq.a[:, :N], ALU.mult, ALU.mult, [ii, sq], [ii])
                    yield
                    h = sq
                    if blk != 2:
                        hv = ch_v[l][j]
                        S.op("dve", lambda e, h=h, r=r, ii=ii, hv=hv, N=N: e.tensor_tensor_scan(h.a[:, :N], r.a[:, :N], ii.a[:, :N], hv.a,
                                                                                              ALU.mult, ALU.add), [r, ii, hv], [h])
                        yield
                        CP(hv.a, h.a[:, N - 1:N], [h], [hv])
                    else:
                        TT(h.a[:, :N], r.a[:, :N], h0t.t[:, j, :], ALU.mult, [r, h0_v[j]], [h])
                        TT(h.a[:, :N], h.a[:, :N], ii.a[:, :N], ALU.add, [h, ii], [h])
                        CP(h0t.t[:, j, :], h.a[:, :N], [h], [h0_v[j]])
                    yield
                    TT(yb[j][blk].a[:, :N], gg.a[:, :N], h.a[:, :N], ALU.mult, [gg, h], [yb[j][blk]])

            run_pipelined(items, [p0, p1, p2, p3, p4])
            if has_s:
                for g in range(2):
                    emit_rows([ex3.t[:, g * 4 + i, :, 3] for i in range(4)], NSMP, o_rc_s[l, :, 2, g * 512:(g + 1) * 512], ex3_v)
                    emit_rows([h0t.t[:, g * 4 + i, :] for i in range(4)], NSMP, o_h_s[l, :, g * 512:(g + 1) * 512], h0_v)

        def rstd_pow(ssq, n):
            S.op("pool", lambda e: e.tensor_scalar(ssq.a[0:n, :], ssq.a[0:n, :], 1.0 / 512, EPS, ALU.mult, ALU.add), [ssq], [ssq])
            S.op("pool", lambda e: e.tensor_tensor(ssq.a[0:n, :], ssq.a[0:n, :], mhalf.a[0:n, :], ALU.pow), [ssq, mhalf], [ssq])

        def stage_chunk(c, l):
            svd, wvv = fetch2(w_in[l][:, C_V:C_V + 512])
            sus = [fetch([w_in[l][:, C_U + h_ * 256:C_U + (h_ + 1) * 256]]) for h_ in range(2)]

            def u_pairs(g, blk, N):
                return xn_pairs(sus[g // 2][1][0], slice((g % 2) * 128, (g % 2 + 1) * 128), blk, N), [sus[g // 2][0]]
            if 2 in c.blocks:
                N = NSMP
                ps = P.get()
                mm(ps, ps.a[0:N, :], [(xn[kt][2].a[:, 0:N], wvv[:, kt, :], [xn[kt][2]]) for kt in range(8)], svd)
                vg, junk, ssq, vnf = T["r"].get(), T["sq"].get(), T["ssq"].get(), T["acc"].get()
                ACT(vg.a[0:N, :512], ps.a[0:N, :], AF.Gelu_apprx_tanh, [ps], [vg])
                ACT(junk.a[0:N, :512], vg.a[0:N, :512], AF.Square, [vg], [junk, ssq], accum_out=ssq.a[0:N, :])
                rstd_pow(ssq, N)
                STT(vnf.a[0:N, :512], vg.a[0:N, :512], ssq.a[0:N, :], vgb.t[0:N, l, :], ALU.mult, ALU.mult, [vg, ssq, vgb], [vnf])
                S.dma("sp", ch_cv, o_cv_s[l], vnf.a[0:N, :512], reads=[vnf], final=True)
                for g in range(4):
                    pst = P.get()
                    TR(pst, pst.a[:, 0:N], vnf.a[0:N, g * 128:(g + 1) * 128], ident.a[0:N, 0:N], [vnf])
                    mixs = T["i"].get()
                    lg = l * 4 + g
                    ACT(mixs.a[:, :N], pst.a[:, :N], AF.Identity, [pst, wsb, bsb], [mixs], scale=wsb.t[:, lg:lg + 1], bias=bsb.t[:, lg:lg + 1])
                    psu = P.get()
                    prs, rds = u_pairs(g, 2, N)
                    mm(psu, psu.a[:, :N], prs, rds)
                    ug = T["gg"].get()
                    ACT(ug.a[:, :N], psu.a[:, :N], AF.Gelu_apprx_tanh, [psu], [ug])
                    TT(yc[g][2].a[:, :N], ug.a[:, :N], mixs.a[:, :N], ALU.mult, [ug, mixs], [yc[g][2]])
            mixb = P.b[0:4]
            PG = Ring(P.b[4:8])
            for blk in range(2):
                items = [("v", c4) for c4 in range(4)] + [("u", g) for g in range(4)]

                def q0(it, s, blk=blk):
                    kind, i = it
                    ps = PG.get()
                    if kind == "v":
                        tc = slice(i * 128, (i + 1) * 128)
                        mm(ps, ps.a, [(xn[kt][blk].a[:, tc], wvv[:, kt, :], [xn[kt][blk]]) for kt in range(8)], svd)
                        vg, junk, ssq = T["r"].get(), T["sq"].get(), T["ssq"].get()
                        s.update(vg=vg, ssq=ssq)
                        ACT(vg.a[:, :512], ps.a, AF.Gelu_apprx_tanh, [ps], [vg])
                        ACT(junk.a[:, :512], vg.a[:, :512], AF.Square, [vg], [junk, ssq], accum_out=ssq.a)
                    else:
                        prs, rds = u_pairs(i, blk, 512)
                        mm(ps, ps.a, prs, rds)
                        ug = T["gg"].get()
                        s["ug"] = ug
                        ACT(ug.a[:, :512], ps.a, AF.Gelu_apprx_tanh, [ps], [ug])

                def q1(it, s, blk=blk):
                    kind, i = it
                    if kind == "v":
                        tc = slice(i * 128, (i + 1) * 128)
                        vg, ssq, vn = s["vg"], s["ssq"], T["vn"].get()
                        rstd_pow(ssq, 128)
                        STT(vn.a, vg.a[:, :512], ssq.a, vgb.t[:, l, :], ALU.mult, ALU.mult, [vg, ssq, vgb], [vn])
                        for g in range(4):
                            lg = l * 4 + g
                            MM(mixb[g], mixb[g].a[:, tc], vn.a[:, g * 128:(g + 1) * 128], wsT.t[:, lg, :], True, True, [vn, wsT], g == 3)
                    else:
                        ug, tb = s["ug"], T["i"].get()
                        lg = l * 4 + i
                        TT(tb.a[:, :512].rearrange("p (a b) -> p a b", a=4), mixb[i].a.rearrange("p (a b) -> p a b", a=4),
                           bsbt.t[:, lg, :].unsqueeze(1).to_broadcast([128, 4, 128]), ALU.add, [mixb[i], bsbt], [tb])
                        TT(yc[i][blk].a, ug.a[:, :512], tb.a[:, :512], ALU.mult, [ug, tb], [yc[i][blk]])

                run_pipelined(items, [q0, q1], ascending=True)


        def stage_merge(c, l):
            for mg in range(4):
                c0 = mg * 256
                sA, (gAv,) = fetch([w_in[l][:, C_G + c0:C_G + c0 + 256]])
                sP, (pav, pcv) = fetch([w_pa[l][:, c0:c0 + 256], w_pc[l][:, c0:c0 + 256]])
                sB, (gBv,) = fetch([w_in[l][:, C_G + 1024 + c0:C_G + 1024 + c0 + 256]])
                sPb, (pbv,) = fetch([w_pb[l][:, c0:c0 + 256]])
                sC, (gCv,) = fetch([w_in[l][:, C_G + 2048 + c0:C_G + 2048 + c0 + 256]])
                plan = [(gAv, sA, pav, sP, ya, 4), (gBv, sB, pbv, sPb, yb, 8), (gCv, sC, pcv, sP, yc, 4)]
                for m2 in range(2):
                    m = mg * 2 + m2
                    cs = slice(m2 * 128, (m2 + 1) * 128)
                    for blk in c.blocks:
                        N = NW[blk]
                        acc, tmp = T["acc"].get(), T["i"].get()
                        for bi, (gv, gs, pw, pslot, ys, nk) in enumerate(plan):
                            psg = P.get()
                            mm(psg, psg.a[:, :N], xn_pairs(gv, cs, blk, N), [gs] + xn_reads(blk))
                            gt = T["gg"].get()
                            ACT(gt.a[:, :N], psg.a[:, :N], AF.Sigmoid, [psg], [gt])
                            psp = P.get()
                            mm(psp, psp.a[:, :N], [(pw[:, kt, cs], ys[kt][blk].a[:, :N], [ys[kt][blk]]) for kt in range(nk)], [pslot])
                            if bi == 0:
                                TT(acc.a[:, :N], gt.a[:, :N], psp.a[:, :N], ALU.mult, [gt, psp], [acc])
                            else:
                                TT(tmp.a[:, :N], gt.a[:, :N], psp.a[:, :N], ALU.mult, [gt, psp], [tmp])
                                if bi == 1:
                                    TT(acc.a[:, :N], acc.a[:, :N], tmp.a[:, :N], ALU.add, [acc, tmp], [acc])
                                else:
                                    TT(mrg[m][blk].a[:, :N], acc.a[:, :N], tmp.a[:, :N], ALU.add, [acc, tmp], [mrg[m][blk]])

        def stage_wo(c, l, after_blk):
            def make_group(cg):
                so, (wov,) = fetch([w_o[l][:, cg * 256:(cg + 1) * 256]])

                def grp(m4, blk):
                    m, N = cg * 2 + m4, NW[blk]
                    ps = P.get()
                    mm(ps, ps.a[:, :N],
                       [(wov[:, kt, m4 * 128:(m4 + 1) * 128], mrg[kt][blk].a[:, :N], [mrg[kt][blk]]) for kt in range(8)], [so])
                    TT(x[m][blk].a[:, :N], x[m][blk].a[:, :N], ps.a[:, :N], ALU.add, [x[m][blk], ps], [x[m][blk]])
                return grp
            residual_stage(c, 4, make_group, after_blk)

        def stage_ffn_up(c, l):
            items = [(t, blk) for tp in range(0, 24, 2) for blk in c.blocks for t in (tp, tp + 1)]
            sl = {}

            def wts(t):
                cg = t // 2
                if cg not in sl:
                    sg, (wgv,) = fetch([ffn_wg[l][:, cg * 256:(cg + 1) * 256]])
                    su, (wuv,) = fetch([ffn_wu[l][:, cg * 256:(cg + 1) * 256]])
                    sl[cg] = (sg, wgv, su, wuv)
                return sl[cg]

            def p0(it, s):
                t, blk = it
                N = NW[blk]
                sg, wgv, su, wuv = wts(t)
                cs = slice((t % 2) * 128, (t % 2 + 1) * 128)
                ps = P.get()
                mm(ps, ps.a[:, :N], xn_pairs(wgv, cs, blk, N), [sg] + xn_reads(blk))
                if blk != 2:
                    ext = T["ext"].get()
                    cvi, cvo = cf_v[l][t][blk], cf_v[l][t][1 - blk]
                    CP(ext.a[:, 0:2], cvi.a, [cvi], [ext])
                    CP(ext.a[:, 2:514], ps.a, [ps], [ext], eng="act")
                    CP(cvo.a, ps.a[:, 512 - 2:512], [ps], [cvo], eng="act")
                    s["ext"] = ext
                else:
                    CP(ex2.t[:, t, :, 2], ps.a[:, :N], [ps], [ex2_v[t]], eng="act")

            def p1(it, s):
                t, blk = it
                N = NW[blk]
                wk = [pvc(PV_FCW + (l * 3 + k) * 24 + t) for k in range(3)]
                cb = pvc(PV_FCB + l * 24 + t)
                acc = T["acc"].get()
                s["acc"] = acc
                if blk != 2:
                    ext = s["ext"]
                    TS(acc.a[:, :N], ext.a[:, 2:514], wk[2], cb, ALU.mult, ALU.add, [ext, pv], [acc])
                    for k in range(2):
                        STT(acc.a[:, :N], ext.a[:, k:k + 512], wk[k], acc.a[:, :N], ALU.mult, ALU.add, [ext, acc, pv], [acc])
                else:
                    ev = ex2_v[t]
                    TS(acc.a[:, :N], ex2.t[:, t, :, 2], wk[2], cb, ALU.mult, ALU.add, [ev, pv], [acc])
                    for k in range(2):
                        STT(acc.a[:, :N], ex2.t[:, t, :, k], wk[k], acc.a[:, :N], ALU.mult, ALU.add, [ev, acc, pv], [acc])

            def p2(it, s):
                t, blk = it
                N, acc = NW[blk], s["acc"]
                sg, wgv, su, wuv = wts(t)
                cs = slice((t % 2) * 128, (t % 2 + 1) * 128)
                ACT(acc.a[:, :N], acc.a[:, :N], AF.Gelu_apprx_tanh, [acc], [acc])
                psu = P.get()
                mm(psu, psu.a[:, :N], xn_pairs(wuv, cs, blk, N), [su] + xn_reads(blk))
                s["psu"] = psu

            def p3(it, s):
                t, blk = it
                N, acc, psu = NW[blk], s["acc"], s["psu"]
                TT(hf[t][blk].a[:, :N], acc.a[:, :N], psu.a[:, :N], ALU.mult, [acc, psu], [hf[t][blk]])
                if blk == c.blocks[-1] and t % 4 == 3 and 2 in c.blocks:
                    cg = t // 4
                    emit_rows([ex2.t[:, cg * 4 + i, :, 2] for i in range(4)], NSMP, o_ff_s[l, :, 1, cg * 512:(cg + 1) * 512],
                              ex2_v[cg * 4:cg * 4 + 4])

            run_pipelined(items, [p0, p1, p2, p3])


        def stage_ffn_down(c, l, after_blk):
            def make_group(cg):
                fs = [fetch([ffn_wd[l][kc * 1024:(kc + 1) * 1024, cg * 256:(cg + 1) * 256]]) for kc in range(3)]

                def grp(m4, blk):
                    m, N = cg * 2 + m4, NW[blk]
                    ps = P.get()
                    mm(ps, ps.a[:, :N],
                       [(fs[kt // 8][1][0][:, kt % 8, m4 * 128:(m4 + 1) * 128], hf[kt][blk].a[:, :N], [hf[kt][blk]]) for kt in range(24)],
                       [f[0] for f in fs])
                    TT(x[m][blk].a[:, :N], x[m][blk].a[:, :N], ps.a[:, :N], ALU.add, [x[m][blk], ps], [x[m][blk]])
                return grp
            residual_stage(c, 4, make_group, after_blk)

        def load_sample_state(l):
            for hs in range(2):
                rw = T["rows"].get()
                S.dma("sp", rw.ch, rw.a[0:120, :], st_pool[l, hs * 8:(hs + 1) * 8].rearrange("s r c -> (s r) c"), writes=[rw])
                ps = P.get()
                for j in range(4):
                    TR(ps, ps.a[:, j * 120:(j + 1) * 120], rw.a[0:120, j * 128:(j + 1) * 128], ident.a[0:120, 0:120], [rw], sig=(j == 3))
                for j in range(4):
                    CP(exs.t[:, j, hs * 8:(hs + 1) * 8, 0:15], ps.a[:, j * 120:(j + 1) * 120].rearrange("p (s r) -> p s r", r=15),
                       [ps], [exs_v[j]])
            stg = StageA
            stg.load(st_rc[l].rearrange("s r c -> (s r) c"), 48)
            for g in range(2):
                ps = P.get()
                for i in range(4):
                    j = g * 4 + i
                    TR(ps, ps.a[:, i * 48:(i + 1) * 48], stg.ap(j, 48), ident.a[0:48, 0:48], stg.deps, sig=(i == 3))
                for i in range(4):
                    j = g * 4 + i
                    CP(ex3.t[:, j, :, 0:3], ps.a[:, i * 48:(i + 1) * 48].rearrange("p (s r) -> p s r", r=3), [ps], [ex3_v[j]])
            stg = StageB
            stg.load(st_h[l], NSMP)
            for g in range(2):
                ps = P.get()
                for i in range(4):
                    j = g * 4 + i
                    TR(ps, ps.a[:, i * NSMP:(i + 1) * NSMP], stg.ap(j, NSMP), ident.a[0:NSMP, 0:NSMP], stg.deps, sig=(i == 3))
                CP(h0t.t[:, g * 4:(g + 1) * 4, :], ps.a[:, 0:4 * NSMP].rearrange("p (a b) -> p a b", a=4), [ps],
                   [h0_v[g * 4 + i] for i in range(4)])
            for q in range(3):
                stg = stages[q % 2]
                stg.load(st_ff[l][:, :, q * 1024:(q + 1) * 1024].rearrange("s r c -> (s r) c"), 32)
                for g in range(2):
                    ps = P.get()
                    for i in range(4):
                        t = q * 8 + g * 4 + i
                        TR(ps, ps.a[:, i * 32:(i + 1) * 32], stg.ap(g * 4 + i, 32), ident.a[0:32, 0:32], stg.deps, sig=(i == 3))
                    for i in range(4):
                        t = q * 8 + g * 4 + i
                        CP(ex2.t[:, t, :, 0:2], ps.a[:, i * 32:(i + 1) * 32].rearrange("p (s r) -> p s r", r=2), [ps], [ex2_v[t]])
            S.dma("sp", ch_d2d, o_pool_s[l, :, 0:14, :], st_pool[l, :, 1:15, :], final=True)
            S.dma("sp", ch_d2d, o_rc_s[l, :, 0:2, :], st_rc[l, :, 1:3, :], final=True)
            S.dma("sp", ch_d2d, o_ff_s[l, :, 0:1, :], st_ff[l, :, 1:2, :], final=True)

        def emit_prompt_state(l):
            emit_rows([cp_v[l][j][0].a for j in range(4)], 15, o_pool_p[l], [cp_v[l][j][0] for j in range(4)])
            for g in range(2):
                emit_rows([cr_v[l][g * 4 + i][0].a for i in range(4)], 3, o_rc_p[l, :, g * 512:(g + 1) * 512],
                          [cr_v[l][g * 4 + i][0] for i in range(4)])
                emit_rows([ch_v[l][g * 4 + i].a for i in range(4)], 1, o_h_p[l, :, g * 512:(g + 1) * 512], ch_v[l])
            for g in range(6):
                emit_rows([cf_v[l][g * 4 + i][0].a for i in range(4)], 2, o_ff_p[l, :, g * 512:(g + 1) * 512],
                          [cf_v[l][g * 4 + i][0] for i in range(4)])

        def final_store_blk(c, blk):
            if True:
                N = NW[blk]
                rs = rstd_blk(c, blk)
                for g in range(2):
                    yts = [T["acc"].get(), T["i"].get(), T["gg"].get(), T["sq"].get()]
                    for i in range(4):
                        kt = g * 4 + i
                        STT(yts[i].a[:, :N], x[kt][blk].a[:, :N], pvc(PV_FNG + kt), rs.a[:, :N], ALU.mult, ALU.mult,
                            [x[kt][blk], rs, pv], [yts[i]])
                    if blk == 2:
                        ps = P.get()
                        for i in range(4):
                            TR(ps, ps.a[0:N, i * 128:(i + 1) * 128], yts[i].a[:, 0:N], ident.a, [yts[i]], sig=(i == 3))
                        rw = T["rows"].get()
                        CP(rw.a[0:N, :], ps.a[0:N, :], [ps], [rw])
                        S.dma("sp", rw.ch, y_sample[:, g * 512:(g + 1) * 512], rw.a[0:N, :], reads=[rw], final=True)
                    else:
                        for tt in range(4):
                            ps = P.get()
                            for i in range(4):
                                TR(ps, ps.a[:, i * 128:(i + 1) * 128], yts[i].a[:, tt * 128:(tt + 1) * 128], ident.a, [yts[i]], sig=(i == 3))
                            rw = T["rows"].get()
                            CP(rw.a, ps.a, [ps], [rw], eng="act" if tt % 2 == 0 else "dve")
                            r0 = c.tok0 + blk * NB + tt * 128
                            S.dma("sp", rw.ch, y_prompt[r0:r0 + 128, g * 512:(g + 1) * 512], rw.a, reads=[rw], final=True)

        marks = []

        def mark(name):
            marks.append((name, dict(S.nins)))

        def layer(c, l):
            mark(f"L{l}.smpstate")
            if 2 in c.blocks:
                load_sample_state(l)
            mark(f"L{l}.norm1")
            mark(f"L{l}.pool")
            stage_pool(c, l)
            mark(f"L{l}.rnn")
            stage_rnn(c, l)
            mark(f"L{l}.chunk")
            stage_chunk(c, l)
            mark(f"L{l}.merge")
            stage_merge(c, l)
            mark(f"L{l}.wo")
            stage_wo(c, l, lambda blk: norm_blk(c, PV_N2G + l * 8, blk))
            mark(f"L{l}.ffn_up")
            stage_ffn_up(c, l)
            mark(f"L{l}.ffn_down")
            if l + 1 < NL:
                stage_ffn_down(c, l, lambda blk: norm_blk(c, PV_N1G + (l + 1) * 8, blk))
            else:
                stage_ffn_down(c, l, lambda blk: final_store_blk(c, blk))

        for c in (Ctx(0, (0, 1)), Ctx(TH, (0, 1, 2))):
            mark("load_x")
            load_x(c)
            for l in range(NL):
                layer(c, l)
                if c.tok0 == TH:
                    emit_prompt_state(l)
        S.finish()
        build_program.stats = dict(nins=dict(S.nins), nwait=S.nwait)
        build_program.log = S.log
        mark("end")
        build_program.marks = marks
    return nc


_NC = None


def _pack_pvec(p):
    rows = [p["norm1_g"].reshape(16, 128), p["norm2_g"].reshape(16, 128), p["pool_scale"].reshape(8, 128),
            p["rnn_conv_w"].reshape(64, 128), p["rnn_conv_b"].reshape(16, 128), p["lru_ba"].reshape(16, 128),
            p["lru_bx"].reshape(16, 128), p["lru_lambda"].reshape(16, 128), p["ffn_conv_w"].reshape(144, 128),
            p["ffn_conv_b"].reshape(48, 128), p["final_norm_g"].reshape(8, 128)]
    pv = np.concatenate(rows, axis=0)
    out = np.zeros((PV_ROWS, 128), np.float32)
    out[:pv.shape[0]] = pv
    return out


def kernel(**inputs):
    global _NC
    p = {k: np.ascontiguousarray(np.asarray(v, dtype=np.float32)) for k, v in inputs.items()}
    if _NC is None:
        _NC = build_program()
    nc = _NC
    ncore = 8
    shared = {k: p[k] for k in ("w_in", "pool_w", "lru_wa", "lru_wx", "chunk_ws", "chunk_bs", "chunk_vnorm_g", "w_pa", "w_pb",
                                "w_pc", "w_o", "ffn_wg", "ffn_wu", "ffn_wd")}
    shared["pvec"] = _pack_pvec(p)
    shared["ident"] = np.eye(128, dtype=np.float32)
    shared["tril"] = np.tril(np.ones((128, 128), np.float32))
    ic = np.zeros((4, 16), np.float32)
    for j, w in enumerate(WIN):
        ic[j] = 1.0 / np.minimum(np.arange(16) + 1, w)
    shared["invcnt"] = ic.reshape(1, 64)
    shared["ws00"] = np.ascontiguousarray(p["chunk_ws"][:, :, 0, 0]).reshape(1, 8)
    shared["bs0"] = np.ascontiguousarray(p["chunk_bs"][:, :, 0]).reshape(1, 8)
    in_maps = []
    for i in range(ncore):
        m = dict(shared)
        m["x_prompt"] = np.ascontiguousarray(p["x_prompt"][i])
        sl = slice(i * NSMP, (i + 1) * NSMP)
        m["x_sample"] = np.ascontiguousarray(p["x_sample"][sl, 0, :])
        m["state_pool"] = np.ascontiguousarray(p["state_pool"][:, sl])
        m["state_rnn_conv"] = np.ascontiguousarray(p["state_rnn_conv"][:, sl])
        m["state_rnn_h"] = np.ascontiguousarray(p["state_rnn_h"][:, sl])
        m["state_ffn_conv"] = np.ascontiguousarray(p["state_ffn_conv"][:, sl])
        in_maps.append(m)
    res = run_bass_kernel_spmd(nc, in_maps, core_ids=list(range(ncore)))
    R = res.results
    f32 = np.float32
    y_prompt = np.stack([R[i]["y_prompt"] for i in range(ncore)], 0).astype(f32)
    y_sample = np.concatenate([R[i]["y_sample"] for i in range(ncore)], 0)[:, None, :].astype(f32)
    pool_p = np.stack([R[i]["new_pool_prompt"] for i in range(ncore)], 1).astype(f32)
    pool_s = np.concatenate([R[i]["new_pool_sample"] for i in range(ncore)], 1).astype(f32)
    rc_p = np.stack([R[i]["new_rconv_prompt"] for i in range(ncore)], 1).astype(f32)
    rc_s = np.concatenate([R[i]["new_rconv_sample"] for i in range(ncore)], 1).astype(f32)
    h_p = np.stack([R[i]["new_h_prompt"][:, 0, :] for i in range(ncore)], 1).astype(f32)
    h_s = np.concatenate([R[i]["new_h_sample"] for i in range(ncore)], 1).astype(f32)
    ff_p = np.stack([R[i]["new_ffn_prompt"] for i in range(ncore)], 1).astype(f32)
    ff_s = np.concatenate([R[i]["new_ffn_sample"] for i in range(ncore)], 1).astype(f32)
    cv_s = np.concatenate([R[i]["new_chunk_v_sample"] for i in range(ncore)], 1)[:, :, None, :].astype(f32)
    return (y_prompt, y_sample, pool_p, pool_s, rc_p, rc_s, h_p, h_s, ff_p, ff_s, cv_s)
```

```python
import numpy as np
from contextlib import ExitStack
import concourse.bass as bass
import concourse.mybir as mybir
from concourse.bass_utils import run_bass_kernel_spmd

F32 = mybir.dt.float32
BF16 = mybir.dt.bfloat16
AF = mybir.ActivationFunctionType
ALU = mybir.AluOpType
AX = mybir.AxisListType


class Dep:
    __slots__ = ("w", "r")

    def __init__(self):
        self.w = []
        self.r = []


class Buf(Dep):
    __slots__ = ("t", "a", "ch")

    def __init__(self, t, a=None, ch=None):
        super().__init__()
        self.t = t
        self.a = t[:] if a is None else a
        self.ch = ch


def V(ap):
    return Buf(None, ap)


class Sched:
    def __init__(self, nc, st):
        self.nc, self.st = nc, st
        self.eng = dict(pe=nc.tensor, act=nc.scalar, dve=nc.vector, pool=nc.gpsimd, sp=nc.sync)
        self.sems, self.cnt = {}, {}
        for k in self.eng:
            self.sems[k] = st.enter_context(nc.semaphore("sem_" + k))
            self.cnt[k] = 0
        self.waited = {k: {} for k in self.eng}
        self.finals = []
        self.nwait = 0
        self.nins = {k: 0 for k in self.eng}
        self.log = {k: [] for k in self.eng}

    def sb(self, name, shape, dtype):
        return Buf(self.st.enter_context(self.nc.sbuf_tensor(name, list(shape), dtype)))

    def ps(self, name):
        return Buf(self.st.enter_context(self.nc.psum_tensor(name, [128, 512], F32)))

    def chan(self, name):
        self.sems[name] = self.st.enter_context(self.nc.semaphore("sem_" + name))
        self.cnt[name] = 0
        return name

    def _wait(self, e, evs):
        need = {}
        for (k, v) in evs:
            if v > need.get(k, 0):
                need[k] = v
        wd = self.waited[e]
        for k, v in need.items():
            if wd.get(k, 0) >= v:
                continue
            if k == e and e == "pe":
                continue
            self.eng[e].wait_ge(self.sems[k], v)
            self.log[e].append(("w", k, v))
            self.nwait += 1
            wd[k] = v

    def _deps(self, reads, writes):
        evs = []
        for d in reads:
            evs += d.w
        for d in writes:
            evs += d.w
            evs += d.r
        return evs

    def _record(self, ev, reads, writes):
        for d in reads:
            d.r.append(ev)
            if len(d.r) > 24:
                d.r = _compact(d.r)
        for d in writes:
            d.w = [ev]
            d.r = []

    def op(self, e, fn, reads=(), writes=(), sig=True):
        self._wait(e, self._deps(reads, writes))
        ins = fn(self.eng[e])
        self.nins[e] += 1
        if sig:
            self.cnt[e] += 1
            ins.then_inc(self.sems[e], 1)
            self.log[e].append(("i", e, 1))
            ev = (e, self.cnt[e])
        else:
            ev = (e, self.cnt[e] + 1)
        self._record(ev, reads, writes)
        return ev

    def sbc(self, name, shape, dtype):
        b = self.sb(name, shape, dtype)
        b.ch = self.chan("c_" + name)
        return b

    def batch_end(self, ch, deps):
        for d in deps:
            d.w = [(ch, self.cnt[ch])]

    def dma(self, q, ch, out, in_, reads=(), writes=(), final=False, join=(), **kw):
        self._wait(q, self._deps(reads, writes))
        ins = self.eng[q].dma_start(out=out, in_=in_, **kw)
        self.nins[q] += 1
        self.cnt[ch] += 16
        ins.then_inc(self.sems[ch], 16)
        self.log[q].append(("i", ch, 16))
        ev = (ch, self.cnt[ch])
        self._record(ev, reads, writes)
        for d in join:
            d.w.append(ev)
        if final:
            self.finals.append(ev)
        return ev

    def finish(self):
        self._wait("sp", self.finals)
        for e in ("act", "dve", "pool", "pe"):
            self._wait(e, self.finals)


def _compact(evs):
    m = {}
    for k, v in evs:
        if v > m.get(k, 0):
            m[k] = v
    return list(m.items())


NL, D, KT, NB = 2, 1024, 8, 512
SEQ, TH = 2048, 1024
NSMP = 16
WIN = (2, 4, 8, 16)
EPS = 1e-6
C_POOL, C_RX, C_RG, C_U, C_V, C_G = 0, 512, 1536, 2560, 3072, 3584
PV_N1G, PV_N2G, PV_PSC, PV_RCW, PV_RCB, PV_LBA, PV_LBX, PV_LAM, PV_FCW, PV_FCB, PV_FNG = (
    0, 16, 32, 40, 104, 120, 136, 152, 168, 312, 360)
PV_ROWS = 384
NSLOT = 8
SLOTW = 2048


class Ring:
    def __init__(self, bufs):
        self.b, self.i = bufs, 0

    def get(self):
        b = self.b[self.i % len(self.b)]
        self.i += 1
        return b


NW = (NB, NB, NSMP)


class Ctx:
    def __init__(self, tok0, blocks):
        self.tok0, self.blocks = tok0, blocks


def build_program():
    nc = bass.Bass("TRN2", target_bir_lowering=False)

    def din(n, s):
        return nc.dram_tensor(n, list(s), F32, kind="ExternalInput").ap()

    def dout(n, s):
        return nc.dram_tensor(n, list(s), F32, kind="ExternalOutput").ap()

    x_prompt = din("x_prompt", [SEQ, D])
    x_sample = din("x_sample", [NSMP, D])
    st_pool = din("state_pool", [NL, NSMP, 15, 512])
    st_rc = din("state_rnn_conv", [NL, NSMP, 3, 1024])
    st_h = din("state_rnn_h", [NL, NSMP, 1024])
    st_ff = din("state_ffn_conv", [NL, NSMP, 2, 3072])
    w_in = din("w_in", [NL, 1024, 6656])
    pool_w = din("pool_w", [NL, 4, 128, 128])
    lru_wa = din("lru_wa", [NL, 8, 128, 128])
    lru_wx = din("lru_wx", [NL, 8, 128, 128])
    chunk_ws = din("chunk_ws", [NL, 4, 128, 128])
    chunk_bs = din("chunk_bs", [NL, 4, 128])
    chunk_vg = din("chunk_vnorm_g", [NL, 512])
    w_pa = din("w_pa", [NL, 512, 1024])
    w_pb = din("w_pb", [NL, 1024, 1024])
    w_pc = din("w_pc", [NL, 512, 1024])
    w_o = din("w_o", [NL, 1024, 1024])
    ffn_wg = din("ffn_wg", [NL, 1024, 3072])
    ffn_wu = din("ffn_wu", [NL, 1024, 3072])
    ffn_wd = din("ffn_wd", [NL, 3072, 1024])
    pvec = din("pvec", [PV_ROWS, 128])
    ident_d = din("ident", [128, 128])
    tril_d = din("tril", [128, 128])
    invcnt_d = din("invcnt", [1, 64])
    ws00_d = din("ws00", [1, 8])
    bs0_d = din("bs0", [1, 8])

    y_prompt = dout("y_prompt", [SEQ, D])
    y_sample = dout("y_sample", [NSMP, D])
    o_pool_p = dout("new_pool_prompt", [NL, 15, 512])
    o_pool_s = dout("new_pool_sample", [NL, NSMP, 15, 512])
    o_rc_p = dout("new_rconv_prompt", [NL, 3, 1024])
    o_rc_s = dout("new_rconv_sample", [NL, NSMP, 3, 1024])
    o_h_p = dout("new_h_prompt", [NL, 1, 1024])
    o_h_s = dout("new_h_sample", [NL, NSMP, 1024])
    o_ff_p = dout("new_ffn_prompt", [NL, 2, 3072])
    o_ff_s = dout("new_ffn_sample", [NL, NSMP, 2, 3072])
    o_cv_s = dout("new_chunk_v_sample", [NL, NSMP, 512])

    with ExitStack() as st:
        S = Sched(nc, st)
        xt = S.sb("xt", [128, 8, TH], F32)
        xnt = S.sb("xnt", [128, 8, TH], BF16)
        hft = S.sb("hft", [128, 24, TH], BF16)
        xst = S.sb("xst", [128, 8, NSMP], F32)
        xnst = S.sb("xnst", [128, 8, NSMP], BF16)
        hfst = S.sb("hfst", [128, 24, NSMP], BF16)
        x = [[V(xt.t[:, k, b * NB:(b + 1) * NB]) for b in range(2)] + [V(xst.t[:, k, :])] for k in range(8)]
        xn = [[V(xnt.t[:, k, b * NB:(b + 1) * NB]) for b in range(2)] + [V(xnst.t[:, k, :])] for k in range(8)]
        hf = [[V(hft.t[:, k, b * NB:(b + 1) * NB]) for b in range(2)] + [V(hfst.t[:, k, :])] for k in range(24)]
        ya, yb, yc, mrg = hf[0:4], hf[4:12], hf[12:16], hf[16:24]

        TW = 528
        T = {}
        for role, n in (("ext", 2), ("acc", 3), ("r", 2), ("i", 2), ("sq", 2), ("gg", 2)):
            T[role] = Ring([S.sb(f"t_{role}{i}", [128, TW], F32) for i in range(n)])
        T["bcb"] = Ring([S.sb(f"t_bcb{i}", [128, NB], BF16) for i in range(2)])
        TSm = {}
        tsm_t = S.sb("tsm", [128, 11, NSMP], F32)
        tsm_b = S.sb("tsmb", [128, 2, NSMP], BF16)
        k_ = 0
        for role, n in (("acc", 3), ("r", 2), ("i", 2), ("sq", 2), ("gg", 2)):
            TSm[role] = Ring([V(tsm_t.t[:, k_ + i, :]) for i in range(n)])
            k_ += n
        TSm["bcb"] = Ring([V(tsm_b.t[:, i, :]) for i in range(2)])
        T["sqn"] = Ring([S.sb(f"t_sqn{i}", [128, NB], BF16) for i in range(2)])
        T["vn"] = Ring([S.sb(f"t_vn{i}", [128, NB], BF16) for i in range(2)])
        ssq_t = S.sb("ssq", [128, 4], F32)
        T["ssq"] = Ring([V(ssq_t.t[:, i:i + 1]) for i in range(4)])
        P = Ring([S.ps(f"ps{i}") for i in range(8)])

        xstage = S.sbc("xstage", [128, D], F32)
        T["rows"] = Ring([S.sbc(f"rows{i}", [128, NB], F32) for i in range(2)])

        wring = S.sb("wring", [128, NSLOT * SLOTW], BF16)
        slots = [Buf(None, wring.t[:, i * SLOTW:(i + 1) * SLOTW], S.chan(f"c_wslot{i}")) for i in range(NSLOT)]
        wctr = [0]

        ident = S.sb("ident_sb", [128, 128], F32)
        tril = S.sb("tril_sb", [128, 128], F32)
        pv = S.sb("pv", [128, PV_ROWS], F32)
        clv = S.sb("clv", [128, 32], F32)
        hbv = S.sb("hbv", [128, 32], F32)
        vgb = S.sb("vgb", [128, NL, 512], F32)
        icn = S.sb("icn", [128, 64], F32)
        wsb = S.sb("wsb", [128, 8], F32)
        bsb = S.sb("bsb", [128, 8], F32)
        bsbt = S.sb("bsbt", [128, NL * 4, 128], F32)
        mhalf = S.sb("mhalf", [128, 1], F32)
        ones_bf = S.sb("ones_bf", [128, 128], BF16)
        poolw = S.sb("poolw", [128, NL * 4, 128], BF16)
        lruwa = S.sb("lruwa", [128, NL * 8, 128], BF16)
        lruwx = S.sb("lruwx", [128, NL * 8, 128], BF16)
        wsT = S.sb("wsT", [128, NL * 4, 128], BF16)
        cpt = S.sb("cpt", [128, 2, NL, 4, 15], F32)
        crt = S.sb("crt", [128, 2, NL, 8, 3], F32)
        cht = S.sb("cht", [128, NL, 8, 1], F32)
        cft = S.sb("cft", [128, 2, NL, 24, 2], F32)
        cp_v = [[[V(cpt.t[:, q, l, j, :]) for q in range(2)] for j in range(4)] for l in range(NL)]
        cr_v = [[[V(crt.t[:, q, l, j, :]) for q in range(2)] for j in range(8)] for l in range(NL)]
        ch_v = [[V(cht.t[:, l, j, :]) for j in range(8)] for l in range(NL)]
        cf_v = [[[V(cft.t[:, q, l, j, :]) for q in range(2)] for j in range(24)] for l in range(NL)]
        exs = S.sb("exs", [128, 4, NSMP, 16], F32)
        ex3 = S.sb("ex3", [128, 8, NSMP, 4], F32)
        h0t = S.sb("h0t", [128, 8, NSMP], F32)
        ex2 = S.sb("ex2", [128, 24, NSMP, 3], F32)
        exs_v = [V(exs.t[:, j]) for j in range(4)]
        ex3_v = [V(ex3.t[:, j]) for j in range(8)]
        h0_v = [V(h0t.t[:, j]) for j in range(8)]
        ex2_v = [V(ex2.t[:, j]) for j in range(24)]

        ch_misc = S.chan("misc")
        ch_d2d = S.chan("d2d")
        ch_cv = S.chan("cvout")

        def ACT(out, in_, func, reads, writes, **kw):
            S.op("act", lambda e: e.activation(out, in_, func, **kw), reads, writes)

        def TT(out, a, b, op, reads, writes, eng="dve"):
            S.op(eng, lambda e: e.tensor_tensor(out, a, b, op), reads, writes)

        def TS(out, a, s1, s2, op0, op1, reads, writes, eng="dve"):
            S.op(eng, lambda e: e.tensor_scalar(out, a, s1, s2, op0, op1), reads, writes)

        def STT(out, a, s, b, op0, op1, reads, writes):
            S.op("dve", lambda e: e.scalar_tensor_tensor(out, a, s, b, op0, op1), reads, writes)

        def CP(out, in_, reads, writes, eng="dve"):
            if eng == "act":
                S.op("act", lambda e: e.activation(out, in_, AF.Copy), reads, writes)
            else:
                S.op(eng, lambda e: e.tensor_copy(out, in_), reads, writes)

        def MM(psb, out, l, r, start, stop, reads, sig):
            S.op("pe", lambda e: e.matmul(out, l, r, start=start, stop=stop), reads, [psb], sig=sig)

        def mm(psb, out, pairs, reads):
            n = len(pairs)
            for i, pr in enumerate(pairs):
                last = i == n - 1
                rd = list(pr[2]) if len(pr) > 2 else []
                if i == 0 or last:
                    rd = rd + list(reads)
                MM(psb, out, pr[0], pr[1], i == 0, last, rd, last)

        def TR(psb, out, in_, idn, reads, sig=True):
            S.op("pe", lambda e: e.transpose(out, in_, idn), list(reads) + [ident], [psb], sig=sig)

        def pvc(i):
            return pv.t[:, i:i + 1]

        def fetch(pieces):
            slot = slots[wctr[0] % NSLOT]
            wctr[0] += 1
            views, off = [], 0
            for pi, p in enumerate(pieces):
                K, ncol = p.shape
                kt = K // 128
                v = slot.a[:, off:off + kt * ncol].rearrange("p (k c) -> p k c", k=kt)
                S.dma("pool", slot.ch, v, p.rearrange("(k p) c -> p k c", p=128),
                      writes=[slot] if pi == 0 else (), join=() if pi == 0 else [slot])
                views.append(v)
                off += kt * ncol
            assert off <= SLOTW
            return slot, views

        def fetch2(piece):
            if wctr[0] % 2:
                wctr[0] += 1
            i0 = wctr[0] % NSLOT
            wctr[0] += 2
            sa, sb_ = slots[i0], slots[i0 + 1]
            v = wring.t[:, i0 * SLOTW:(i0 + 2) * SLOTW].rearrange("p (k c) -> p k c", k=8)
            S.dma("pool", sa.ch, v, piece.rearrange("(k p) c -> p k c", p=128), writes=[sa, sb_])
            return [sa, sb_], v

        def emit_rows(srcs, n, dst, reads, final=True):
            ps = P.get()
            for i, s_ in enumerate(srcs):
                TR(ps, ps.a[0:n, i * 128:(i + 1) * 128], s_, ident.a, reads, sig=(i == len(srcs) - 1))
            rw = T["rows"].get()
            CP(rw.a[0:n, :], ps.a[0:n, :], [ps], [rw])
            S.dma("sp", rw.ch, dst, rw.a[0:n, :], reads=[rw], final=final)

        class StageA:
            deps = [xstage]

            @staticmethod
            def load(src_ap, n):
                S.dma("sp", xstage.ch, xstage.a[0:n, :], src_ap, writes=[xstage])

            @staticmethod
            def ap(kt, n):
                return xstage.a[0:n, kt * 128:(kt + 1) * 128]

        class StageB:
            deps = list(T["rows"].b)

            @staticmethod
            def load(src_ap, n):
                for h_ in range(2):
                    rb = T["rows"].b[h_]
                    S.dma("sp", rb.ch, rb.a[0:n, :], src_ap[:, h_ * 512:(h_ + 1) * 512], writes=[rb])

            @staticmethod
            def ap(kt, n):
                return T["rows"].b[kt // 4].a[0:n, (kt % 4) * 128:(kt % 4 + 1) * 128]

        stages = [StageA, StageB]

        setup = [ident, tril, vgb, icn, wsb, bsb]
        S.dma("sp", ch_misc, ident.a, ident_d, writes=[ident])
        S.dma("sp", ch_misc, tril.a, tril_d, writes=[tril])
        for l in range(NL):
            S.dma("sp", ch_misc, vgb.t[:, l, :], chunk_vg[l].partition_broadcast(128), writes=[vgb] if l == 0 else (),
                  join=() if l == 0 else [vgb])
        S.dma("sp", ch_misc, icn.a, invcnt_d[0].partition_broadcast(128), writes=[icn])
        S.dma("sp", ch_misc, wsb.a, ws00_d[0].partition_broadcast(128), writes=[wsb])
        S.dma("sp", ch_misc, bsb.a, bs0_d[0].partition_broadcast(128), writes=[bsb])
        S.batch_end(ch_misc, setup)
        rw0 = T["rows"].get()
        S.dma("sp", rw0.ch, rw0.a[:, 0:384].rearrange("p (r c) -> p r c", r=3),
              pvec.rearrange("(r p) c -> p r c", p=128), writes=[rw0])
        S.dma("sp", xstage.ch, xstage.a.rearrange("p (g j) -> p g j", g=8),
              chunk_ws.rearrange("l g i j -> i (l g) j"), writes=[xstage])
        ch_miscw = S.chan("miscw")
        for dst_, src_ in ((poolw, pool_w), (lruwa, lru_wa), (lruwx, lru_wx)):
            S.dma("pool", ch_miscw, dst_.a, src_.rearrange("l g i j -> i (l g) j"), writes=[dst_])
        S.batch_end(ch_miscw, [poolw, lruwa, lruwx])
        S.op("dve", lambda e: e.memset(ones_bf.a, 1.0), (), [ones_bf])
        S.op("dve", lambda e: e.memset(mhalf.a, -0.5), (), [mhalf])
        for cb_ in (cpt, crt, cht, cft):
            S.op("dve", lambda e, cb_=cb_: e.memset(cb_.a, 0.0), (), [cb_])
        for l in range(NL):
            for lst, n in ((cp_v, 4), (cr_v, 8), (cf_v, 24)):
                for j in range(n):
                    for q in range(2):
                        lst[l][j][q].w = [("dve", S.cnt["dve"])]
            for j in range(8):
                ch_v[l][j].w = [("dve", S.cnt["dve"])]
        ps = P.get()
        for r_ in range(3):
            TR(ps, ps.a[:, r_ * 128:(r_ + 1) * 128], rw0.a[:, r_ * 128:(r_ + 1) * 128], ident.a, [rw0], sig=(r_ == 2))
        CP(pv.a, ps.a[:, 0:PV_ROWS], [ps], [pv])
        ACT(clv.t[:, 0:16], pv.t[:, PV_LAM:PV_LAM + 16], AF.Exp, [pv], [clv], scale=-1.0)
        ACT(clv.t[:, 0:16], clv.t[:, 0:16], AF.Ln, [clv], [clv], bias=1.0)
        TS(clv.t[:, 16:32], clv.t[:, 0:16], -4.0, None, ALU.mult, ALU.bypass, [clv], [clv])
        TS(clv.t[:, 0:16], clv.t[:, 0:16], -8.0, None, ALU.mult, ALU.bypass, [clv], [clv])
        TS(hbv.a, pv.t[:, PV_LBA:PV_LBA + 32], 0.5, None, ALU.mult, ALU.bypass, [pv], [hbv])
        for lg in range(8):
            blk_ = xstage.a[:, lg * 128:(lg + 1) * 128]
            TT(blk_, blk_, tril.a, ALU.mult, [xstage, tril], [xstage])
        for h_ in range(2):
            ps = P.get()
            for i in range(4):
                lg = h_ * 4 + i
                TR(ps, ps.a[:, i * 128:(i + 1) * 128], xstage.a[:, lg * 128:(lg + 1) * 128], ident.a, [xstage], sig=(i == 3))
            CP(wsT.t[:, h_ * 4:(h_ + 1) * 4, :], ps.a.rearrange("p (a b) -> p a b", a=4), [ps], [wsT])
        ch_misc2 = S.chan("misc2")
        S.dma("sp", ch_misc2, bsbt.a.rearrange("p a b -> p (a b)"), chunk_bs.rearrange("l g i -> (l g i)").partition_broadcast(128),
              writes=[bsbt])

        def load_x(c):
            if 2 in c.blocks:
                S.dma("sp", xstage.ch, xstage.a[0:NSMP, :], x_sample, writes=[xstage])
                for g in range(2):
                    ps = P.get()
                    for i in range(4):
                        kt = g * 4 + i
                        TR(ps, ps.a[:, i * NSMP:(i + 1) * NSMP], xstage.a[0:NSMP, kt * 128:(kt + 1) * 128],
                           ident.a[0:NSMP, 0:NSMP], [xstage], sig=(i == 3))
                    CP(xst.t[:, g * 4:(g + 1) * 4, :], ps.a[:, 0:4 * NSMP].rearrange("p (a b) -> p a b", a=4),
                       [ps], [x[g * 4 + i][2] for i in range(4)])
            for tt in range(8):
                stg = stages[tt % 2]
                stg.load(x_prompt[c.tok0 + tt * 128:c.tok0 + (tt + 1) * 128, :], 128)
                blk, c0 = tt // 4, (tt % 4) * 128
                for g in range(2):
                    ps = P.get()
                    for i in range(4):
                        kt = g * 4 + i
                        TR(ps, ps.a[:, i * 128:(i + 1) * 128], stg.ap(kt, 128), ident.a, stg.deps, sig=(i == 3))
                    CP(xt.t[:, g * 4:(g + 1) * 4, blk * NB + c0:blk * NB + c0 + 128],
                       ps.a.rearrange("p (a b) -> p a b", a=4), [ps], [x[g * 4 + i][blk] for i in range(4)],
                       eng="act" if g == 0 else "dve")
                if tt % 4 == 3:
                    norm_blk(c, PV_N1G, blk)
            if 2 in c.blocks:
                norm_blk(c, PV_N1G, 2)

        def rstd_blk(c, blk):
            N = NW[blk]
            ps = P.get()
            for kt in range(8):
                sq = T["sqn"].get()
                ACT(sq.a[:, :N], x[kt][blk].a[:, :N], AF.Square, [x[kt][blk]], [sq])
                MM(ps, ps.a[:, :N], ones_bf.a, sq.a[:, :N], kt == 0, kt == 7, [sq, ones_bf], True)
            rs = T["r"].get()
            ACT(rs.a[:, :N], ps.a[:, :N], AF.Ln, [ps], [rs], bias=EPS, scale=1.0 / D)
            ACT(rs.a[:, :N], rs.a[:, :N], AF.Exp, [rs], [rs], scale=-0.5)
            return rs

        def norm_blk(c, g0, blk):
            N = NW[blk]
            rs = rstd_blk(c, blk)
            for kt in range(8):
                STT(xn[kt][blk].a[:, :N], x[kt][blk].a[:, :N], pvc(g0 + kt), rs.a[:, :N], ALU.mult, ALU.mult,
                    [x[kt][blk], rs, pv], [xn[kt][blk]])

        def norm_to_xn(c, g0):
            for blk in c.blocks:
                norm_blk(c, g0, blk)

        def residual_stage(c, ncg, make_group, after_blk, tail=2):
            for cg in range(ncg - tail):
                grp = make_group(cg)
                for m2 in range(2):
                    for blk in c.blocks:
                        grp(m2, blk)
            grps = [make_group(cg) for cg in range(ncg - tail, ncg)]
            for bi, blk in enumerate(c.blocks):
                first = True
                for grp in grps:
                    for m2 in range(2):
                        grp(m2, blk)
                        if bi > 0 and first:
                            after_blk(c.blocks[bi - 1])
                        first = False
            after_blk(c.blocks[-1])

        def xn_pairs(wv, cs, blk, N):
            return [(wv[:, kt, cs], xn[kt][blk].a[:, :N], [xn[kt][blk]]) for kt in range(8)]

        def xn_reads(blk):
            return []

        def run_pipelined(items, phase_fns, ascending=False):
            n, npz = len(items), len(phase_fns)
            states = [dict() for _ in items]
            import types
            for step in range(n + npz - 1):
                gens = []
                for ph in (range(npz) if ascending else range(npz - 1, -1, -1)):
                    it = step - ph
                    if 0 <= it < n:
                        g = phase_fns[ph](items[it], states[it])
                        if isinstance(g, types.GeneratorType):
                            gens.append(g)
                while gens:
                    for g in list(gens):
                        try:
                            next(g)
                        except StopIteration:
                            gens.remove(g)

        def stage_pool(c, l):
            items = [(j, blk) for jp in range(0, 4, 2) for blk in c.blocks for j in (jp, jp + 1)]
            sl = {}

            def wts(j):
                g2 = j // 2
                if g2 not in sl:
                    sl[g2] = fetch([w_in[l][:, C_POOL + g2 * 256:C_POOL + (g2 + 1) * 256]])
                return sl[g2][0], sl[g2][1][0]

            def p0(it, s):
                j, blk = it
                N = NW[blk]
                slot, wv = wts(j)
                ps = P.get()
                mm(ps, ps.a[:, :N], xn_pairs(wv, slice((j % 2) * 128, (j % 2 + 1) * 128), blk, N), [slot])
                if blk != 2:
                    ext = T["ext"].get()
                    cvi, cvo = cp_v[l][j][blk], cp_v[l][j][1 - blk]
                    CP(ext.a[:, 0:15], cvi.a, [cvi], [ext])
                    CP(ext.a[:, 15:527], ps.a, [ps], [ext], eng="act")
                    CP(cvo.a, ps.a[:, 512 - 15:512], [ps], [cvo], eng="act")
                    s["ext"] = ext
                else:
                    CP(exs.t[:, j, :, 15], ps.a[:, :N], [ps], [exs_v[j]], eng="act")

            def p1(it, s):
                j, blk = it
                N, w = NW[blk], WIN[j]
                d = T["bcb"].get()
                s["d"] = d
                if blk != 2:
                    ext = s["ext"]
                    top = j + 1
                    los = {top: 15}
                    for k in range(top, 1, -1):
                        los[k - 1] = los[k] - 2 ** (k - 1)
                    bufs = [T["acc"].get(), T["r"].get()]
                    prev = ext
                    for k in range(1, top + 1):
                        lo, sh = los[k], 2 ** (k - 1)
                        cur = bufs[(k - 1) % 2]
                        TT(cur.a[:, lo:527], prev.a[:, lo:527], prev.a[:, lo - sh:527 - sh], ALU.add, [prev], [cur])
                        prev = cur
                    STT(d.a, prev.a[:, 15:527], 1.0 / w, ext.a[:, 15:527], ALU.mult, ALU.subtract, [prev, ext], [d])
                    if c.tok0 == 0 and blk == 0:
                        tm = T["sq"].get()
                        TT(tm.a[:, 0:16], prev.a[:, 15:31], icn.t[:, j * 16:(j + 1) * 16], ALU.mult, [prev, icn], [tm])
                        TT(d.a[:, 0:16], tm.a[:, 0:16], ext.a[:, 15:31], ALU.subtract, [tm, ext], [d])
                else:
                    ev = exs_v[j]
                    sr = T["acc"].get()
                    S.op("dve", lambda e: e.tensor_reduce(sr.a[:, :N], exs.t[:, j, :, 16 - w:16], AX.X, ALU.add), [ev], [sr])
                    STT(d.a[:, :N], sr.a[:, :N], 1.0 / w, exs.t[:, j, :, 15], ALU.mult, ALU.subtract, [sr, ev], [d])

            def p2(it, s):
                j, blk = it
                N, d = NW[blk], s["d"]
                ps2 = P.get()
                MM(ps2, ps2.a[:, :N], poolw.t[:, l * 4 + j, :], d.a[:, :N], True, True, [poolw, d], True)
                ACT(ya[j][blk].a[:, :N], ps2.a[:, :N], AF.Identity, [ps2, pv], [ya[j][blk]], scale=pvc(PV_PSC + l * 4 + j))

            run_pipelined(items, [p0, p1, p2])
            if 2 in c.blocks:
                emit_rows([exs.t[:, j, :, 15] for j in range(4)], NSMP, o_pool_s[l, :, 14, :], exs_v)

        def stage_rnn(c, l):
            items = [(j, blk) for jp in range(0, 8, 2) for blk in (0, 1) for j in (jp, jp + 1)]
            has_s = 2 in c.blocks
            sl = {}

            def wts(j):
                jg = j // 2
                if jg not in sl:
                    sx, (wxv,) = fetch([w_in[l][:, C_RX + jg * 256:C_RX + (jg + 1) * 256]])
                    sg, (wgv,) = fetch([w_in[l][:, C_RG + jg * 256:C_RG + (jg + 1) * 256]])
                    sl[jg] = (sx, wxv, sg, wgv)
                return sl[jg]

            def tmp(role, blk):
                return (TSm if blk == 2 else T)[role].get()

            def subs(it, s):
                j, blk = it
                out = [(j, blk, s)]
                if blk == 1 and has_s:
                    out.append((j, 2, s.setdefault("smp", {})))
                return out

            def p0(it, s):
                late = []
                for j, blk, st_ in subs(it, s):
                    N = NW[blk]
                    sx, wxv, sg, wgv = wts(j)
                    cs = slice((j % 2) * 128, (j % 2 + 1) * 128)
                    ps = P.get()
                    mm(ps, ps.a[:, :N], xn_pairs(wxv, cs, blk, N), [sx])
                    if blk == 2:
                        CP(ex3.t[:, j, :, 3], ps.a[:, :N], [ps], [ex3_v[j]], eng="act")
                    else:
                        late.append((j, blk, st_, ps))
                for _ in range(10):
                    yield
                for j, blk, st_, ps in late:
                    ext = T["ext"].get()
                    cvi, cvo = cr_v[l][j][blk], cr_v[l][j][1 - blk]
                    CP(ext.a[:, 0:3], cvi.a, [cvi], [ext])
                    CP(ext.a[:, 3:515], ps.a, [ps], [ext])
                    CP(cvo.a, ps.a[:, 512 - 3:512], [ps], [cvo])
                    st_["ext"] = ext

            def p1(it, s):
                first = True
                for j, blk, st_ in subs(it, s):
                    N = NW[blk]
                    wk = [pvc(PV_RCW + (l * 4 + k) * 8 + j) for k in range(4)]
                    cb = pvc(PV_RCB + l * 8 + j)
                    acc, bcb = tmp("acc", blk), tmp("bcb", blk)
                    st_.update(acc=acc, bcb=bcb)
                    if first:
                        yield
                        first = False
                    if blk != 2:
                        ext = st_["ext"]
                        taps = [ext.a[:, k:k + 512] for k in range(4)]
                        rd = [ext]
                    else:
                        taps = [ex3.t[:, j, :, k] for k in range(4)]
                        rd = [ex3_v[j]]
                    TS(acc.a[:, :N], taps[3], wk[3], cb, ALU.mult, ALU.add, rd + [pv], [acc])
                    yield
                    for k in range(3):
                        STT(acc.a[:, :N], taps[k], wk[k], acc.a[:, :N], ALU.mult, ALU.add, rd + [acc, pv], [acc])
                        yield
                    CP(bcb.a[:, :N], acc.a[:, :N], [acc], [bcb])

            def p2(it, s):
                for j, blk, st_ in subs(it, s):
                    N, bcb = NW[blk], st_["bcb"]
                    sx, wxv, sg, wgv = wts(j)
                    cs = slice((j % 2) * 128, (j % 2 + 1) * 128)
                    lj = l * 8 + j
                    if blk != 2:
                        psg, psr, psi = P.get(), P.get(), P.get()
                        og, orr, oi = psg.a[:, :N], psr.a[:, :N], psi.a[:, :N]
                    else:
                        psg = psr = psi = P.get()
                        og, orr, oi = psg.a[:, 0:N], psg.a[:, N:2 * N], psg.a[:, 2 * N:3 * N]
                    mm(psg, og, xn_pairs(wgv, cs, blk, N), [sg])
                    MM(psr, orr, lruwa.t[:, lj, :], bcb.a[:, :N], True, True, [lruwa, bcb], True)
                    MM(psi, oi, lruwx.t[:, lj, :], bcb.a[:, :N], True, True, [lruwx, bcb], True)
                    st_.update(bg=psg, br=psr, bi=psi, og=og, orr=orr, oi=oi)

            def p3(it, s):
                ss = subs(it, s)
                for j, blk, st_ in ss:
                    N, lj = NW[blk], l * 8 + j
                    r, ii, sq, gg = tmp("r", blk), tmp("i", blk), tmp("sq", blk), tmp("gg", blk)
                    st_.update(r=r, ii=ii, sq=sq, gg=gg)
                    ACT(gg.a[:, :N], st_["og"], AF.Gelu_apprx_tanh, [st_["bg"]], [gg])
                    ACT(r.a[:, :N], st_["orr"], AF.Tanh, [st_["br"], hbv], [r], bias=hbv.t[:, lj:lj + 1], scale=0.5)
                    ACT(ii.a[:, :N], st_["oi"], AF.Tanh, [st_["bi"], hbv], [ii], bias=hbv.t[:, 16 + lj:17 + lj], scale=0.5)
                for j, blk, st_ in ss:
                    N, lj, r, sq = NW[blk], l * 8 + j, st_["r"], st_["sq"]
                    ACT(sq.a[:, :N], r.a[:, :N], AF.Exp, [r, clv], [sq], scale=clv.t[:, lj:lj + 1], bias=clv.t[:, lj:lj + 1])
                    ACT(r.a[:, :N], r.a[:, :N], AF.Exp, [r, clv], [r], scale=clv.t[:, 16 + lj:17 + lj], bias=clv.t[:, 16 + lj:17 + lj])
                    ACT(sq.a[:, :N], sq.a[:, :N], AF.Ln, [sq], [sq], scale=-1.0, bias=1.0)
                    ACT(sq.a[:, :N], sq.a[:, :N], AF.Exp, [sq], [sq], scale=0.5)

            def p4(it, s):
                for j, blk, st_ in subs(it, s):
                    N, acc, r, ii, sq, gg = NW[blk], st_["acc"], st_["r"], st_["ii"], st_["sq"], st_["gg"]
                    STT(ii.a[:, :N], ii.a[:, :N], 1.0, acc.a[:, :N], ALU.add, ALU.mult, [ii, acc], [ii])
                    yield
                    STT(ii.a[:, :N], ii.a[:, :N], 0.5, sq.a[:, :N], ALU.mult, ALU.mult, [ii, sq], [ii])
                    yield
                    h = sq
                    if blk != 2:
                        hv = ch_v[l][j]
                        S.op("dve", lambda e, h=h, r=r, ii=ii, hv=hv, N=N: e.tensor_tensor_scan(h.a[:, :N], r.a[:, :N], ii.a[:, :N], hv.a,
                                                                                              ALU.mult, ALU.add), [r, ii, hv], [h])
                        yield
                        CP(hv.a, h.a[:, N - 1:N], [h], [hv])
                    else:
                        TT(h.a[:, :N], r.a[:, :N], h0t.t[:, j, :], ALU.mult, [r, h0_v[j]], [h])
                        TT(h.a[:, :N], h.a[:, :N], ii.a[:, :N], ALU.add, [h, ii], [h])
                        CP(h0t.t[:, j, :], h.a[:, :N], [h], [h0_v[j]])
                    yield
                    TT(yb[j][blk].a[:, :N], gg.a[:, :N], h.a[:, :N], ALU.mult, [gg, h], [yb[j][blk]])

            run_pipelined(items, [p0, p1, p2, p3, p4])
            if has_s:
                for g in range(2):
                    emit_rows([ex3.t[:, g * 4 + i, :, 3] for i in range(4)], NSMP, o_rc_s[l, :, 2, g * 512:(g + 1) * 512], ex3_v)
                    emit_rows([h0t.t[:, g * 4 + i, :] for i in range(4)], NSMP, o_h_s[l, :, g * 512:(g + 1) * 512], h0_v)

        def rstd_pow(ssq, n):
            S.op("pool", lambda e: e.tensor_scalar(ssq.a[0:n, :], ssq.a[0:n, :], 1.0 / 512, EPS, ALU.mult, ALU.add), [ssq], [ssq])
            S.op("pool", lambda e: e.tensor_tensor(ssq.a[0:n, :], ssq.a[0:n, :], mhalf.a[0:n, :], ALU.pow), [ssq, mhalf], [ssq])

        def stage_chunk(c, l):
            svd, wvv = fetch2(w_in[l][:, C_V:C_V + 512])
            sus = [fetch([w_in[l][:, C_U + h_ * 256:C_U + (h_ + 1) * 256]]) for h_ in range(2)]

            def u_pairs(g, blk, N):
                return xn_pairs(sus[g // 2][1][0], slice((g % 2) * 128, (g % 2 + 1) * 128), blk, N), [sus[g // 2][0]]
            if 2 in c.blocks:
                N = NSMP
                ps = P.get()
                mm(ps, ps.a[0:N, :], [(xn[kt][2].a[:, 0:N], wvv[:, kt, :], [xn[kt][2]]) for kt in range(8)], svd)
                vg, junk, ssq, vnf = T["r"].get(), T["sq"].get(), T["ssq"].get(), T["acc"].get()
                ACT(vg.a[0:N, :512], ps.a[0:N, :], AF.Gelu_apprx_tanh, [ps], [vg])
                ACT(junk.a[0:N, :512], vg.a[0:N, :512], AF.Square, [vg], [junk, ssq], accum_out=ssq.a[0:N, :])
                rstd_pow(ssq, N)
                STT(vnf.a[0:N, :512], vg.a[0:N, :512], ssq.a[0:N, :], vgb.t[0:N, l, :], ALU.mult, ALU.mult, [vg, ssq, vgb], [vnf])
                S.dma("sp", ch_cv, o_cv_s[l], vnf.a[0:N, :512], reads=[vnf], final=True)
                for g in range(4):
                    pst = P.get()
                    TR(pst, pst.a[:, 0:N], vnf.a[0:N, g * 128:(g + 1) * 128], ident.a[0:N, 0:N], [vnf])
                    mixs = T["i"].get()
                    lg = l * 4 + g
                    ACT(mixs.a[:, :N], pst.a[:, :N], AF.Identity, [pst, wsb, bsb], [mixs], scale=wsb.t[:, lg:lg + 1], bias=bsb.t[:, lg:lg + 1])
                    psu = P.get()
                    prs, rds = u_pairs(g, 2, N)
                    mm(psu, psu.a[:, :N], prs, rds)
                    ug = T["gg"].get()
                    ACT(ug.a[:, :N], psu.a[:, :N], AF.Gelu_apprx_tanh, [psu], [ug])
                    TT(yc[g][2].a[:, :N], ug.a[:, :N], mixs.a[:, :N], ALU.mult, [ug, mixs], [yc[g][2]])
            mixb = P.b[0:4]
            PG = Ring(P.b[4:8])
            for blk in range(2):
                items = [("v", c4) for c4 in range(4)] + [("u", g) for g in range(4)]

                def q0(it, s, blk=blk):
                    kind, i = it
                    ps = PG.get()
                    if kind == "v":
                        tc = slice(i * 128, (i + 1) * 128)
                        mm(ps, ps.a, [(xn[kt][blk].a[:, tc], wvv[:, kt, :], [xn[kt][blk]]) for kt in range(8)], svd)
                        vg, junk, ssq = T["r"].get(), T["sq"].get(), T["ssq"].get()
                        s.update(vg=vg, ssq=ssq)
                        ACT(vg.a[:, :512], ps.a, AF.Gelu_apprx_tanh, [ps], [vg])
                        ACT(junk.a[:, :512], vg.a[:, :512], AF.Square, [vg], [junk, ssq], accum_out=ssq.a)
                    else:
                        prs, rds = u_pairs(i, blk, 512)
                        mm(ps, ps.a, prs, rds)
                        ug = T["gg"].get()
                        s["ug"] = ug
                        ACT(ug.a[:, :512], ps.a, AF.Gelu_apprx_tanh, [ps], [ug])

                def q1(it, s, blk=blk):
                    kind, i = it
                    if kind == "v":
                        tc = slice(i * 128, (i + 1) * 128)
                        vg, ssq, vn = s["vg"], s["ssq"], T["vn"].get()
                        rstd_pow(ssq, 128)
                        STT(vn.a, vg.a[:, :512], ssq.a, vgb.t[:, l, :], ALU.mult, ALU.mult, [vg, ssq, vgb], [vn])
                        for g in range(4):
                            lg = l * 4 + g
                            MM(mixb[g], mixb[g].a[:, tc], vn.a[:, g * 128:(g + 1) * 128], wsT.t[:, lg, :], True, True, [vn, wsT], g == 3)
                    else:
                        ug, tb = s["ug"], T["i"].get()
                        lg = l * 4 + i
                        TT(tb.a[:, :512].rearrange("p (a b) -> p a b", a=4), mixb[i].a.rearrange("p (a b) -> p a b", a=4),
                           bsbt.t[:, lg, :].unsqueeze(1).to_broadcast([128, 4, 128]), ALU.add, [mixb[i], bsbt], [tb])
                        TT(yc[i][blk].a, ug.a[:, :512], tb.a[:, :512], ALU.mult, [ug, tb], [yc[i][blk]])

                run_pipelined(items, [q0, q1], ascending=True)


        def stage_merge(c, l):
            for mg in range(4):
                c0 = mg * 256
                sA, (gAv,) = fetch([w_in[l][:, C_G + c0:C_G + c0 + 256]])
                sP, (pav, pcv) = fetch([w_pa[l][:, c0:c0 + 256], w_pc[l][:, c0:c0 + 256]])
                sB, (gBv,) = fetch([w_in[l][:, C_G + 1024 + c0:C_G + 1024 + c0 + 256]])
                sPb, (pbv,) = fetch([w_pb[l][:, c0:c0 + 256]])
                sC, (gCv,) = fetch([w_in[l][:, C_G + 2048 + c0:C_G + 2048 + c0 + 256]])
                plan = [(gAv, sA, pav, sP, ya, 4), (gBv, sB, pbv, sPb, yb, 8), (gCv, sC, pcv, sP, yc, 4)]
                for m2 in range(2):
                    m = mg * 2 + m2
                    cs = slice(m2 * 128, (m2 + 1) * 128)
                    for blk in c.blocks:
                        N = NW[blk]
                        acc, tmp = T["acc"].get(), T["i"].get()
                        for bi, (gv, gs, pw, pslot, ys, nk) in enumerate(plan):
                            psg = P.get()
                            mm(psg, psg.a[:, :N], xn_pairs(gv, cs, blk, N), [gs] + xn_reads(blk))
                            gt = T["gg"].get()
                            ACT(gt.a[:, :N], psg.a[:, :N], AF.Sigmoid, [psg], [gt])
                            psp = P.get()
                            mm(psp, psp.a[:, :N], [(pw[:, kt, cs], ys[kt][blk].a[:, :N], [ys[kt][blk]]) for kt in range(nk)], [pslot])
                            if bi == 0:
                                TT(acc.a[:, :N], gt.a[:, :N], psp.a[:, :N], ALU.mult, [gt, psp], [acc])
                            else:
                                TT(tmp.a[:, :N], gt.a[:, :N], psp.a[:, :N], ALU.mult, [gt, psp], [tmp])
                                if bi == 1:
                                    TT(acc.a[:, :N], acc.a[:, :N], tmp.a[:, :N], ALU.add, [acc, tmp], [acc])
                                else:
                                    TT(mrg[m][blk].a[:, :N], acc.a[:, :N], tmp.a[:, :N], ALU.add, [acc, tmp], [mrg[m][blk]])

        def stage_wo(c, l, after_blk):
            def make_group(cg):
                so, (wov,) = fetch([w_o[l][:, cg * 256:(cg + 1) * 256]])

                def grp(m4, blk):
                    m, N = cg * 2 + m4, NW[blk]
                    ps = P.get()
                    mm(ps, ps.a[:, :N],
                       [(wov[:, kt, m4 * 128:(m4 + 1) * 128], mrg[kt][blk].a[:, :N], [mrg[kt][blk]]) for kt in range(8)], [so])
                    TT(x[m][blk].a[:, :N], x[m][blk].a[:, :N], ps.a[:, :N], ALU.add, [x[m][blk], ps], [x[m][blk]])
                return grp
            residual_stage(c, 4, make_group, after_blk)

        def stage_ffn_up(c, l):
            items = [(t, blk) for tp in range(0, 24, 2) for blk in c.blocks for t in (tp, tp + 1)]
            sl = {}

            def wts(t):
                cg = t // 2
                if cg not in sl:
                    sg, (wgv,) = fetch([ffn_wg[l][:, cg * 256:(cg + 1) * 256]])
                    su, (wuv,) = fetch([ffn_wu[l][:, cg * 256:(cg + 1) * 256]])
                    sl[cg] = (sg, wgv, su, wuv)
                return sl[cg]

            def p0(it, s):
                t, blk = it
                N = NW[blk]
                sg, wgv, su, wuv = wts(t)
                cs = slice((t % 2) * 128, (t % 2 + 1) * 128)
                ps = P.get()
                mm(ps, ps.a[:, :N], xn_pairs(wgv, cs, blk, N), [sg] + xn_reads(blk))
                if blk != 2:
                    ext = T["ext"].get()
                    cvi, cvo = cf_v[l][t][blk], cf_v[l][t][1 - blk]
                    CP(ext.a[:, 0:2], cvi.a, [cvi], [ext])
                    CP(ext.a[:, 2:514], ps.a, [ps], [ext], eng="act")
                    CP(cvo.a, ps.a[:, 512 - 2:512], [ps], [cvo], eng="act")
                    s["ext"] = ext
                else:
                    CP(ex2.t[:, t, :, 2], ps.a[:, :N], [ps], [ex2_v[t]], eng="act")

            def p1(it, s):
                t, blk = it
                N = NW[blk]
                wk = [pvc(PV_FCW + (l * 3 + k) * 24 + t) for k in range(3)]
                cb = pvc(PV_FCB + l * 24 + t)
                acc = T["acc"].get()
                s["acc"] = acc
                if blk != 2:
                    ext = s["ext"]
                    TS(acc.a[:, :N], ext.a[:, 2:514], wk[2], cb, ALU.mult, ALU.add, [ext, pv], [acc])
                    for k in range(2):
                        STT(acc.a[:, :N], ext.a[:, k:k + 512], wk[k], acc.a[:, :N], ALU.mult, ALU.add, [ext, acc, pv], [acc])
                else:
                    ev = ex2_v[t]
                    TS(acc.a[:, :N], ex2.t[:, t, :, 2], wk[2], cb, ALU.mult, ALU.add, [ev, pv], [acc])
                    for k in range(2):
                        STT(acc.a[:, :N], ex2.t[:, t, :, k], wk[k], acc.a[:, :N], ALU.mult, ALU.add, [ev, acc, pv], [acc])

            def p2(it, s):
                t, blk = it
                N, acc = NW[blk], s["acc"]
                sg, wgv, su, wuv = wts(t)
                cs = slice((t % 2) * 128, (t % 2 + 1) * 128)
                ACT(acc.a[:, :N], acc.a[:, :N], AF.Gelu_apprx_tanh, [acc], [acc])
                psu = P.get()
                mm(psu, psu.a[:, :N], xn_pairs(wuv, cs, blk, N), [su] + xn_reads(blk))
                s["psu"] = psu

            def p3(it, s):
                t, blk = it
                N, acc, psu = NW[blk], s["acc"], s["psu"]
                TT(hf[t][blk].a[:, :N], acc.a[:, :N], psu.a[:, :N], ALU.mult, [acc, psu], [hf[t][blk]])
                if blk == c.blocks[-1] and t % 4 == 3 and 2 in c.blocks:
                    cg = t // 4
                    emit_rows([ex2.t[:, cg * 4 + i, :, 2] for i in range(4)], NSMP, o_ff_s[l, :, 1, cg * 512:(cg + 1) * 512],
                              ex2_v[cg * 4:cg * 4 + 4])

            run_pipelined(items, [p0, p1, p2, p3])


        def stage_ffn_down(c, l, after_blk):
            def make_group(cg):
                fs = [fetch([ffn_wd[l][kc * 1024:(kc + 1) * 1024, cg * 256:(cg + 1) * 256]]) for kc in range(3)]

                def grp(m4, blk):
                    m, N = cg * 2 + m4, NW[blk]
                    ps = P.get()
                    mm(ps, ps.a[:, :N],
                       [(fs[kt // 8][1][0][:, kt % 8, m4 * 128:(m4 + 1) * 128], hf[kt][blk].a[:, :N], [hf[kt][blk]]) for kt in range(24)],
                       [f[0] for f in fs])
                    TT(x[m][blk].a[:, :N], x[m][blk].a[:, :N], ps.a[:, :N], ALU.add, [x[m][blk], ps], [x[m][blk]])
                return grp
            residual_stage(c, 4, make_group, after_blk)

        def load_sample_state(l):
            for hs in range(2):
                rw = T["rows"].get()
                S.dma("sp", rw.ch, rw.a[0:120, :], st_pool[l, hs * 8:(hs + 1) * 8].rearrange("s r c -> (s r) c"), writes=[rw])
                ps = P.get()
                for j in range(4):
                    TR(ps, ps.a[:, j * 120:(j + 1) * 120], rw.a[0:120, j * 128:(j + 1) * 128], ident.a[0:120, 0:120], [rw], sig=(j == 3))
                for j in range(4):
                    CP(exs.t[:, j, hs * 8:(hs + 1) * 8, 0:15], ps.a[:, j * 120:(j + 1) * 120].rearrange("p (s r) -> p s r", r=15),
                       [ps], [exs_v[j]])
            stg = StageA
            stg.load(st_rc[l].rearrange("s r c -> (s r) c"), 48)
            for g in range(2):
                ps = P.get()
                for i in range(4):
                    j = g * 4 + i
                    TR(ps, ps.a[:, i * 48:(i + 1) * 48], stg.ap(j, 48), ident.a[0:48, 0:48], stg.deps, sig=(i == 3))
                for i in range(4):
                    j = g * 4 + i
                    CP(ex3.t[:, j, :, 0:3], ps.a[:, i * 48:(i + 1) * 48].rearrange("p (s r) -> p s r", r=3), [ps], [ex3_v[j]])
            stg = StageB
            stg.load(st_h[l], NSMP)
            for g in range(2):
                ps = P.get()
                for i in range(4):
                    j = g * 4 + i
                    TR(ps, ps.a[:, i * NSMP:(i + 1) * NSMP], stg.ap(j, NSMP), ident.a[0:NSMP, 0:NSMP], stg.deps, sig=(i == 3))
                CP(h0t.t[:, g * 4:(g + 1) * 4, :], ps.a[:, 0:4 * NSMP].rearrange("p (a b) -> p a b", a=4), [ps],
                   [h0_v[g * 4 + i] for i in range(4)])
            for q in range(3):
                stg = stages[q % 2]
                stg.load(st_ff[l][:, :, q * 1024:(q + 1) * 1024].rearrange("s r c -> (s r) c"), 32)
                for g in range(2):
                    ps = P.get()
                    for i in range(4):
                        t = q * 8 + g * 4 + i
                        TR(ps, ps.a[:, i * 32:(i + 1) * 32], stg.ap(g * 4 + i, 32), ident.a[0:32, 0:32], stg.deps, sig=(i == 3))
                    for i in range(4):
                        t = q * 8 + g * 4 + i
                        CP(ex2.t[:, t, :, 0:2], ps.a[:, i * 32:(i + 1) * 32].rearrange("p (s r) -> p s r", r=2), [ps], [ex2_v[t]])
            S.dma("sp", ch_d2d, o_pool_s[l, :, 0:14, :], st_pool[l, :, 1:15, :], final=True)
            S.dma("sp", ch_d2d, o_rc_s[l, :, 0:2, :], st_rc[l, :, 1:3, :], final=True)
            S.dma("sp", ch_d2d, o_ff_s[l, :, 0:1, :], st_ff[l, :, 1:2, :], final=True)

        def emit_prompt_state(l):
            emit_rows([cp_v[l][j][0].a for j in range(4)], 15, o_pool_p[l], [cp_v[l][j][0] for j in range(4)])
            for g in range(2):
                emit_rows([cr_v[l][g * 4 + i][0].a for i in range(4)], 3, o_rc_p[l, :, g * 512:(g + 1) * 512],
                          [cr_v[l][g * 4 + i][0] for i in range(4)])
                emit_rows([ch_v[l][g * 4 + i].a for i in range(4)], 1, o_h_p[l, :, g * 512:(g + 1) * 512], ch_v[l])
            for g in range(6):
                emit_rows([cf_v[l][g * 4 + i][0].a for i in range(4)], 2, o_ff_p[l, :, g * 512:(g + 1) * 512],
                          [cf_v[l][g * 4 + i][0] for i in range(4)])

        def final_store_blk(c, blk):
            if True:
                N = NW[blk]
                rs = rstd_blk(c, blk)
                for g in range(2):
                    yts = [T["acc"].get(), T["i"].get(), T["gg"].get(), T["sq"].get()]
                    for i in range(4):
                        kt = g * 4 + i
                        STT(yts[i].a[:, :N], x[kt][blk].a[:, :N], pvc(PV_FNG + kt), rs.a[:, :N], ALU.mult, ALU.mult,
                            [x[kt][blk], rs, pv], [yts[i]])
                    if blk == 2:
                        ps = P.get()
                        for i in range(4):
                            TR(ps, ps.a[0:N, i * 128:(i + 1) * 128], yts[i].a[:, 0:N], ident.a, [yts[i]], sig=(i == 3))
                        rw = T["rows"].get()
                        CP(rw.a[0:N, :], ps.a[0:N, :], [ps], [rw])
                        S.dma("sp", rw.ch, y_sample[:, g * 512:(g + 1) * 512], rw.a[0:N, :], reads=[rw], final=True)
                    else:
                        for tt in range(4):
                            ps = P.get()
                            for i in range(4):
                                TR(ps, ps.a[:, i * 128:(i + 1) * 128], yts[i].a[:, tt * 128:(tt + 1) * 128], ident.a, [yts[i]], sig=(i == 3))
                            rw = T["rows"].get()
                            CP(rw.a, ps.a, [ps], [rw], eng="act" if tt % 2 == 0 else "dve")
                            r0 = c.tok0 + blk * NB + tt * 128
                            S.dma("sp", rw.ch, y_prompt[r0:r0 + 128, g * 512:(g + 1) * 512], rw.a, reads=[rw], final=True)

        marks = []

        def mark(name):
            marks.append((name, dict(S.nins)))

        def layer(c, l):
            mark(f"L{l}.smpstate")
            if 2 in c.blocks:
                load_sample_state(l)
            mark(f"L{l}.norm1")
            mark(f"L{l}.pool")
            stage_pool(c, l)
            mark(f"L{l}.rnn")
            stage_rnn(c, l)
            mark(f"L{l}.chunk")
            stage_chunk(c, l)
            mark(f"L{l}.merge")
            stage_merge(c, l)
            mark(f"L{l}.wo")
            stage_wo(c, l, lambda blk: norm_blk(c, PV_N2G + l * 8, blk))
            mark(f"L{l}.ffn_up")
            stage_ffn_up(c, l)
            mark(f"L{l}.ffn_down")
            if l + 1 < NL:
                stage_ffn_down(c, l, lambda blk: norm_blk(c, PV_N1G + (l + 1) * 8, blk))
            else:
                stage_ffn_down(c, l, lambda blk: final_store_blk(c, blk))

        for c in (Ctx(0, (0, 1)), Ctx(TH, (0, 1, 2))):
            mark("load_x")
            load_x(c)
            for l in range(NL):
                layer(c, l)
                if c.tok0 == TH:
                    emit_prompt_state(l)
        S.finish()
        build_program.stats = dict(nins=dict(S.nins), nwait=S.nwait)
        build_program.log = S.log
        mark("end")
        build_program.marks = marks
    return nc


_NC = None


def _pack_pvec(p):
    rows = [p["norm1_g"].reshape(16, 128), p["norm2_g"].reshape(16, 128), p["pool_scale"].reshape(8, 128),
            p["rnn_conv_w"].reshape(64, 128), p["rnn_conv_b"].reshape(16, 128), p["lru_ba"].reshape(16, 128),
            p["lru_bx"].reshape(16, 128), p["lru_lambda"].reshape(16, 128), p["ffn_conv_w"].reshape(144, 128),
            p["ffn_conv_b"].reshape(48, 128), p["final_norm_g"].reshape(8, 128)]
    pv = np.concatenate(rows, axis=0)
    out = np.zeros((PV_ROWS, 128), np.float32)
    out[:pv.shape[0]] = pv
    return out


def kernel(**inputs):
    global _NC
    p = {k: np.ascontiguousarray(np.asarray(v, dtype=np.float32)) for k, v in inputs.items()}
    if _NC is None:
        _NC = build_program()
    nc = _NC
    ncore = 8
    shared = {k: p[k] for k in ("w_in", "pool_w", "lru_wa", "lru_wx", "chunk_ws", "chunk_bs", "chunk_vnorm_g", "w_pa", "w_pb",
                                "w_pc", "w_o", "ffn_wg", "ffn_wu", "ffn_wd")}
    shared["pvec"] = _pack_pvec(p)
    shared["ident"] = np.eye(128, dtype=np.float32)
    shared["tril"] = np.tril(np.ones((128, 128), np.float32))
    ic = np.zeros((4, 16), np.float32)
    for j, w in enumerate(WIN):
        ic[j] = 1.0 / np.minimum(np.arange(16) + 1, w)
    shared["invcnt"] = ic.reshape(1, 64)
    shared["ws00"] = np.ascontiguousarray(p["chunk_ws"][:, :, 0, 0]).reshape(1, 8)
    shared["bs0"] = np.ascontiguousarray(p["chunk_bs"][:, :, 0]).reshape(1, 8)
    in_maps = []
    for i in range(ncore):
        m = dict(shared)
        m["x_prompt"] = np.ascontiguousarray(p["x_prompt"][i])
        sl = slice(i * NSMP, (i + 1) * NSMP)
        m["x_sample"] = np.ascontiguousarray(p["x_sample"][sl, 0, :])
        m["state_pool"] = np.ascontiguousarray(p["state_pool"][:, sl])
        m["state_rnn_conv"] = np.ascontiguousarray(p["state_rnn_conv"][:, sl])
        m["state_rnn_h"] = np.ascontiguousarray(p["state_rnn_h"][:, sl])
        m["state_ffn_conv"] = np.ascontiguousarray(p["state_ffn_conv"][:, sl])
        in_maps.append(m)
    res = run_bass_kernel_spmd(nc, in_maps, core_ids=list(range(ncore)))
    R = res.results
    f32 = np.float32
    y_prompt = np.stack([R[i]["y_prompt"] for i in range(ncore)], 0).astype(f32)
    y_sample = np.concatenate([R[i]["y_sample"] for i in range(ncore)], 0)[:, None, :].astype(f32)
    pool_p = np.stack([R[i]["new_pool_prompt"] for i in range(ncore)], 1).astype(f32)
    pool_s = np.concatenate([R[i]["new_pool_sample"] for i in range(ncore)], 1).astype(f32)
    rc_p = np.stack([R[i]["new_rconv_prompt"] for i in range(ncore)], 1).astype(f32)
    rc_s = np.concatenate([R[i]["new_rconv_sample"] for i in range(ncore)], 1).astype(f32)
    h_p = np.stack([R[i]["new_h_prompt"][:, 0, :] for i in range(ncore)], 1).astype(f32)
    h_s = np.concatenate([R[i]["new_h_sample"] for i in range(ncore)], 1).astype(f32)
    ff_p = np.stack([R[i]["new_ffn_prompt"] for i in range(ncore)], 1).astype(f32)
    ff_s = np.concatenate([R[i]["new_ffn_sample"] for i in range(ncore)], 1).astype(f32)
    cv_s = np.concatenate([R[i]["new_chunk_v_sample"] for i in range(ncore)], 1)[:, :, None, :].astype(f32)
    return (y_prompt, y_sample, pool_p, pool_s, rc_p, rc_s, h_p, h_s, ff_p, ff_s, cv_s)
```

```python
import numpy as np
from contextlib import ExitStack
import concourse.bass as bass
import concourse.mybir as mybir
from concourse.bass_utils import run_bass_kernel_spmd

F32 = mybir.dt.float32
BF16 = mybir.dt.bfloat16
AF = mybir.ActivationFunctionType
ALU = mybir.AluOpType
AX = mybir.AxisListType


class Dep:
    __slots__ = ("w", "r")

    def __init__(self):
        self.w = []
        self.r = []


class Buf(Dep):
    __slots__ = ("t", "a", "ch")

    def __init__(self, t, a=None, ch=None):
        super().__init__()
        self.t = t
        self.a = t[:] if a is None else a
        self.ch = ch


def V(ap):
    return Buf(None, ap)


class Sched:
    def __init__(self, nc, st):
        self.nc, self.st = nc, st
        self.eng = dict(pe=nc.tensor, act=nc.scalar, dve=nc.vector, pool=nc.gpsimd, sp=nc.sync)
        self.sems, self.cnt = {}, {}
        for k in self.eng:
            self.sems[k] = st.enter_context(nc.semaphore("sem_" + k))
            self.cnt[k] = 0
        self.waited = {k: {} for k in self.eng}
        self.finals = []
        self.nwait = 0
        self.nins = {k: 0 for k in self.eng}
        self.log = {k: [] for k in self.eng}

    def sb(self, name, shape, dtype):
        return Buf(self.st.enter_context(self.nc.sbuf_tensor(name, list(shape), dtype)))

    def ps(self, name):
        return Buf(self.st.enter_context(self.nc.psum_tensor(name, [128, 512], F32)))

    def chan(self, name):
        self.sems[name] = self.st.enter_context(self.nc.semaphore("sem_" + name))
        self.cnt[name] = 0
        return name

    def _wait(self, e, evs):
        need = {}
        for (k, v) in evs:
            if v > need.get(k, 0):
                need[k] = v
        wd = self.waited[e]
        for k, v in need.items():
            if wd.get(k, 0) >= v:
                continue
            if k == e and e == "pe":
                continue
            self.eng[e].wait_ge(self.sems[k], v)
            self.log[e].append(("w", k, v))
            self.nwait += 1
            wd[k] = v

    def _deps(self, reads, writes):
        evs = []
        for d in reads:
            evs += d.w
        for d in writes:
            evs += d.w
            evs += d.r
        return evs

    def _record(self, ev, reads, writes):
        for d in reads:
            d.r.append(ev)
            if len(d.r) > 24:
                d.r = _compact(d.r)
        for d in writes:
            d.w = [ev]
            d.r = []

    def op(self, e, fn, reads=(), writes=(), sig=True):
        self._wait(e, self._deps(reads, writes))
        ins = fn(self.eng[e])
        self.nins[e] += 1
        if sig:
            self.cnt[e] += 1
            ins.then_inc(self.sems[e], 1)
            self.log[e].append(("i", e, 1))
            ev = (e, self.cnt[e])
        else:
            ev = (e, self.cnt[e] + 1)
        self._record(ev, reads, writes)
        return ev

    def sbc(self, name, shape, dtype):
        b = self.sb(name, shape, dtype)
        b.ch = self.chan("c_" + name)
        return b

    def batch_end(self, ch, deps):
        for d in deps:
            d.w = [(ch, self.cnt[ch])]

    def dma(self, q, ch, out, in_, reads=(), writes=(), final=False, join=(), **kw):
        self._wait(q, self._deps(reads, writes))
        ins = self.eng[q].dma_start(out=out, in_=in_, **kw)
        self.nins[q] += 1
        self.cnt[ch] += 16
        ins.then_inc(self.sems[ch], 16)
        self.log[q].append(("i", ch, 16))
        ev = (ch, self.cnt[ch])
        self._record(ev, reads, writes)
        for d in join:
            d.w.append(ev)
        if final:
            self.finals.append(ev)
        return ev

    def finish(self):
        self._wait("sp", self.finals)
        for e in ("act", "dve", "pool", "pe"):
            self._wait(e, self.finals)


def _compact(evs):
    m = {}
    for k, v in evs:
        if v > m.get(k, 0):
            m[k] = v
    return list(m.items())


NL, D, KT, NB = 2, 1024, 8, 512
SEQ, TH = 2048, 1024
NSMP = 16
WIN = (2, 4, 8, 16)
EPS = 1e-6
C_POOL, C_RX, C_RG, C_U, C_V, C_G = 0, 512, 1536, 2560, 3072, 3584
PV_N1G, PV_N2G, PV_PSC, PV_RCW, PV_RCB, PV_LBA, PV_LBX, PV_LAM, PV_FCW, PV_FCB, PV_FNG = (
    0, 16, 32, 40, 104, 120, 136, 152, 168, 312, 360)
PV_ROWS = 384
NSLOT = 8
SLOTW = 2048


class Ring:
    def __init__(self, bufs):
        self.b, self.i = bufs, 0

    def get(self):
        b = self.b[self.i % len(self.b)]
        self.i += 1
        return b


NW = (NB, NB, NSMP)


class Ctx:
    def __init__(self, tok0, blocks):
        self.tok0, self.blocks = tok0, blocks


def build_program():
    nc = bass.Bass("TRN2", target_bir_lowering=False)

    def din(n, s):
        return nc.dram_tensor(n, list(s), F32, kind="ExternalInput").ap()

    def dout(n, s):
        return nc.dram_tensor(n, list(s), F32, kind="ExternalOutput").ap()

    x_prompt = din("x_prompt", [SEQ, D])
    x_sample = din("x_sample", [NSMP, D])
    st_pool = din("state_pool", [NL, NSMP, 15, 512])
    st_rc = din("state_rnn_conv", [NL, NSMP, 3, 1024])
    st_h = din("state_rnn_h", [NL, NSMP, 1024])
    st_ff = din("state_ffn_conv", [NL, NSMP, 2, 3072])
    w_in = din("w_in", [NL, 1024, 6656])
    pool_w = din("pool_w", [NL, 4, 128, 128])
    lru_wa = din("lru_wa", [NL, 8, 128, 128])
    lru_wx = din("lru_wx", [NL, 8, 128, 128])
    chunk_ws = din("chunk_ws", [NL, 4, 128, 128])
    chunk_bs = din("chunk_bs", [NL, 4, 128])
    chunk_vg = din("chunk_vnorm_g", [NL, 512])
    w_pa = din("w_pa", [NL, 512, 1024])
    w_pb = din("w_pb", [NL, 1024, 1024])
    w_pc = din("w_pc", [NL, 512, 1024])
    w_o = din("w_o", [NL, 1024, 1024])
    ffn_wg = din("ffn_wg", [NL, 1024, 3072])
    ffn_wu = din("ffn_wu", [NL, 1024, 3072])
    ffn_wd = din("ffn_wd", [NL, 3072, 1024])
    pvec = din("pvec", [PV_ROWS, 128])
    ident_d = din("ident", [128, 128])
    tril_d = din("tril", [128, 128])
    invcnt_d = din("invcnt", [1, 64])
    ws00_d = din("ws00", [1, 8])
    bs0_d = din("bs0", [1, 8])

    y_prompt = dout("y_prompt", [SEQ, D])
    y_sample = dout("y_sample", [NSMP, D])
    o_pool_p = dout("new_pool_prompt", [NL, 15, 512])
    o_pool_s = dout("new_pool_sample", [NL, NSMP, 15, 512])
    o_rc_p = dout("new_rconv_prompt", [NL, 3, 1024])
    o_rc_s = dout("new_rconv_sample", [NL, NSMP, 3, 1024])
    o_h_p = dout("new_h_prompt", [NL, 1, 1024])
    o_h_s = dout("new_h_sample", [NL, NSMP, 1024])
    o_ff_p = dout("new_ffn_prompt", [NL, 2, 3072])
    o_ff_s = dout("new_ffn_sample", [NL, NSMP, 2, 3072])
    o_cv_s = dout("new_chunk_v_sample", [NL, NSMP, 512])

    with ExitStack() as st:
        S = Sched(nc, st)
        xt = S.sb("xt", [128, 8, TH], F32)
        xnt = S.sb("xnt", [128, 8, TH], BF16)
        hft = S.sb("hft", [128, 24, TH], BF16)
        xst = S.sb("xst", [128, 8, NSMP], F32)
        xnst = S.sb("xnst", [128, 8, NSMP], BF16)
        hfst = S.sb("hfst", [128, 24, NSMP], BF16)
        x = [[V(xt.t[:, k, b * NB:(b + 1) * NB]) for b in range(2)] + [V(xst.t[:, k, :])] for k in range(8)]
        xn = [[V(xnt.t[:, k, b * NB:(b + 1) * NB]) for b in range(2)] + [V(xnst.t[:, k, :])] for k in range(8)]
        hf = [[V(hft.t[:, k, b * NB:(b + 1) * NB]) for b in range(2)] + [V(hfst.t[:, k, :])] for k in range(24)]
        ya, yb, yc, mrg = hf[0:4], hf[4:12], hf[12:16], hf[16:24]

        TW = 528
        T = {}
        for role, n in (("ext", 2), ("acc", 3), ("r", 2), ("i", 2), ("sq", 2), ("gg", 2)):
            T[role] = Ring([S.sb(f"t_{role}{i}", [128, TW], F32) for i in range(n)])
        T["bcb"] = Ring([S.sb(f"t_bcb{i}", [128, NB], BF16) for i in range(2)])
        TSm = {}
        tsm_t = S.sb("tsm", [128, 11, NSMP], F32)
        tsm_b = S.sb("tsmb", [128, 2, NSMP], BF16)
        k_ = 0
        for role, n in (("acc", 3), ("r", 2), ("i", 2), ("sq", 2), ("gg", 2)):
            TSm[role] = Ring([V(tsm_t.t[:, k_ + i, :]) for i in range(n)])
            k_ += n
        TSm["bcb"] = Ring([V(tsm_b.t[:, i, :]) for i in range(2)])
        T["sqn"] = Ring([S.sb(f"t_sqn{i}", [128, NB], BF16) for i in range(2)])
        T["vn"] = Ring([S.sb(f"t_vn{i}", [128, NB], BF16) for i in range(2)])
        ssq_t = S.sb("ssq", [128, 4], F32)
        T["ssq"] = Ring([V(ssq_t.t[:, i:i + 1]) for i in range(4)])
        P = Ring([S.ps(f"ps{i}") for i in range(8)])

        xstage = S.sbc("xstage", [128, D], F32)
        T["rows"] = Ring([S.sbc(f"rows{i}", [128, NB], F32) for i in range(2)])

        wring = S.sb("wring", [128, NSLOT * SLOTW], BF16)
        slots = [Buf(None, wring.t[:, i * SLOTW:(i + 1) * SLOTW], S.chan(f"c_wslot{i}")) for i in range(NSLOT)]
        wctr = [0]

        ident = S.sb("ident_sb", [128, 128], F32)
        tril = S.sb("tril_sb", [128, 128], F32)
        pv = S.sb("pv", [128, PV_ROWS], F32)
        clv = S.sb("clv", [128, 32], F32)
        hbv = S.sb("hbv", [128, 32], F32)
        vgb = S.sb("vgb", [128, NL, 512], F32)
        icn = S.sb("icn", [128, 64], F32)
        wsb = S.sb("wsb", [128, 8], F32)
        bsb = S.sb("bsb", [128, 8], F32)
        bsbt = S.sb("bsbt", [128, NL * 4, 128], F32)
        mhalf = S.sb("mhalf", [128, 1], F32)
        ones_bf = S.sb("ones_bf", [128, 128], BF16)
        poolw = S.sb("poolw", [128, NL * 4, 128], BF16)
        lruwa = S.sb("lruwa", [128, NL * 8, 128], BF16)
        lruwx = S.sb("lruwx", [128, NL * 8, 128], BF16)
        wsT = S.sb("wsT", [128, NL * 4, 128], BF16)
        cpt = S.sb("cpt", [128, 2, NL, 4, 15], F32)
        crt = S.sb("crt", [128, 2, NL, 8, 3], F32)
        cht = S.sb("cht", [128, NL, 8, 1], F32)
        cft = S.sb("cft", [128, 2, NL, 24, 2], F32)
        cp_v = [[[V(cpt.t[:, q, l, j, :]) for q in range(2)] for j in range(4)] for l in range(NL)]
        cr_v = [[[V(crt.t[:, q, l, j, :]) for q in range(2)] for j in range(8)] for l in range(NL)]
        ch_v = [[V(cht.t[:, l, j, :]) for j in range(8)] for l in range(NL)]
        cf_v = [[[V(cft.t[:, q, l, j, :]) for q in range(2)] for j in range(24)] for l in range(NL)]
        exs = S.sb("exs", [128, 4, NSMP, 16], F32)
        ex3 = S.sb("ex3", [128, 8, NSMP, 4], F32)
        h0t = S.sb("h0t", [128, 8, NSMP], F32)
        ex2 = S.sb("ex2", [128, 24, NSMP, 3], F32)
        exs_v = [V(exs.t[:, j]) for j in range(4)]
        ex3_v = [V(ex3.t[:, j]) for j in range(8)]
        h0_v = [V(h0t.t[:, j]) for j in range(8)]
        ex2_v = [V(ex2.t[:, j]) for j in range(24)]

        ch_misc = S.chan("misc")
        ch_d2d = S.chan("d2d")
        ch_cv = S.chan("cvout")

        def ACT(out, in_, func, reads, writes, **kw):
            S.op("act", lambda e: e.activation(out, in_, func, **kw), reads, writes)

        def TT(out, a, b, op, reads, writes, eng="dve"):
            S.op(eng, lambda e: e.tensor_tensor(out, a, b, op), reads, writes)

        def TS(out, a, s1, s2, op0, op1, reads, writes, eng="dve"):
            S.op(eng, lambda e: e.tensor_scalar(out, a, s1, s2, op0, op1), reads, writes)

        def STT(out, a, s, b, op0, op1, reads, writes):
            S.op("dve", lambda e: e.scalar_tensor_tensor(out, a, s, b, op0, op1), reads, writes)

        def CP(out, in_, reads, writes, eng="dve"):
            if eng == "act":
                S.op("act", lambda e: e.activation(out, in_, AF.Copy), reads, writes)
            else:
                S.op(eng, lambda e: e.tensor_copy(out, in_), reads, writes)

        def MM(psb, out, l, r, start, stop, reads, sig):
            S.op("pe", lambda e: e.matmul(out, l, r, start=start, stop=stop), reads, [psb], sig=sig)

        def mm(psb, out, pairs, reads):
            n = len(pairs)
            for i, pr in enumerate(pairs):
                last = i == n - 1
                rd = list(pr[2]) if len(pr) > 2 else []
                if i == 0 or last:
                    rd = rd + list(reads)
                MM(psb, out, pr[0], pr[1], i == 0, last, rd, last)

        def mm_multi(groups):
            n = max(len(g[2]) for g in groups)
            for i in range(n):
                for psb, out, pairs, reads in groups:
                    if i < len(pairs):
                        pr = pairs[i]
                        last = i == len(pairs) - 1
                        rd = list(pr[2]) if len(pr) > 2 else []
                        if i == 0 or last:
                            rd = rd + list(reads)
                        MM(psb, out, pr[0], pr[1], i == 0, last, rd, last)

        def TR(psb, out, in_, idn, reads, sig=True):
            S.op("pe", lambda e: e.transpose(out, in_, idn), list(reads) + [ident], [psb], sig=sig)

        def pvc(i):
            return pv.t[:, i:i + 1]

        def fetch(pieces):
            slot = slots[wctr[0] % NSLOT]
            wctr[0] += 1
            views, off = [], 0
            for pi, p in enumerate(pieces):
                K, ncol = p.shape
                kt = K // 128
                v = slot.a[:, off:off + kt * ncol].rearrange("p (k c) -> p k c", k=kt)
                S.dma("pool", slot.ch, v, p.rearrange("(k p) c -> p k c", p=128),
                      writes=[slot] if pi == 0 else (), join=() if pi == 0 else [slot])
                views.append(v)
                off += kt * ncol
            assert off <= SLOTW
            return slot, views

        def fetch2(piece):
            if wctr[0] % 2:
                wctr[0] += 1
            i0 = wctr[0] % NSLOT
            wctr[0] += 2
            sa, sb_ = slots[i0], slots[i0 + 1]
            v = wring.t[:, i0 * SLOTW:(i0 + 2) * SLOTW].rearrange("p (k c) -> p k c", k=8)
            S.dma("pool", sa.ch, v, piece.rearrange("(k p) c -> p k c", p=128), writes=[sa, sb_])
            return [sa, sb_], v

        def emit_rows(srcs, n, dst, reads, final=True):
            ps = P.get()
            for i, s_ in enumerate(srcs):
                TR(ps, ps.a[0:n, i * 128:(i + 1) * 128], s_, ident.a, reads, sig=(i == len(srcs) - 1))
            rw = T["rows"].get()
            CP(rw.a[0:n, :], ps.a[0:n, :], [ps], [rw])
            S.dma("sp", rw.ch, dst, rw.a[0:n, :], reads=[rw], final=final)

        class StageA:
            deps = [xstage]

            @staticmethod
            def load(src_ap, n):
                S.dma("sp", xstage.ch, xstage.a[0:n, :], src_ap, writes=[xstage])

            @staticmethod
            def ap(kt, n):
                return xstage.a[0:n, kt * 128:(kt + 1) * 128]

        class StageB:
            deps = list(T["rows"].b)

            @staticmethod
            def load(src_ap, n):
                for h_ in range(2):
                    rb = T["rows"].b[h_]
                    S.dma("sp", rb.ch, rb.a[0:n, :], src_ap[:, h_ * 512:(h_ + 1) * 512], writes=[rb])

            @staticmethod
            def ap(kt, n):
                return T["rows"].b[kt // 4].a[0:n, (kt % 4) * 128:(kt % 4 + 1) * 128]

        stages = [StageA, StageB]

        setup = [ident, tril, vgb, icn, wsb, bsb]
        S.dma("sp", ch_misc, ident.a, ident_d, writes=[ident])
        S.dma("sp", ch_misc, tril.a, tril_d, writes=[tril])
        for l in range(NL):
            S.dma("sp", ch_misc, vgb.t[:, l, :], chunk_vg[l].partition_broadcast(128), writes=[vgb] if l == 0 else (),
                  join=() if l == 0 else [vgb])
        S.dma("sp", ch_misc, icn.a, invcnt_d[0].partition_broadcast(128), writes=[icn])
        S.dma("sp", ch_misc, wsb.a, ws00_d[0].partition_broadcast(128), writes=[wsb])
        S.dma("sp", ch_misc, bsb.a, bs0_d[0].partition_broadcast(128), writes=[bsb])
        S.batch_end(ch_misc, setup)
        rw0 = T["rows"].get()
        S.dma("sp", rw0.ch, rw0.a[:, 0:384].rearrange("p (r c) -> p r c", r=3),
              pvec.rearrange("(r p) c -> p r c", p=128), writes=[rw0])
        S.dma("sp", xstage.ch, xstage.a.rearrange("p (g j) -> p g j", g=8),
              chunk_ws.rearrange("l g i j -> i (l g) j"), writes=[xstage])
        ch_miscw = S.chan("miscw")
        for dst_, src_ in ((poolw, pool_w), (lruwa, lru_wa), (lruwx, lru_wx)):
            S.dma("pool", ch_miscw, dst_.a, src_.rearrange("l g i j -> i (l g) j"), writes=[dst_])
        S.batch_end(ch_miscw, [poolw, lruwa, lruwx])
        S.op("dve", lambda e: e.memset(ones_bf.a, 1.0), (), [ones_bf])
        S.op("dve", lambda e: e.memset(mhalf.a, -0.5), (), [mhalf])
        for cb_ in (cpt, crt, cht, cft):
            S.op("dve", lambda e, cb_=cb_: e.memset(cb_.a, 0.0), (), [cb_])
        for l in range(NL):
            for lst, n in ((cp_v, 4), (cr_v, 8), (cf_v, 24)):
                for j in range(n):
                    for q in range(2):
                        lst[l][j][q].w = [("dve", S.cnt["dve"])]
            for j in range(8):
                ch_v[l][j].w = [("dve", S.cnt["dve"])]
        ps = P.get()
        for r_ in range(3):
            TR(ps, ps.a[:, r_ * 128:(r_ + 1) * 128], rw0.a[:, r_ * 128:(r_ + 1) * 128], ident.a, [rw0], sig=(r_ == 2))
        CP(pv.a, ps.a[:, 0:PV_ROWS], [ps], [pv])
        ACT(clv.t[:, 0:16], pv.t[:, PV_LAM:PV_LAM + 16], AF.Exp, [pv], [clv], scale=-1.0)
        ACT(clv.t[:, 0:16], clv.t[:, 0:16], AF.Ln, [clv], [clv], bias=1.0)
        TS(clv.t[:, 16:32], clv.t[:, 0:16], -4.0, None, ALU.mult, ALU.bypass, [clv], [clv])
        TS(clv.t[:, 0:16], clv.t[:, 0:16], -8.0, None, ALU.mult, ALU.bypass, [clv], [clv])
        TS(hbv.a, pv.t[:, PV_LBA:PV_LBA + 32], 0.5, None, ALU.mult, ALU.bypass, [pv], [hbv])
        for lg in range(8):
            blk_ = xstage.a[:, lg * 128:(lg + 1) * 128]
            TT(blk_, blk_, tril.a, ALU.mult, [xstage, tril], [xstage])
        for h_ in range(2):
            ps = P.get()
            for i in range(4):
                lg = h_ * 4 + i
                TR(ps, ps.a[:, i * 128:(i + 1) * 128], xstage.a[:, lg * 128:(lg + 1) * 128], ident.a, [xstage], sig=(i == 3))
            CP(wsT.t[:, h_ * 4:(h_ + 1) * 4, :], ps.a.rearrange("p (a b) -> p a b", a=4), [ps], [wsT])
        ch_misc2 = S.chan("misc2")
        S.dma("sp", ch_misc2, bsbt.a.rearrange("p a b -> p (a b)"), chunk_bs.rearrange("l g i -> (l g i)").partition_broadcast(128),
              writes=[bsbt])

        def load_x(c):
            if 2 in c.blocks:
                S.dma("sp", xstage.ch, xstage.a[0:NSMP, :], x_sample, writes=[xstage])
                for g in range(2):
                    ps = P.get()
                    for i in range(4):
                        kt = g * 4 + i
                        TR(ps, ps.a[:, i * NSMP:(i + 1) * NSMP], xstage.a[0:NSMP, kt * 128:(kt + 1) * 128],
                           ident.a[0:NSMP, 0:NSMP], [xstage], sig=(i == 3))
                    CP(xst.t[:, g * 4:(g + 1) * 4, :], ps.a[:, 0:4 * NSMP].rearrange("p (a b) -> p a b", a=4),
                       [ps], [x[g * 4 + i][2] for i in range(4)])
            for tt in range(8):
                stg = stages[tt % 2]
                stg.load(x_prompt[c.tok0 + tt * 128:c.tok0 + (tt + 1) * 128, :], 128)
                blk, c0 = tt // 4, (tt % 4) * 128
                for g in range(2):
                    ps = P.get()
                    for i in range(4):
                        kt = g * 4 + i
                        TR(ps, ps.a[:, i * 128:(i + 1) * 128], stg.ap(kt, 128), ident.a, stg.deps, sig=(i == 3))
                    CP(xt.t[:, g * 4:(g + 1) * 4, blk * NB + c0:blk * NB + c0 + 128],
                       ps.a.rearrange("p (a b) -> p a b", a=4), [ps], [x[g * 4 + i][blk] for i in range(4)],
                       eng="act" if g == 0 else "dve")
                if tt % 4 == 3:
                    norm_blk(c, PV_N1G, blk)
            if 2 in c.blocks:
                norm_blk(c, PV_N1G, 2)

        def rstd_blk(c, blk):
            N = NW[blk]
            ps = P.get()
            for kt in range(8):
                sq = T["sqn"].get()
                ACT(sq.a[:, :N], x[kt][blk].a[:, :N], AF.Square, [x[kt][blk]], [sq])
                MM(ps, ps.a[:, :N], ones_bf.a, sq.a[:, :N], kt == 0, kt == 7, [sq, ones_bf], True)
            rs = T["r"].get()
            ACT(rs.a[:, :N], ps.a[:, :N], AF.Ln, [ps], [rs], bias=EPS, scale=1.0 / D)
            ACT(rs.a[:, :N], rs.a[:, :N], AF.Exp, [rs], [rs], scale=-0.5)
            return rs

        def norm_blk(c, g0, blk):
            N = NW[blk]
            rs = rstd_blk(c, blk)
            for kt in range(8):
                STT(xn[kt][blk].a[:, :N], x[kt][blk].a[:, :N], pvc(g0 + kt), rs.a[:, :N], ALU.mult, ALU.mult,
                    [x[kt][blk], rs, pv], [xn[kt][blk]])

        def norm_to_xn(c, g0):
            for blk in c.blocks:
                norm_blk(c, g0, blk)

        def residual_stage(c, ncg, make_group, after_blk, tail=2):
            units = [(0,), (1, 2)] if 2 in c.blocks else [(0,), (1,)]
            for cg in range(ncg - tail):
                grp = make_group(cg)
                for m2 in range(2):
                    for unit in units:
                        grp(m2, unit)
            grps = [make_group(cg) for cg in range(ncg - tail, ncg)]
            for ui, unit in enumerate(units):
                first = True
                for grp in grps:
                    for m2 in range(2):
                        grp(m2, unit)
                        if ui > 0 and first:
                            for b_ in units[ui - 1]:
                                after_blk(b_)
                        first = False
            for b_ in units[-1]:
                after_blk(b_)

        def xn_pairs(wv, cs, blk, N):
            return [(wv[:, kt, cs], xn[kt][blk].a[:, :N], [xn[kt][blk]]) for kt in range(8)]

        def xn_reads(blk):
            return []

        def run_pipelined(items, phase_fns, ascending=False):
            n, npz = len(items), len(phase_fns)
            states = [dict() for _ in items]
            import types
            for step in range(n + npz - 1):
                gens = []
                for ph in (range(npz) if ascending else range(npz - 1, -1, -1)):
                    it = step - ph
                    if 0 <= it < n:
                        g = phase_fns[ph](items[it], states[it])
                        if isinstance(g, types.GeneratorType):
                            gens.append(g)
                while gens:
                    for g in list(gens):
                        try:
                            next(g)
                        except StopIteration:
                            gens.remove(g)

        def stage_pool(c, l):
            items = [(j, blk) for jp in range(0, 4, 2) for blk in c.blocks for j in (jp, jp + 1)]
            sl = {}

            def wts(j):
                g2 = j // 2
                if g2 not in sl:
                    sl[g2] = fetch([w_in[l][:, C_POOL + g2 * 256:C_POOL + (g2 + 1) * 256]])
                return sl[g2][0], sl[g2][1][0]

            def p0(it, s):
                j, blk = it
                N = NW[blk]
                slot, wv = wts(j)
                ps = P.get()
                mm(ps, ps.a[:, :N], xn_pairs(wv, slice((j % 2) * 128, (j % 2 + 1) * 128), blk, N), [slot])
                if blk != 2:
                    ext = T["ext"].get()
                    cvi, cvo = cp_v[l][j][blk], cp_v[l][j][1 - blk]
                    CP(ext.a[:, 0:15], cvi.a, [cvi], [ext])
                    CP(ext.a[:, 15:527], ps.a, [ps], [ext], eng="act")
                    CP(cvo.a, ps.a[:, 512 - 15:512], [ps], [cvo], eng="act")
                    s["ext"] = ext
                else:
                    CP(exs.t[:, j, :, 15], ps.a[:, :N], [ps], [exs_v[j]], eng="act")

            def p1(it, s):
                j, blk = it
                N, w = NW[blk], WIN[j]
                d = T["bcb"].get()
                s["d"] = d
                if blk != 2:
                    ext = s["ext"]
                    top = j + 1
                    los = {top: 15}
                    for k in range(top, 1, -1):
                        los[k - 1] = los[k] - 2 ** (k - 1)
                    bufs = [T["acc"].get(), T["r"].get()]
                    prev = ext
                    for k in range(1, top + 1):
                        lo, sh = los[k], 2 ** (k - 1)
                        cur = bufs[(k - 1) % 2]
                        TT(cur.a[:, lo:527], prev.a[:, lo:527], prev.a[:, lo - sh:527 - sh], ALU.add, [prev], [cur])
                        prev = cur
                    STT(d.a, prev.a[:, 15:527], 1.0 / w, ext.a[:, 15:527], ALU.mult, ALU.subtract, [prev, ext], [d])
                    if c.tok0 == 0 and blk == 0:
                        tm = T["sq"].get()
                        TT(tm.a[:, 0:16], prev.a[:, 15:31], icn.t[:, j * 16:(j + 1) * 16], ALU.mult, [prev, icn], [tm])
                        TT(d.a[:, 0:16], tm.a[:, 0:16], ext.a[:, 15:31], ALU.subtract, [tm, ext], [d])
                else:
                    ev = exs_v[j]
                    sr = T["acc"].get()
                    S.op("dve", lambda e: e.tensor_reduce(sr.a[:, :N], exs.t[:, j, :, 16 - w:16], AX.X, ALU.add), [ev], [sr])
                    STT(d.a[:, :N], sr.a[:, :N], 1.0 / w, exs.t[:, j, :, 15], ALU.mult, ALU.subtract, [sr, ev], [d])

            def p2(it, s):
                j, blk = it
                N, d = NW[blk], s["d"]
                ps2 = P.get()
                MM(ps2, ps2.a[:, :N], poolw.t[:, l * 4 + j, :], d.a[:, :N], True, True, [poolw, d], True)
                ACT(ya[j][blk].a[:, :N], ps2.a[:, :N], AF.Identity, [ps2, pv], [ya[j][blk]], scale=pvc(PV_PSC + l * 4 + j))

            run_pipelined(items, [p0, p1, p2])
            if 2 in c.blocks:
                emit_rows([exs.t[:, j, :, 15] for j in range(4)], NSMP, o_pool_s[l, :, 14, :], exs_v)

        def stage_rnn(c, l):
            items = [(j, blk) for jp in range(0, 8, 2) for blk in (0, 1) for j in (jp, jp + 1)]
            has_s = 2 in c.blocks
            sl = {}

            def wts(j):
                jg = j // 2
                if jg not in sl:
                    sx, (wxv,) = fetch([w_in[l][:, C_RX + jg * 256:C_RX + (jg + 1) * 256]])
                    sg, (wgv,) = fetch([w_in[l][:, C_RG + jg * 256:C_RG + (jg + 1) * 256]])
                    sl[jg] = (sx, wxv, sg, wgv)
                return sl[jg]

            def tmp(role, blk):
                return (TSm if blk == 2 else T)[role].get()

            def subs(it, s):
                j, blk = it
                out = [(j, blk, s)]
                if blk == 1 and has_s:
                    out.append((j, 2, s.setdefault("smp", {})))
                return out

            def p0(it, s):
                late = []
                for j, blk, st_ in subs(it, s):
                    N = NW[blk]
                    sx, wxv, sg, wgv = wts(j)
                    cs = slice((j % 2) * 128, (j % 2 + 1) * 128)
                    ps = P.get()
                    mm(ps, ps.a[:, :N], xn_pairs(wxv, cs, blk, N), [sx])
                    if blk == 2:
                        CP(ex3.t[:, j, :, 3], ps.a[:, :N], [ps], [ex3_v[j]], eng="act")
                    else:
                        late.append((j, blk, st_, ps))
                for _ in range(10):
                    yield
                for j, blk, st_, ps in late:
                    ext = T["ext"].get()
                    cvi, cvo = cr_v[l][j][blk], cr_v[l][j][1 - blk]
                    CP(ext.a[:, 0:3], cvi.a, [cvi], [ext])
                    CP(ext.a[:, 3:515], ps.a, [ps], [ext])
                    CP(cvo.a, ps.a[:, 512 - 3:512], [ps], [cvo])
                    st_["ext"] = ext

            def p1(it, s):
                first = True
                for j, blk, st_ in subs(it, s):
                    N = NW[blk]
                    wk = [pvc(PV_RCW + (l * 4 + k) * 8 + j) for k in range(4)]
                    cb = pvc(PV_RCB + l * 8 + j)
                    acc, bcb = tmp("acc", blk), tmp("bcb", blk)
                    st_.update(acc=acc, bcb=bcb)
                    if first:
                        yield
                        first = False
                    if blk != 2:
                        ext = st_["ext"]
                        taps = [ext.a[:, k:k + 512] for k in range(4)]
                        rd = [ext]
                    else:
                        taps = [ex3.t[:, j, :, k] for k in range(4)]
                        rd = [ex3_v[j]]
                    TS(acc.a[:, :N], taps[3], wk[3], cb, ALU.mult, ALU.add, rd + [pv], [acc])
                    yield
                    for k in range(3):
                        STT(acc.a[:, :N], taps[k], wk[k], acc.a[:, :N], ALU.mult, ALU.add, rd + [acc, pv], [acc])
                        yield
                    CP(bcb.a[:, :N], acc.a[:, :N], [acc], [bcb])

            def p2(it, s):
                for j, blk, st_ in subs(it, s):
                    N, bcb = NW[blk], st_["bcb"]
                    sx, wxv, sg, wgv = wts(j)
                    cs = slice((j % 2) * 128, (j % 2 + 1) * 128)
                    lj = l * 8 + j
                    if blk != 2:
                        psg, psr, psi = P.get(), P.get(), P.get()
                        og, orr, oi = psg.a[:, :N], psr.a[:, :N], psi.a[:, :N]
                    else:
                        psg = psr = psi = P.get()
                        og, orr, oi = psg.a[:, 0:N], psg.a[:, N:2 * N], psg.a[:, 2 * N:3 * N]
                    mm(psg, og, xn_pairs(wgv, cs, blk, N), [sg])
                    MM(psr, orr, lruwa.t[:, lj, :], bcb.a[:, :N], True, True, [lruwa, bcb], True)
                    MM(psi, oi, lruwx.t[:, lj, :], bcb.a[:, :N], True, True, [lruwx, bcb], True)
                    st_.update(bg=psg, br=psr, bi=psi, og=og, orr=orr, oi=oi)

            def p3(it, s):
                ss = subs(it, s)
                for j, blk, st_ in ss:
                    N, lj = NW[blk], l * 8 + j
                    r, ii, sq, gg = tmp("r", blk), tmp("i", blk), tmp("sq", blk), tmp("gg", blk)
                    st_.update(r=r, ii=ii, sq=sq, gg=gg)
                    ACT(gg.a[:, :N], st_["og"], AF.Gelu_apprx_tanh, [st_["bg"]], [gg])
                    ACT(r.a[:, :N], st_["orr"], AF.Tanh, [st_["br"], hbv], [r], bias=hbv.t[:, lj:lj + 1], scale=0.5)
                    ACT(ii.a[:, :N], st_["oi"], AF.Tanh, [st_["bi"], hbv], [ii], bias=hbv.t[:, 16 + lj:17 + lj], scale=0.5)
                for j, blk, st_ in ss:
                    N, lj, r, sq = NW[blk], l * 8 + j, st_["r"], st_["sq"]
                    ACT(sq.a[:, :N], r.a[:, :N], AF.Exp, [r, clv], [sq], scale=clv.t[:, lj:lj + 1], bias=clv.t[:, lj:lj + 1])
                    ACT(r.a[:, :N], r.a[:, :N], AF.Exp, [r, clv], [r], scale=clv.t[:, 16 + lj:17 + lj], bias=clv.t[:, 16 + lj:17 + lj])
                    ACT(sq.a[:, :N], sq.a[:, :N], AF.Ln, [sq], [sq], scale=-1.0, bias=1.0)
                    ACT(sq.a[:, :N], sq.a[:, :N], AF.Exp, [sq], [sq], scale=0.5)

            def p4(it, s):
                for j, blk, st_ in subs(it, s):
                    N, acc, r, ii, sq, gg = NW[blk], st_["acc"], st_["r"], st_["ii"], st_["sq"], st_["gg"]
                    STT(ii.a[:, :N], ii.a[:, :N], 1.0, acc.a[:, :N], ALU.add, ALU.mult, [ii, acc], [ii])
                    yield
                    STT(ii.a[:, :N], ii.a[:, :N], 0.5, sq.a[:, :N], ALU.mult, ALU.mult, [ii, sq], [ii])
                    yield
                    h = sq
                    if blk != 2:
                        hv = ch_v[l][j]
                        S.op("dve", lambda e, h=h, r=r, ii=ii, hv=hv, N=N: e.tensor_tensor_scan(h.a[:, :N], r.a[:, :N], ii.a[:, :N], hv.a,
                                                                                              ALU.mult, ALU.add), [r, ii, hv], [h])
                        yield
                        CP(hv.a, h.a[:, N - 1:N], [h], [hv])
                    else:
                        TT(h.a[:, :N], r.a[:, :N], h0t.t[:, j, :], ALU.mult, [r, h0_v[j]], [h])
                        TT(h.a[:, :N], h.a[:, :N], ii.a[:, :N], ALU.add, [h, ii], [h])
                        CP(h0t.t[:, j, :], h.a[:, :N], [h], [h0_v[j]])
                    yield
                    TT(yb[j][blk].a[:, :N], gg.a[:, :N], h.a[:, :N], ALU.mult, [gg, h], [yb[j][blk]])

            run_pipelined(items, [p0, p1, p2, p3, p4])
            if has_s:
                for g in range(2):
                    emit_rows([ex3.t[:, g * 4 + i, :, 3] for i in range(4)], NSMP, o_rc_s[l, :, 2, g * 512:(g + 1) * 512], ex3_v)
                    emit_rows([h0t.t[:, g * 4 + i, :] for i in range(4)], NSMP, o_h_s[l, :, g * 512:(g + 1) * 512], h0_v)

        def rstd_pow(ssq, n):
            S.op("pool", lambda e: e.tensor_scalar(ssq.a[0:n, :], ssq.a[0:n, :], 1.0 / 512, EPS, ALU.mult, ALU.add), [ssq], [ssq])
            S.op("pool", lambda e: e.tensor_tensor(ssq.a[0:n, :], ssq.a[0:n, :], mhalf.a[0:n, :], ALU.pow), [ssq, mhalf], [ssq])

        def stage_chunk(c, l):
            svd, wvv = fetch2(w_in[l][:, C_V:C_V + 512])
            sus = [fetch([w_in[l][:, C_U + h_ * 256:C_U + (h_ + 1) * 256]]) for h_ in range(2)]

            def u_pairs(g, blk, N):
                return xn_pairs(sus[g // 2][1][0], slice((g % 2) * 128, (g % 2 + 1) * 128), blk, N), [sus[g // 2][0]]
            if 2 in c.blocks:
                N = NSMP
                ps = P.get()
                mm(ps, ps.a[0:N, :], [(xn[kt][2].a[:, 0:N], wvv[:, kt, :], [xn[kt][2]]) for kt in range(8)], svd)
                vg, junk, ssq, vnf = T["r"].get(), T["sq"].get(), T["ssq"].get(), T["acc"].get()
                ACT(vg.a[0:N, :512], ps.a[0:N, :], AF.Gelu_apprx_tanh, [ps], [vg])
                ACT(junk.a[0:N, :512], vg.a[0:N, :512], AF.Square, [vg], [junk, ssq], accum_out=ssq.a[0:N, :])
                rstd_pow(ssq, N)
                STT(vnf.a[0:N, :512], vg.a[0:N, :512], ssq.a[0:N, :], vgb.t[0:N, l, :], ALU.mult, ALU.mult, [vg, ssq, vgb], [vnf])
                S.dma("sp", ch_cv, o_cv_s[l], vnf.a[0:N, :512], reads=[vnf], final=True)
                for g in range(4):
                    pst = P.get()
                    TR(pst, pst.a[:, 0:N], vnf.a[0:N, g * 128:(g + 1) * 128], ident.a[0:N, 0:N], [vnf])
                    mixs = T["i"].get()
                    lg = l * 4 + g
                    ACT(mixs.a[:, :N], pst.a[:, :N], AF.Identity, [pst, wsb, bsb], [mixs], scale=wsb.t[:, lg:lg + 1], bias=bsb.t[:, lg:lg + 1])
                    psu = P.get()
                    prs, rds = u_pairs(g, 2, N)
                    mm(psu, psu.a[:, :N], prs, rds)
                    ug = T["gg"].get()
                    ACT(ug.a[:, :N], psu.a[:, :N], AF.Gelu_apprx_tanh, [psu], [ug])
                    TT(yc[g][2].a[:, :N], ug.a[:, :N], mixs.a[:, :N], ALU.mult, [ug, mixs], [yc[g][2]])
            mixb = P.b[0:4]
            PG = Ring(P.b[4:8])
            for blk in range(2):
                items = [("v", c4) for c4 in range(4)] + [("u", g) for g in range(4)]

                def q0(it, s, blk=blk):
                    kind, i = it
                    ps = PG.get()
                    if kind == "v":
                        tc = slice(i * 128, (i + 1) * 128)
                        mm(ps, ps.a, [(xn[kt][blk].a[:, tc], wvv[:, kt, :], [xn[kt][blk]]) for kt in range(8)], svd)
                        vg, junk, ssq = T["r"].get(), T["sq"].get(), T["ssq"].get()
                        s.update(vg=vg, ssq=ssq)
                        ACT(vg.a[:, :512], ps.a, AF.Gelu_apprx_tanh, [ps], [vg])
                        ACT(junk.a[:, :512], vg.a[:, :512], AF.Square, [vg], [junk, ssq], accum_out=ssq.a)
                    else:
                        prs, rds = u_pairs(i, blk, 512)
                        mm(ps, ps.a, prs, rds)
                        ug = T["gg"].get()
                        s["ug"] = ug
                        ACT(ug.a[:, :512], ps.a, AF.Gelu_apprx_tanh, [ps], [ug])

                def q1(it, s, blk=blk):
                    kind, i = it
                    if kind == "v":
                        tc = slice(i * 128, (i + 1) * 128)
                        vg, ssq, vn = s["vg"], s["ssq"], T["vn"].get()
                        rstd_pow(ssq, 128)
                        STT(vn.a, vg.a[:, :512], ssq.a, vgb.t[:, l, :], ALU.mult, ALU.mult, [vg, ssq, vgb], [vn])
                        for g in range(4):
                            lg = l * 4 + g
                            MM(mixb[g], mixb[g].a[:, tc], vn.a[:, g * 128:(g + 1) * 128], wsT.t[:, lg, :], True, True, [vn, wsT], g == 3)
                    else:
                        ug, tb = s["ug"], T["i"].get()
                        lg = l * 4 + i
                        TT(tb.a[:, :512].rearrange("p (a b) -> p a b", a=4), mixb[i].a.rearrange("p (a b) -> p a b", a=4),
                           bsbt.t[:, lg, :].unsqueeze(1).to_broadcast([128, 4, 128]), ALU.add, [mixb[i], bsbt], [tb])
                        TT(yc[i][blk].a, ug.a[:, :512], tb.a[:, :512], ALU.mult, [ug, tb], [yc[i][blk]])

                run_pipelined(items, [q0, q1], ascending=True)


        def stage_merge(c, l):
            for mg in range(4):
                c0 = mg * 256
                sA, (gAv,) = fetch([w_in[l][:, C_G + c0:C_G + c0 + 256]])
                sP, (pav, pcv) = fetch([w_pa[l][:, c0:c0 + 256], w_pc[l][:, c0:c0 + 256]])
                sB, (gBv,) = fetch([w_in[l][:, C_G + 1024 + c0:C_G + 1024 + c0 + 256]])
                sPb, (pbv,) = fetch([w_pb[l][:, c0:c0 + 256]])
                sC, (gCv,) = fetch([w_in[l][:, C_G + 2048 + c0:C_G + 2048 + c0 + 256]])
                plan = [(gAv, sA, pav, sP, ya, 4), (gBv, sB, pbv, sPb, yb, 8), (gCv, sC, pcv, sP, yc, 4)]
                for m2 in range(2):
                    m = mg * 2 + m2
                    cs = slice(m2 * 128, (m2 + 1) * 128)
                    for blk in c.blocks:
                        N = NW[blk]
                        acc, tmp = T["acc"].get(), T["i"].get()
                        for bi, (gv, gs, pw, pslot, ys, nk) in enumerate(plan):
                            psg = P.get()
                            mm(psg, psg.a[:, :N], xn_pairs(gv, cs, blk, N), [gs] + xn_reads(blk))
                            gt = T["gg"].get()
                            ACT(gt.a[:, :N], psg.a[:, :N], AF.Sigmoid, [psg], [gt])
                            psp = P.get()
                            mm(psp, psp.a[:, :N], [(pw[:, kt, cs], ys[kt][blk].a[:, :N], [ys[kt][blk]]) for kt in range(nk)], [pslot])
                            if bi == 0:
                                TT(acc.a[:, :N], gt.a[:, :N], psp.a[:, :N], ALU.mult, [gt, psp], [acc])
                            else:
                                TT(tmp.a[:, :N], gt.a[:, :N], psp.a[:, :N], ALU.mult, [gt, psp], [tmp])
                                if bi == 1:
                                    TT(acc.a[:, :N], acc.a[:, :N], tmp.a[:, :N], ALU.add, [acc, tmp], [acc])
                                else:
                                    TT(mrg[m][blk].a[:, :N], acc.a[:, :N], tmp.a[:, :N], ALU.add, [acc, tmp], [mrg[m][blk]])

        def stage_wo(c, l, after_blk):
            def make_group(cg):
                so, (wov,) = fetch([w_o[l][:, cg * 256:(cg + 1) * 256]])

                def grp(m4, blks):
                    m = cg * 2 + m4
                    gl = []
                    for blk in blks:
                        N, ps = NW[blk], P.get()
                        gl.append((ps, ps.a[:, :N],
                                   [(wov[:, kt, m4 * 128:(m4 + 1) * 128], mrg[kt][blk].a[:, :N], [mrg[kt][blk]]) for kt in range(8)], [so]))
                    mm_multi(gl)
                    for blk, g_ in zip(blks, gl):
                        N = NW[blk]
                        TT(x[m][blk].a[:, :N], x[m][blk].a[:, :N], g_[0].a[:, :N], ALU.add, [x[m][blk], g_[0]], [x[m][blk]])
                return grp
            residual_stage(c, 4, make_group, after_blk)

        def stage_ffn_up(c, l):
            items = [(t, blk) for tp in range(0, 24, 2) for blk in c.blocks for t in (tp, tp + 1)]
            sl = {}

            def wts(t):
                cg = t // 2
                if cg not in sl:
                    sg, (wgv,) = fetch([ffn_wg[l][:, cg * 256:(cg + 1) * 256]])
                    su, (wuv,) = fetch([ffn_wu[l][:, cg * 256:(cg + 1) * 256]])
                    sl[cg] = (sg, wgv, su, wuv)
                return sl[cg]

            def p0(it, s):
                t, blk = it
                N = NW[blk]
                sg, wgv, su, wuv = wts(t)
                cs = slice((t % 2) * 128, (t % 2 + 1) * 128)
                ps = P.get()
                mm(ps, ps.a[:, :N], xn_pairs(wgv, cs, blk, N), [sg] + xn_reads(blk))
                if blk != 2:
                    ext = T["ext"].get()
                    cvi, cvo = cf_v[l][t][blk], cf_v[l][t][1 - blk]
                    CP(ext.a[:, 0:2], cvi.a, [cvi], [ext])
                    CP(ext.a[:, 2:514], ps.a, [ps], [ext], eng="act")
                    CP(cvo.a, ps.a[:, 512 - 2:512], [ps], [cvo], eng="act")
                    s["ext"] = ext
                else:
                    CP(ex2.t[:, t, :, 2], ps.a[:, :N], [ps], [ex2_v[t]], eng="act")

            def p1(it, s):
                t, blk = it
                N = NW[blk]
                wk = [pvc(PV_FCW + (l * 3 + k) * 24 + t) for k in range(3)]
                cb = pvc(PV_FCB + l * 24 + t)
                acc = T["acc"].get()
                s["acc"] = acc
                if blk != 2:
                    ext = s["ext"]
                    TS(acc.a[:, :N], ext.a[:, 2:514], wk[2], cb, ALU.mult, ALU.add, [ext, pv], [acc])
                    for k in range(2):
                        STT(acc.a[:, :N], ext.a[:, k:k + 512], wk[k], acc.a[:, :N], ALU.mult, ALU.add, [ext, acc, pv], [acc])
                else:
                    ev = ex2_v[t]
                    TS(acc.a[:, :N], ex2.t[:, t, :, 2], wk[2], cb, ALU.mult, ALU.add, [ev, pv], [acc])
                    for k in range(2):
                        STT(acc.a[:, :N], ex2.t[:, t, :, k], wk[k], acc.a[:, :N], ALU.mult, ALU.add, [ev, acc, pv], [acc])

            def p2(it, s):
                t, blk = it
                N, acc = NW[blk], s["acc"]
                sg, wgv, su, wuv = wts(t)
                cs = slice((t % 2) * 128, (t % 2 + 1) * 128)
                ACT(acc.a[:, :N], acc.a[:, :N], AF.Gelu_apprx_tanh, [acc], [acc])
                psu = P.get()
                mm(psu, psu.a[:, :N], xn_pairs(wuv, cs, blk, N), [su] + xn_reads(blk))
                s["psu"] = psu

            def p3(it, s):
                t, blk = it
                N, acc, psu = NW[blk], s["acc"], s["psu"]
                TT(hf[t][blk].a[:, :N], acc.a[:, :N], psu.a[:, :N], ALU.mult, [acc, psu], [hf[t][blk]])
                if blk == c.blocks[-1] and t % 4 == 3 and 2 in c.blocks:
                    cg = t // 4
                    emit_rows([ex2.t[:, cg * 4 + i, :, 2] for i in range(4)], NSMP, o_ff_s[l, :, 1, cg * 512:(cg + 1) * 512],
                              ex2_v[cg * 4:cg * 4 + 4])

            run_pipelined(items, [p0, p1, p2, p3])


        def stage_ffn_down(c, l, after_blk):
            def make_group(cg):
                fs = [fetch([ffn_wd[l][kc * 1024:(kc + 1) * 1024, cg * 256:(cg + 1) * 256]]) for kc in range(3)]

                def grp(m4, blks):
                    m = cg * 2 + m4
                    gl = []
                    for blk in blks:
                        N, ps = NW[blk], P.get()
                        gl.append((ps, ps.a[:, :N],
                                   [(fs[kt // 8][1][0][:, kt % 8, m4 * 128:(m4 + 1) * 128], hf[kt][blk].a[:, :N], [hf[kt][blk]])
                                    for kt in range(24)], [f[0] for f in fs]))
                    mm_multi(gl)
                    for blk, g_ in zip(blks, gl):
                        N = NW[blk]
                        TT(x[m][blk].a[:, :N], x[m][blk].a[:, :N], g_[0].a[:, :N], ALU.add, [x[m][blk], g_[0]], [x[m][blk]])
                return grp
            residual_stage(c, 4, make_group, after_blk)

        def load_sample_state(l):
            for hs in range(2):
                rw = T["rows"].get()
                S.dma("sp", rw.ch, rw.a[0:120, :], st_pool[l, hs * 8:(hs + 1) * 8].rearrange("s r c -> (s r) c"), writes=[rw])
                ps = P.get()
                for j in range(4):
                    TR(ps, ps.a[:, j * 120:(j + 1) * 120], rw.a[0:120, j * 128:(j + 1) * 128], ident.a[0:120, 0:120], [rw], sig=(j == 3))
                for j in range(4):
                    CP(exs.t[:, j, hs * 8:(hs + 1) * 8, 0:15], ps.a[:, j * 120:(j + 1) * 120].rearrange("p (s r) -> p s r", r=15),
                       [ps], [exs_v[j]])
            stg = StageA
            stg.load(st_rc[l].rearrange("s r c -> (s r) c"), 48)
            for g in range(2):
                ps = P.get()
                for i in range(4):
                    j = g * 4 + i
                    TR(ps, ps.a[:, i * 48:(i + 1) * 48], stg.ap(j, 48), ident.a[0:48, 0:48], stg.deps, sig=(i == 3))
                for i in range(4):
                    j = g * 4 + i
                    CP(ex3.t[:, j, :, 0:3], ps.a[:, i * 48:(i + 1) * 48].rearrange("p (s r) -> p s r", r=3), [ps], [ex3_v[j]])
            stg = StageB
            stg.load(st_h[l], NSMP)
            for g in range(2):
                ps = P.get()
                for i in range(4):
                    j = g * 4 + i
                    TR(ps, ps.a[:, i * NSMP:(i + 1) * NSMP], stg.ap(j, NSMP), ident.a[0:NSMP, 0:NSMP], stg.deps, sig=(i == 3))
                CP(h0t.t[:, g * 4:(g + 1) * 4, :], ps.a[:, 0:4 * NSMP].rearrange("p (a b) -> p a b", a=4), [ps],
                   [h0_v[g * 4 + i] for i in range(4)])
            for q in range(3):
                stg = stages[q % 2]
                stg.load(st_ff[l][:, :, q * 1024:(q + 1) * 1024].rearrange("s r c -> (s r) c"), 32)
                for g in range(2):
                    ps = P.get()
                    for i in range(4):
                        t = q * 8 + g * 4 + i
                        TR(ps, ps.a[:, i * 32:(i + 1) * 32], stg.ap(g * 4 + i, 32), ident.a[0:32, 0:32], stg.deps, sig=(i == 3))
                    for i in range(4):
                        t = q * 8 + g * 4 + i
                        CP(ex2.t[:, t, :, 0:2], ps.a[:, i * 32:(i + 1) * 32].rearrange("p (s r) -> p s r", r=2), [ps], [ex2_v[t]])
            S.dma("sp", ch_d2d, o_pool_s[l, :, 0:14, :], st_pool[l, :, 1:15, :], final=True)
            S.dma("sp", ch_d2d, o_rc_s[l, :, 0:2, :], st_rc[l, :, 1:3, :], final=True)
            S.dma("sp", ch_d2d, o_ff_s[l, :, 0:1, :], st_ff[l, :, 1:2, :], final=True)

        def emit_prompt_state(l):
            emit_rows([cp_v[l][j][0].a for j in range(4)], 15, o_pool_p[l], [cp_v[l][j][0] for j in range(4)])
            for g in range(2):
                emit_rows([cr_v[l][g * 4 + i][0].a for i in range(4)], 3, o_rc_p[l, :, g * 512:(g + 1) * 512],
                          [cr_v[l][g * 4 + i][0] for i in range(4)])
                emit_rows([ch_v[l][g * 4 + i].a for i in range(4)], 1, o_h_p[l, :, g * 512:(g + 1) * 512], ch_v[l])
            for g in range(6):
                emit_rows([cf_v[l][g * 4 + i][0].a for i in range(4)], 2, o_ff_p[l, :, g * 512:(g + 1) * 512],
                          [cf_v[l][g * 4 + i][0] for i in range(4)])

        def final_store_blk(c, blk):
            if True:
                N = NW[blk]
                rs = rstd_blk(c, blk)
                for g in range(2):
                    yts = [T["acc"].get(), T["i"].get(), T["gg"].get(), T["sq"].get()]
                    for i in range(4):
                        kt = g * 4 + i
                        STT(yts[i].a[:, :N], x[kt][blk].a[:, :N], pvc(PV_FNG + kt), rs.a[:, :N], ALU.mult, ALU.mult,
                            [x[kt][blk], rs, pv], [yts[i]])
                    if blk == 2:
                        ps = P.get()
                        for i in range(4):
                            TR(ps, ps.a[0:N, i * 128:(i + 1) * 128], yts[i].a[:, 0:N], ident.a, [yts[i]], sig=(i == 3))
                        rw = T["rows"].get()
                        CP(rw.a[0:N, :], ps.a[0:N, :], [ps], [rw])
                        S.dma("sp", rw.ch, y_sample[:, g * 512:(g + 1) * 512], rw.a[0:N, :], reads=[rw], final=True)
                    else:
                        for tt in range(4):
                            ps = P.get()
                            for i in range(4):
                                TR(ps, ps.a[:, i * 128:(i + 1) * 128], yts[i].a[:, tt * 128:(tt + 1) * 128], ident.a, [yts[i]], sig=(i == 3))
                            rw = T["rows"].get()
                            CP(rw.a, ps.a, [ps], [rw], eng="act" if tt % 2 == 0 else "dve")
                            r0 = c.tok0 + blk * NB + tt * 128
                            S.dma("sp", rw.ch, y_prompt[r0:r0 + 128, g * 512:(g + 1) * 512], rw.a, reads=[rw], final=True)

        marks = []

        def mark(name):
            marks.append((name, dict(S.nins)))

        def layer(c, l):
            mark(f"L{l}.smpstate")
            if 2 in c.blocks:
                load_sample_state(l)
            mark(f"L{l}.norm1")
            mark(f"L{l}.pool")
            stage_pool(c, l)
            mark(f"L{l}.rnn")
            stage_rnn(c, l)
            mark(f"L{l}.chunk")
            stage_chunk(c, l)
            mark(f"L{l}.merge")
            stage_merge(c, l)
            mark(f"L{l}.wo")
            stage_wo(c, l, lambda blk: norm_blk(c, PV_N2G + l * 8, blk))
            mark(f"L{l}.ffn_up")
            stage_ffn_up(c, l)
            mark(f"L{l}.ffn_down")
            if l + 1 < NL:
                stage_ffn_down(c, l, lambda blk: norm_blk(c, PV_N1G + (l + 1) * 8, blk))
            else:
                stage_ffn_down(c, l, lambda blk: final_store_blk(c, blk))

        for c in (Ctx(0, (0, 1)), Ctx(TH, (0, 1, 2))):
            mark("load_x")
            load_x(c)
            for l in range(NL):
                layer(c, l)
                if c.tok0 == TH:
                    emit_prompt_state(l)
        S.finish()
        build_program.stats = dict(nins=dict(S.nins), nwait=S.nwait)
        build_program.log = S.log
        mark("end")
        build_program.marks = marks
    return nc


_NC = None


def _pack_pvec(p):
    rows = [p["norm1_g"].reshape(16, 128), p["norm2_g"].reshape(16, 128), p["pool_scale"].reshape(8, 128),
            p["rnn_conv_w"].reshape(64, 128), p["rnn_conv_b"].reshape(16, 128), p["lru_ba"].reshape(16, 128),
            p["lru_bx"].reshape(16, 128), p["lru_lambda"].reshape(16, 128), p["ffn_conv_w"].reshape(144, 128),
            p["ffn_conv_b"].reshape(48, 128), p["final_norm_g"].reshape(8, 128)]
    pv = np.concatenate(rows, axis=0)
    out = np.zeros((PV_ROWS, 128), np.float32)
    out[:pv.shape[0]] = pv
    return out


def kernel(**inputs):
    global _NC
    p = {k: np.ascontiguousarray(np.asarray(v, dtype=np.float32)) for k, v in inputs.items()}
    if _NC is None:
        _NC = build_program()
    nc = _NC
    ncore = 8
    shared = {k: p[k] for k in ("w_in", "pool_w", "lru_wa", "lru_wx", "chunk_ws", "chunk_bs", "chunk_vnorm_g", "w_pa", "w_pb",
                                "w_pc", "w_o", "ffn_wg", "ffn_wu", "ffn_wd")}
    shared["pvec"] = _pack_pvec(p)
    shared["ident"] = np.eye(128, dtype=np.float32)
    shared["tril"] = np.tril(np.ones((128, 128), np.float32))
    ic = np.zeros((4, 16), np.float32)
    for j, w in enumerate(WIN):
        ic[j] = 1.0 / np.minimum(np.arange(16) + 1, w)
    shared["invcnt"] = ic.reshape(1, 64)
    shared["ws00"] = np.ascontiguousarray(p["chunk_ws"][:, :, 0, 0]).reshape(1, 8)
    shared["bs0"] = np.ascontiguousarray(p["chunk_bs"][:, :, 0]).reshape(1, 8)
    in_maps = []
    for i in range(ncore):
        m = dict(shared)
        m["x_prompt"] = np.ascontiguousarray(p["x_prompt"][i])
        sl = slice(i * NSMP, (i + 1) * NSMP)
        m["x_sample"] = np.ascontiguousarray(p["x_sample"][sl, 0, :])
        m["state_pool"] = np.ascontiguousarray(p["state_pool"][:, sl])
        m["state_rnn_conv"] = np.ascontiguousarray(p["state_rnn_conv"][:, sl])
        m["state_rnn_h"] = np.ascontiguousarray(p["state_rnn_h"][:, sl])
        m["state_ffn_conv"] = np.ascontiguousarray(p["state_ffn_conv"][:, sl])
        in_maps.append(m)
    res = run_bass_kernel_spmd(nc, in_maps, core_ids=list(range(ncore)))
    R = res.results
    f32 = np.float32
    y_prompt = np.stack([R[i]["y_prompt"] for i in range(ncore)], 0).astype(f32)
    y_sample = np.concatenate([R[i]["y_sample"] for i in range(ncore)], 0)[:, None, :].astype(f32)
    pool_p = np.stack([R[i]["new_pool_prompt"] for i in range(ncore)], 1).astype(f32)
    pool_s = np.concatenate([R[i]["new_pool_sample"] for i in range(ncore)], 1).astype(f32)
    rc_p = np.stack([R[i]["new_rconv_prompt"] for i in range(ncore)], 1).astype(f32)
    rc_s = np.concatenate([R[i]["new_rconv_sample"] for i in range(ncore)], 1).astype(f32)
    h_p = np.stack([R[i]["new_h_prompt"][:, 0, :] for i in range(ncore)], 1).astype(f32)
    h_s = np.concatenate([R[i]["new_h_sample"] for i in range(ncore)], 1).astype(f32)
    ff_p = np.stack([R[i]["new_ffn_prompt"] for i in range(ncore)], 1).astype(f32)
    ff_s = np.concatenate([R[i]["new_ffn_sample"] for i in range(ncore)], 1).astype(f32)
    cv_s = np.concatenate([R[i]["new_chunk_v_sample"] for i in range(ncore)], 1)[:, :, None, :].astype(f32)
    return (y_prompt, y_sample, pool_p, pool_s, rc_p, rc_s, h_p, h_s, ff_p, ff_s, cv_s)
```

```python
import numpy as np
from contextlib import ExitStack
import concourse.bass as bass
import concourse.mybir as mybir
from concourse.bass_utils import run_bass_kernel_spmd

F32 = mybir.dt.float32
BF16 = mybir.dt.bfloat16
AF = mybir.ActivationFunctionType
ALU = mybir.AluOpType
AX = mybir.AxisListType


class Dep:
    __slots__ = ("w", "r")

    def __init__(self):
        self.w = []
        self.r = []


class Buf(Dep):
    __slots__ = ("t", "a", "ch")

    def __init__(self, t, a=None, ch=None):
        super().__init__()
        self.t = t
        self.a = t[:] if a is None else a
        self.ch = ch


def V(ap):
    return Buf(None, ap)


class Sched:
    def __init__(self, nc, st):
        self.nc, self.st = nc, st
        self.eng = dict(pe=nc.tensor, act=nc.scalar, dve=nc.vector, pool=nc.gpsimd, sp=nc.sync)
        self.sems, self.cnt = {}, {}
        for k in self.eng:
            self.sems[k] = st.enter_context(nc.semaphore("sem_" + k))
            self.cnt[k] = 0
        self.waited = {k: {} for k in self.eng}
        self.finals = []
        self.nwait = 0
        self.nins = {k: 0 for k in self.eng}
        self.log = {k: [] for k in self.eng}

    def sb(self, name, shape, dtype):
        return Buf(self.st.enter_context(self.nc.sbuf_tensor(name, list(shape), dtype)))

    def ps(self, name):
        return Buf(self.st.enter_context(self.nc.psum_tensor(name, [128, 512], F32)))

    def chan(self, name):
        self.sems[name] = self.st.enter_context(self.nc.semaphore("sem_" + name))
        self.cnt[name] = 0
        return name

    def _wait(self, e, evs):
        need = {}
        for (k, v) in evs:
            if v > need.get(k, 0):
                need[k] = v
        wd = self.waited[e]
        for k, v in need.items():
            if wd.get(k, 0) >= v:
                continue
            if k == e and e == "pe":
                continue
            self.eng[e].wait_ge(self.sems[k], v)
            self.log[e].append(("w", k, v))
            self.nwait += 1
            wd[k] = v

    def _deps(self, reads, writes):
        evs = []
        for d in reads:
            evs += d.w
        for d in writes:
            evs += d.w
            evs += d.r
        return evs

    def _record(self, ev, reads, writes):
        for d in reads:
            d.r.append(ev)
            if len(d.r) > 24:
                d.r = _compact(d.r)
        for d in writes:
            d.w = [ev]
            d.r = []

    def op(self, e, fn, reads=(), writes=(), sig=True):
        self._wait(e, self._deps(reads, writes))
        ins = fn(self.eng[e])
        self.nins[e] += 1
        if sig:
            self.cnt[e] += 1
            ins.then_inc(self.sems[e], 1)
            self.log[e].append(("i", e, 1))
            ev = (e, self.cnt[e])
        else:
            ev = (e, self.cnt[e] + 1)
        self._record(ev, reads, writes)
        return ev

    def sbc(self, name, shape, dtype):
        b = self.sb(name, shape, dtype)
        b.ch = self.chan("c_" + name)
        return b

    def batch_end(self, ch, deps):
        for d in deps:
            d.w = [(ch, self.cnt[ch])]

    def dma(self, q, ch, out, in_, reads=(), writes=(), final=False, join=(), **kw):
        self._wait(q, self._deps(reads, writes))
        ins = self.eng[q].dma_start(out=out, in_=in_, **kw)
        self.nins[q] += 1
        self.cnt[ch] += 16
        ins.then_inc(self.sems[ch], 16)
        self.log[q].append(("i", ch, 16))
        ev = (ch, self.cnt[ch])
        self._record(ev, reads, writes)
        for d in join:
            d.w.append(ev)
        if final:
            self.finals.append(ev)
        return ev

    def finish(self):
        self._wait("sp", self.finals)
        for e in ("act", "dve", "pool", "pe"):
            self._wait(e, self.finals)


def _compact(evs):
    m = {}
    for k, v in evs:
        if v > m.get(k, 0):
            m[k] = v
    return list(m.items())


NL, D, KT, NB = 2, 1024, 8, 512
SEQ, TH = 2048, 1024
NSMP = 16
WIN = (2, 4, 8, 16)
EPS = 1e-6
C_POOL, C_RX, C_RG, C_U, C_V, C_G = 0, 512, 1536, 2560, 3072, 3584
PV_N1G, PV_N2G, PV_PSC, PV_RCW, PV_RCB, PV_LBA, PV_LBX, PV_LAM, PV_FCW, PV_FCB, PV_FNG = (
    0, 16, 32, 40, 104, 120, 136, 152, 168, 312, 360)
PV_ROWS = 384
NSLOT = 8
SLOTW = 2048


class Ring:
    def __init__(self, bufs):
        self.b, self.i = bufs, 0

    def get(self):
        b = self.b[self.i % len(self.b)]
        self.i += 1
        return b


NW = (NB, NB, NSMP)


class Ctx:
    def __init__(self, tok0, blocks):
        self.tok0, self.blocks = tok0, blocks


def build_program():
    nc = bass.Bass("TRN2", target_bir_lowering=False)

    def din(n, s):
        return nc.dram_tensor(n, list(s), F32, kind="ExternalInput").ap()

    def dout(n, s):
        return nc.dram_tensor(n, list(s), F32, kind="ExternalOutput").ap()

    x_prompt = din("x_prompt", [SEQ, D])
    x_sample = din("x_sample", [NSMP, D])
    st_pool = din("state_pool", [NL, NSMP, 15, 512])
    st_rc = din("state_rnn_conv", [NL, NSMP, 3, 1024])
    st_h = din("state_rnn_h", [NL, NSMP, 1024])
    st_ff = din("state_ffn_conv", [NL, NSMP, 2, 3072])
    w_in = din("w_in", [NL, 1024, 6656])
    pool_w = din("pool_w", [NL, 4, 128, 128])
    lru_wa = din("lru_wa", [NL, 8, 128, 128])
    lru_wx = din("lru_wx", [NL, 8, 128, 128])
    chunk_ws = din("chunk_ws", [NL, 4, 128, 128])
    chunk_bs = din("chunk_bs", [NL, 4, 128])
    chunk_vg = din("chunk_vnorm_g", [NL, 512])
    w_pa = din("w_pa", [NL, 512, 1024])
    w_pb = din("w_pb", [NL, 1024, 1024])
    w_pc = din("w_pc", [NL, 512, 1024])
    w_o = din("w_o", [NL, 1024, 1024])
    ffn_wg = din("ffn_wg", [NL, 1024, 3072])
    ffn_wu = din("ffn_wu", [NL, 1024, 3072])
    ffn_wd = din("ffn_wd", [NL, 3072, 1024])
    pvec = din("pvec", [PV_ROWS, 128])
    ident_d = din("ident", [128, 128])
    tril_d = din("tril", [128, 128])
    invcnt_d = din("invcnt", [1, 64])
    ws00_d = din("ws00", [1, 8])
    bs0_d = din("bs0", [1, 8])

    y_prompt = dout("y_prompt", [SEQ, D])
    y_sample = dout("y_sample", [NSMP, D])
    o_pool_p = dout("new_pool_prompt", [NL, 15, 512])
    o_pool_s = dout("new_pool_sample", [NL, NSMP, 15, 512])
    o_rc_p = dout("new_rconv_prompt", [NL, 3, 1024])
    o_rc_s = dout("new_rconv_sample", [NL, NSMP, 3, 1024])
    o_h_p = dout("new_h_prompt", [NL, 1, 1024])
    o_h_s = dout("new_h_sample", [NL, NSMP, 1024])
    o_ff_p = dout("new_ffn_prompt", [NL, 2, 3072])
    o_ff_s = dout("new_ffn_sample", [NL, NSMP, 2, 3072])
    o_cv_s = dout("new_chunk_v_sample", [NL, NSMP, 512])

    with ExitStack() as st:
        S = Sched(nc, st)
        xt = S.sb("xt", [128, 8, TH], F32)
        xnt = S.sb("xnt", [128, 8, TH], BF16)
        hft = S.sb("hft", [128, 24, TH], BF16)
        xst = S.sb("xst", [128, 8, NSMP], F32)
        xnst = S.sb("xnst", [128, 8, NSMP], BF16)
        hfst = S.sb("hfst", [128, 24, NSMP], BF16)
        x = [[V(xt.t[:, k, b * NB:(b + 1) * NB]) for b in range(2)] + [V(xst.t[:, k, :])] for k in range(8)]
        xn = [[V(xnt.t[:, k, b * NB:(b + 1) * NB]) for b in range(2)] + [V(xnst.t[:, k, :])] for k in range(8)]
        hf = [[V(hft.t[:, k, b * NB:(b + 1) * NB]) for b in range(2)] + [V(hfst.t[:, k, :])] for k in range(24)]
        ya, yb, yc, mrg = hf[0:4], hf[4:12], hf[12:16], hf[16:24]

        TW = 528
        T = {}
        for role, n in (("ext", 2), ("acc", 3), ("r", 2), ("i", 2), ("sq", 2), ("gg", 2)):
            T[role] = Ring([S.sb(f"t_{role}{i}", [128, TW], F32) for i in range(n)])
        T["bcb"] = Ring([S.sb(f"t_bcb{i}", [128, NB], BF16) for i in range(2)])
        TSm = {}
        tsm_t = S.sb("tsm", [128, 11, NSMP], F32)
        tsm_b = S.sb("tsmb", [128, 2, NSMP], BF16)
        k_ = 0
        for role, n in (("acc", 3), ("r", 2), ("i", 2), ("sq", 2), ("gg", 2)):
            TSm[role] = Ring([V(tsm_t.t[:, k_ + i, :]) for i in range(n)])
            k_ += n
        TSm["bcb"] = Ring([V(tsm_b.t[:, i, :]) for i in range(2)])
        T["sqn"] = Ring([S.sb(f"t_sqn{i}", [128, NB], BF16) for i in range(2)])
        T["vn"] = Ring([S.sb(f"t_vn{i}", [128, NB], BF16) for i in range(2)])
        ssq_t = S.sb("ssq", [128, 4], F32)
        T["ssq"] = Ring([V(ssq_t.t[:, i:i + 1]) for i in range(4)])
        P = Ring([S.ps(f"ps{i}") for i in range(8)])

        xstage = S.sbc("xstage", [128, D], F32)
        T["rows"] = Ring([S.sbc(f"rows{i}", [128, NB], F32) for i in range(2)])

        wring = S.sb("wring", [128, NSLOT * SLOTW], BF16)
        slots = [Buf(None, wring.t[:, i * SLOTW:(i + 1) * SLOTW], S.chan(f"c_wslot{i}")) for i in range(NSLOT)]
        wctr = [0]

        ident = S.sb("ident_sb", [128, 128], F32)
        tril = S.sb("tril_sb", [128, 128], F32)
        pv = S.sb("pv", [128, PV_ROWS], F32)
        clv = S.sb("clv", [128, 32], F32)
        hbv = S.sb("hbv", [128, 32], F32)
        vgb = S.sb("vgb", [128, NL, 512], F32)
        icn = S.sb("icn", [128, 64], F32)
        wsb = S.sb("wsb", [128, 8], F32)
        bsb = S.sb("bsb", [128, 8], F32)
        bsbt = S.sb("bsbt", [128, NL * 4, 128], F32)
        mhalf = S.sb("mhalf", [128, 1], F32)
        ones_bf = S.sb("ones_bf", [128, 128], BF16)
        poolw = S.sb("poolw", [128, NL * 4, 128], BF16)
        lruwa = S.sb("lruwa", [128, NL * 8, 128], BF16)
        lruwx = S.sb("lruwx", [128, NL * 8, 128], BF16)
        wsT = S.sb("wsT", [128, NL * 4, 128], BF16)
        cpt = S.sb("cpt", [128, 2, NL, 4, 15], F32)
        crt = S.sb("crt", [128, 2, NL, 8, 3], F32)
        cht = S.sb("cht", [128, NL, 8, 1], F32)
        cft = S.sb("cft", [128, 2, NL, 24, 2], F32)
        cp_v = [[[V(cpt.t[:, q, l, j, :]) for q in range(2)] for j in range(4)] for l in range(NL)]
        cr_v = [[[V(crt.t[:, q, l, j, :]) for q in range(2)] for j in range(8)] for l in range(NL)]
        ch_v = [[V(cht.t[:, l, j, :]) for j in range(8)] for l in range(NL)]
        cf_v = [[[V(cft.t[:, q, l, j, :]) for q in range(2)] for j in range(24)] for l in range(NL)]
        exs = S.sb("exs", [128, 4, NSMP, 16], F32)
        ex3 = S.sb("ex3", [128, 8, NSMP, 4], F32)
        h0t = S.sb("h0t", [128, 8, NSMP], F32)
        ex2 = S.sb("ex2", [128, 24, NSMP, 3], F32)
        exs_v = [V(exs.t[:, j]) for j in range(4)]
        ex3_v = [V(ex3.t[:, j]) for j in range(8)]
        h0_v = [V(h0t.t[:, j]) for j in range(8)]
        ex2_v = [V(ex2.t[:, j]) for j in range(24)]

        ch_misc = S.chan("misc")
        ch_d2d = S.chan("d2d")
        ch_cv = S.chan("cvout")

        def ACT(out, in_, func, reads, writes, **kw):
            S.op("act", lambda e: e.activation(out, in_, func, **kw), reads, writes)

        def TT(out, a, b, op, reads, writes, eng="dve"):
            S.op(eng, lambda e: e.tensor_tensor(out, a, b, op), reads, writes)

        def TS(out, a, s1, s2, op0, op1, reads, writes, eng="dve"):
            S.op(eng, lambda e: e.tensor_scalar(out, a, s1, s2, op0, op1), reads, writes)

        def STT(out, a, s, b, op0, op1, reads, writes):
            S.op("dve", lambda e: e.scalar_tensor_tensor(out, a, s, b, op0, op1), reads, writes)

        def CP(out, in_, reads, writes, eng="dve"):
            if eng == "act":
                S.op("act", lambda e: e.activation(out, in_, AF.Copy), reads, writes)
            else:
                S.op(eng, lambda e: e.tensor_copy(out, in_), reads, writes)

        def MM(psb, out, l, r, start, stop, reads, sig):
            S.op("pe", lambda e: e.matmul(out, l, r, start=start, stop=stop), reads, [psb], sig=sig)

        def mm(psb, out, pairs, reads):
            n = len(pairs)
            for i, pr in enumerate(pairs):
                last = i == n - 1
                rd = list(pr[2]) if len(pr) > 2 else []
                if i == 0 or last:
                    rd = rd + list(reads)
                MM(psb, out, pr[0], pr[1], i == 0, last, rd, last)

        def TR(psb, out, in_, idn, reads, sig=True):
            S.op("pe", lambda e: e.transpose(out, in_, idn), list(reads) + [ident], [psb], sig=sig)

        def pvc(i):
            return pv.t[:, i:i + 1]

        def fetch(pieces):
            slot = slots[wctr[0] % NSLOT]
            wctr[0] += 1
            views, off = [], 0
            for pi, p in enumerate(pieces):
                K, ncol = p.shape
                kt = K // 128
                v = slot.a[:, off:off + kt * ncol].rearrange("p (k c) -> p k c", k=kt)
                S.dma("pool", slot.ch, v, p.rearrange("(k p) c -> p k c", p=128),
                      writes=[slot] if pi == 0 else (), join=() if pi == 0 else [slot])
                views.append(v)
                off += kt * ncol
            assert off <= SLOTW
            return slot, views

        def fetch2(piece):
            if wctr[0] % 2:
                wctr[0] += 1
            i0 = wctr[0] % NSLOT
            wctr[0] += 2
            sa, sb_ = slots[i0], slots[i0 + 1]
            v = wring.t[:, i0 * SLOTW:(i0 + 2) * SLOTW].rearrange("p (k c) -> p k c", k=8)
            S.dma("pool", sa.ch, v, piece.rearrange("(k p) c -> p k c", p=128), writes=[sa, sb_])
            return [sa, sb_], v

        def emit_rows(srcs, n, dst, reads, final=True):
            ps = P.get()
            for i, s_ in enumerate(srcs):
                TR(ps, ps.a[0:n, i * 128:(i + 1) * 128], s_, ident.a, reads, sig=(i == len(srcs) - 1))
            rw = T["rows"].get()
            CP(rw.a[0:n, :], ps.a[0:n, :], [ps], [rw])
            S.dma("sp", rw.ch, dst, rw.a[0:n, :], reads=[rw], final=final)

        class StageA:
            deps = [xstage]

            @staticmethod
            def load(src_ap, n):
                S.dma("sp", xstage.ch, xstage.a[0:n, :], src_ap, writes=[xstage])

            @staticmethod
            def ap(kt, n):
                return xstage.a[0:n, kt * 128:(kt + 1) * 128]

        class StageB:
            deps = list(T["rows"].b)

            @staticmethod
            def load(src_ap, n):
                for h_ in range(2):
                    rb = T["rows"].b[h_]
                    S.dma("sp", rb.ch, rb.a[0:n, :], src_ap[:, h_ * 512:(h_ + 1) * 512], writes=[rb])

            @staticmethod
            def ap(kt, n):
                return T["rows"].b[kt // 4].a[0:n, (kt % 4) * 128:(kt % 4 + 1) * 128]

        stages = [StageA, StageB]

        setup = [ident, tril, vgb, icn, wsb, bsb]
        S.dma("sp", ch_misc, ident.a, ident_d, writes=[ident])
        S.dma("sp", ch_misc, tril.a, tril_d, writes=[tril])
        for l in range(NL):
            S.dma("sp", ch_misc, vgb.t[:, l, :], chunk_vg[l].partition_broadcast(128), writes=[vgb] if l == 0 else (),
                  join=() if l == 0 else [vgb])
        S.dma("sp", ch_misc, icn.a, invcnt_d[0].partition_broadcast(128), writes=[icn])
        S.dma("sp", ch_misc, wsb.a, ws00_d[0].partition_broadcast(128), writes=[wsb])
        S.dma("sp", ch_misc, bsb.a, bs0_d[0].partition_broadcast(128), writes=[bsb])
        S.batch_end(ch_misc, setup)
        rw0 = T["rows"].get()
        S.dma("sp", rw0.ch, rw0.a[:, 0:384].rearrange("p (r c) -> p r c", r=3),
              pvec.rearrange("(r p) c -> p r c", p=128), writes=[rw0])
        S.dma("sp", xstage.ch, xstage.a.rearrange("p (g j) -> p g j", g=8),
              chunk_ws.rearrange("l g i j -> i (l g) j"), writes=[xstage])
        ch_miscw = S.chan("miscw")
        for dst_, src_ in ((poolw, pool_w), (lruwa, lru_wa), (lruwx, lru_wx)):
            S.dma("pool", ch_miscw, dst_.a, src_.rearrange("l g i j -> i (l g) j"), writes=[dst_])
        S.batch_end(ch_miscw, [poolw, lruwa, lruwx])
        S.op("dve", lambda e: e.memset(ones_bf.a, 1.0), (), [ones_bf])
        S.op("dve", lambda e: e.memset(mhalf.a, -0.5), (), [mhalf])
        for cb_ in (cpt, crt, cht, cft):
            S.op("dve", lambda e, cb_=cb_: e.memset(cb_.a, 0.0), (), [cb_])
        for l in range(NL):
            for lst, n in ((cp_v, 4), (cr_v, 8), (cf_v, 24)):
                for j in range(n):
                    for q in range(2):
                        lst[l][j][q].w = [("dve", S.cnt["dve"])]
            for j in range(8):
                ch_v[l][j].w = [("dve", S.cnt["dve"])]
        ps = P.get()
        for r_ in range(3):
            TR(ps, ps.a[:, r_ * 128:(r_ + 1) * 128], rw0.a[:, r_ * 128:(r_ + 1) * 128], ident.a, [rw0], sig=(r_ == 2))
        CP(pv.a, ps.a[:, 0:PV_ROWS], [ps], [pv])
        ACT(clv.t[:, 0:16], pv.t[:, PV_LAM:PV_LAM + 16], AF.Exp, [pv], [clv], scale=-1.0)
        ACT(clv.t[:, 0:16], clv.t[:, 0:16], AF.Ln, [clv], [clv], bias=1.0)
        TS(clv.t[:, 16:32], clv.t[:, 0:16], -4.0, None, ALU.mult, ALU.bypass, [clv], [clv])
        TS(clv.t[:, 0:16], clv.t[:, 0:16], -8.0, None, ALU.mult, ALU.bypass, [clv], [clv])
        TS(hbv.a, pv.t[:, PV_LBA:PV_LBA + 32], 0.5, None, ALU.mult, ALU.bypass, [pv], [hbv])
        for lg in range(8):
            blk_ = xstage.a[:, lg * 128:(lg + 1) * 128]
            TT(blk_, blk_, tril.a, ALU.mult, [xstage, tril], [xstage])
        for h_ in range(2):
            ps = P.get()
            for i in range(4):
                lg = h_ * 4 + i
                TR(ps, ps.a[:, i * 128:(i + 1) * 128], xstage.a[:, lg * 128:(lg + 1) * 128], ident.a, [xstage], sig=(i == 3))
            CP(wsT.t[:, h_ * 4:(h_ + 1) * 4, :], ps.a.rearrange("p (a b) -> p a b", a=4), [ps], [wsT])
        ch_misc2 = S.chan("misc2")
        S.dma("sp", ch_misc2, bsbt.a.rearrange("p a b -> p (a b)"), chunk_bs.rearrange("l g i -> (l g i)").partition_broadcast(128),
              writes=[bsbt])

        def load_x(c):
            if 2 in c.blocks:
                S.dma("sp", xstage.ch, xstage.a[0:NSMP, :], x_sample, writes=[xstage])
                for g in range(2):
                    ps = P.get()
                    for i in range(4):
                        kt = g * 4 + i
                        TR(ps, ps.a[:, i * NSMP:(i + 1) * NSMP], xstage.a[0:NSMP, kt * 128:(kt + 1) * 128],
                           ident.a[0:NSMP, 0:NSMP], [xstage], sig=(i == 3))
                    CP(xst.t[:, g * 4:(g + 1) * 4, :], ps.a[:, 0:4 * NSMP].rearrange("p (a b) -> p a b", a=4),
                       [ps], [x[g * 4 + i][2] for i in range(4)])
            for tt in range(8):
                stg = stages[tt % 2]
                stg.load(x_prompt[c.tok0 + tt * 128:c.tok0 + (tt + 1) * 128, :], 128)
                blk, c0 = tt // 4, (tt % 4) * 128
                for g in range(2):
                    ps = P.get()
                    for i in range(4):
                        kt = g * 4 + i
                        TR(ps, ps.a[:, i * 128:(i + 1) * 128], stg.ap(kt, 128), ident.a, stg.deps, sig=(i == 3))
                    CP(xt.t[:, g * 4:(g + 1) * 4, blk * NB + c0:blk * NB + c0 + 128],
                       ps.a.rearrange("p (a b) -> p a b", a=4), [ps], [x[g * 4 + i][blk] for i in range(4)],
                       eng="act" if g == 0 else "dve")
                if tt % 4 == 3:
                    norm_blk(c, PV_N1G, blk)
            if 2 in c.blocks:
                norm_blk(c, PV_N1G, 2)

        def rstd_blk(c, blk):
            N = NW[blk]
            ps = P.get()
            for kt in range(8):
                sq = T["sqn"].get()
                ACT(sq.a[:, :N], x[kt][blk].a[:, :N], AF.Square, [x[kt][blk]], [sq])
                MM(ps, ps.a[:, :N], ones_bf.a, sq.a[:, :N], kt == 0, kt == 7, [sq, ones_bf], True)
            rs = T["r"].get()
            ACT(rs.a[:, :N], ps.a[:, :N], AF.Ln, [ps], [rs], bias=EPS, scale=1.0 / D)
            ACT(rs.a[:, :N], rs.a[:, :N], AF.Exp, [rs], [rs], scale=-0.5)
            return rs

        def norm_blk(c, g0, blk):
            N = NW[blk]
            rs = rstd_blk(c, blk)
            for kt in range(8):
                STT(xn[kt][blk].a[:, :N], x[kt][blk].a[:, :N], pvc(g0 + kt), rs.a[:, :N], ALU.mult, ALU.mult,
                    [x[kt][blk], rs, pv], [xn[kt][blk]])

        def norm_to_xn(c, g0):
            for blk in c.blocks:
                norm_blk(c, g0, blk)

        def residual_stage(c, ncg, make_group, after_blk, tail=2):
            for cg in range(ncg - tail):
                grp = make_group(cg)
                for m2 in range(2):
                    for blk in c.blocks:
                        grp(m2, blk)
            grps = [make_group(cg) for cg in range(ncg - tail, ncg)]
            for bi, blk in enumerate(c.blocks):
                first = True
                for grp in grps:
                    for m2 in range(2):
                        grp(m2, blk)
                        if bi > 0 and first:
                            after_blk(c.blocks[bi - 1])
                        first = False
            after_blk(c.blocks[-1])

        def xn_pairs(wv, cs, blk, N):
            return [(wv[:, kt, cs], xn[kt][blk].a[:, :N], [xn[kt][blk]]) for kt in range(8)]

        def xn_reads(blk):
            return []

        def run_pipelined(items, phase_fns, ascending=False):
            n, npz = len(items), len(phase_fns)
            states = [dict() for _ in items]
            import types
            for step in range(n + npz - 1):
                gens = []
                for ph in (range(npz) if ascending else range(npz - 1, -1, -1)):
                    it = step - ph
                    if 0 <= it < n:
                        g = phase_fns[ph](items[it], states[it])
                        if isinstance(g, types.GeneratorType):
                            gens.append(g)
                while gens:
                    for g in list(gens):
                        try:
                            next(g)
                        except StopIteration:
                            gens.remove(g)

        def stage_pool(c, l):
            items = [(j, blk) for jp in range(0, 4, 2) for blk in c.blocks for j in (jp, jp + 1)]
            sl = {}

            def wts(j):
                g2 = j // 2
                if g2 not in sl:
                    sl[g2] = fetch([w_in[l][:, C_POOL + g2 * 256:C_POOL + (g2 + 1) * 256]])
                return sl[g2][0], sl[g2][1][0]

            def p0(it, s):
                j, blk = it
                N = NW[blk]
                slot, wv = wts(j)
                ps = P.get()
                mm(ps, ps.a[:, :N], xn_pairs(wv, slice((j % 2) * 128, (j % 2 + 1) * 128), blk, N), [slot])
                if blk != 2:
                    ext = T["ext"].get()
                    cvi, cvo = cp_v[l][j][blk], cp_v[l][j][1 - blk]
                    CP(ext.a[:, 0:15], cvi.a, [cvi], [ext])
                    CP(ext.a[:, 15:527], ps.a, [ps], [ext], eng="act")
                    CP(cvo.a, ps.a[:, 512 - 15:512], [ps], [cvo], eng="act")
                    s["ext"] = ext
                else:
                    CP(exs.t[:, j, :, 15], ps.a[:, :N], [ps], [exs_v[j]], eng="act")

            def p1(it, s):
                j, blk = it
                N, w = NW[blk], WIN[j]
                d = T["bcb"].get()
                s["d"] = d
                if blk != 2:
                    ext = s["ext"]
                    top = j + 1
                    los = {top: 15}
                    for k in range(top, 1, -1):
                        los[k - 1] = los[k] - 2 ** (k - 1)
                    bufs = [T["acc"].get(), T["r"].get()]
                    prev = ext
                    for k in range(1, top + 1):
                        lo, sh = los[k], 2 ** (k - 1)
                        cur = bufs[(k - 1) % 2]
                        TT(cur.a[:, lo:527], prev.a[:, lo:527], prev.a[:, lo - sh:527 - sh], ALU.add, [prev], [cur])
                        prev = cur
                    STT(d.a, prev.a[:, 15:527], 1.0 / w, ext.a[:, 15:527], ALU.mult, ALU.subtract, [prev, ext], [d])
                    if c.tok0 == 0 and blk == 0:
                        tm = T["sq"].get()
                        TT(tm.a[:, 0:16], prev.a[:, 15:31], icn.t[:, j * 16:(j + 1) * 16], ALU.mult, [prev, icn], [tm])
                        TT(d.a[:, 0:16], tm.a[:, 0:16], ext.a[:, 15:31], ALU.subtract, [tm, ext], [d])
                else:
                    ev = exs_v[j]
                    sr = T["acc"].get()
                    S.op("dve", lambda e: e.tensor_reduce(sr.a[:, :N], exs.t[:, j, :, 16 - w:16], AX.X, ALU.add), [ev], [sr])
                    STT(d.a[:, :N], sr.a[:, :N], 1.0 / w, exs.t[:, j, :, 15], ALU.mult, ALU.subtract, [sr, ev], [d])

            def p2(it, s):
                j, blk = it
                N, d = NW[blk], s["d"]
                ps2 = P.get()
                MM(ps2, ps2.a[:, :N], poolw.t[:, l * 4 + j, :], d.a[:, :N], True, True, [poolw, d], True)
                ACT(ya[j][blk].a[:, :N], ps2.a[:, :N], AF.Identity, [ps2, pv], [ya[j][blk]], scale=pvc(PV_PSC + l * 4 + j))

            run_pipelined(items, [p0, p1, p2])
            if 2 in c.blocks:
                emit_rows([exs.t[:, j, :, 15] for j in range(4)], NSMP, o_pool_s[l, :, 14, :], exs_v)

        def stage_rnn(c, l):
            items = [(j, blk) for jp in range(0, 8, 2) for blk in (0, 1) for j in (jp, jp + 1)]
            has_s = 2 in c.blocks
            sl = {}

            def wts(j):
                jg = j // 2
                if jg not in sl:
                    sx, (wxv,) = fetch([w_in[l][:, C_RX + jg * 256:C_RX + (jg + 1) * 256]])
                    sg, (wgv,) = fetch([w_in[l][:, C_RG + jg * 256:C_RG + (jg + 1) * 256]])
                    sl[jg] = (sx, wxv, sg, wgv)
                return sl[jg]

            def tmp(role, blk):
                return (TSm if blk == 2 else T)[role].get()

            def subs(it, s):
                j, blk = it
                out = [(j, blk, s)]
                if blk == 1 and has_s:
                    out.append((j, 2, s.setdefault("smp", {})))
                return out

            def p0(it, s):
                late = []
                for j, blk, st_ in subs(it, s):
                    N = NW[blk]
                    sx, wxv, sg, wgv = wts(j)
                    cs = slice((j % 2) * 128, (j % 2 + 1) * 128)
                    ps = P.get()
                    mm(ps, ps.a[:, :N], xn_pairs(wxv, cs, blk, N), [sx])
                    if blk == 2:
                        CP(ex3.t[:, j, :, 3], ps.a[:, :N], [ps], [ex3_v[j]], eng="act")
                    else:
                        late.append((j, blk, st_, ps))
                for _ in range(10):
                    yield
                for j, blk, st_, ps in late:
                    ext = T["ext"].get()
                    cvi, cvo = cr_v[l][j][blk], cr_v[l][j][1 - blk]
                    CP(ext.a[:, 0:3], cvi.a, [cvi], [ext])
                    CP(ext.a[:, 3:515], ps.a, [ps], [ext])
                    CP(cvo.a, ps.a[:, 512 - 3:512], [ps], [cvo])
                    st_["ext"] = ext

            def p1(it, s):
                first = True
                for j, blk, st_ in subs(it, s):
                    N = NW[blk]
                    wk = [pvc(PV_RCW + (l * 4 + k) * 8 + j) for k in range(4)]
                    cb = pvc(PV_RCB + l * 8 + j)
                    acc, bcb = tmp("acc", blk), tmp("bcb", blk)
                    st_.update(acc=acc, bcb=bcb)
                    if first:
                        yield
                        first = False
                    if blk != 2:
                        ext = st_["ext"]
                        taps = [ext.a[:, k:k + 512] for k in range(4)]
                        rd = [ext]
                    else:
                        taps = [ex3.t[:, j, :, k] for k in range(4)]
                        rd = [ex3_v[j]]
                    TS(acc.a[:, :N], taps[3], wk[3], cb, ALU.mult, ALU.add, rd + [pv], [acc])
                    yield
                    for k in range(3):
                        STT(acc.a[:, :N], taps[k], wk[k], acc.a[:, :N], ALU.mult, ALU.add, rd + [acc, pv], [acc])
                        yield
                    CP(bcb.a[:, :N], acc.a[:, :N], [acc], [bcb])

            def p2(it, s):
                for j, blk, st_ in subs(it, s):
                    N, bcb = NW[blk], st_["bcb"]
                    sx, wxv, sg, wgv = wts(j)
                    cs = slice((j % 2) * 128, (j % 2 + 1) * 128)
                    lj = l * 8 + j
                    if blk != 2:
                        psg, psr, psi = P.get(), P.get(), P.get()
                        og, orr, oi = psg.a[:, :N], psr.a[:, :N], psi.a[:, :N]
                    else:
                        psg = psr = psi = P.get()
                        og, orr, oi = psg.a[:, 0:N], psg.a[:, N:2 * N], psg.a[:, 2 * N:3 * N]
                    mm(psg, og, xn_pairs(wgv, cs, blk, N), [sg])
                    MM(psr, orr, lruwa.t[:, lj, :], bcb.a[:, :N], True, True, [lruwa, bcb], True)
                    MM(psi, oi, lruwx.t[:, lj, :], bcb.a[:, :N], True, True, [lruwx, bcb], True)
                    st_.update(bg=psg, br=psr, bi=psi, og=og, orr=orr, oi=oi)

            def p3(it, s):
                ss = subs(it, s)
                for j, blk, st_ in ss:
                    N, lj = NW[blk], l * 8 + j
                    r, ii, sq, gg = tmp("r", blk), tmp("i", blk), tmp("sq", blk), tmp("gg", blk)
                    st_.update(r=r, ii=ii, sq=sq, gg=gg)
                    ACT(gg.a[:, :N], st_["og"], AF.Gelu_apprx_tanh, [st_["bg"]], [gg])
                    ACT(r.a[:, :N], st_["orr"], AF.Tanh, [st_["br"], hbv], [r], bias=hbv.t[:, lj:lj + 1], scale=0.5)
                    ACT(ii.a[:, :N], st_["oi"], AF.Tanh, [st_["bi"], hbv], [ii], bias=hbv.t[:, 16 + lj:17 + lj], scale=0.5)
                for j, blk, st_ in ss:
                    N, lj, r, sq = NW[blk], l * 8 + j, st_["r"], st_["sq"]
                    ACT(sq.a[:, :N], r.a[:, :N], AF.Exp, [r, clv], [sq], scale=clv.t[:, lj:lj + 1], bias=clv.t[:, lj:lj + 1])
                    ACT(r.a[:, :N], r.a[:, :N], AF.Exp, [r, clv], [r], scale=clv.t[:, 16 + lj:17 + lj], bias=clv.t[:, 16 + lj:17 + lj])
                    ACT(sq.a[:, :N], sq.a[:, :N], AF.Ln, [sq], [sq], scale=-1.0, bias=1.0)
                    ACT(sq.a[:, :N], sq.a[:, :N], AF.Exp, [sq], [sq], scale=0.5)

            def p4(it, s):
                for j, blk, st_ in subs(it, s):
                    N, acc, r, ii, sq, gg = NW[blk], st_["acc"], st_["r"], st_["ii"], st_["sq"], st_["gg"]
                    STT(ii.a[:, :N], ii.a[:, :N], 1.0, acc.a[:, :N], ALU.add, ALU.mult, [ii, acc], [ii])
                    yield
                    STT(ii.a[:, :N], ii.a[:, :N], 0.5, sq.a[:, :N], ALU.mult, ALU.mult, [ii, sq], [ii])
                    yield
                    h = sq
                    if blk != 2:
                        hv = ch_v[l][j]
                        S.op("dve", lambda e, h=h, r=r, ii=ii, hv=hv, N=N: e.tensor_tensor_scan(h.a[:, :N], r.a[:, :N], ii.a[:, :N], hv.a,
                                                                                              ALU.mult, ALU.add), [r, ii, hv], [h])
                        yield
                        CP(hv.a, h.a[:, N - 1:N], [h], [hv])
                    else:
                        TT(h.a[:, :N], r.a[:, :N], h0t.t[:, j, :], ALU.mult, [r, h0_v[j]], [h])
                        TT(h.a[:, :N], h.a[:, :N], ii.a[:, :N], ALU.add, [h, ii], [h])
                        CP(h0t.t[:, j, :], h.a[:, :N], [h], [h0_v[j]])
                    yield
                    TT(yb[j][blk].a[:, :N], gg.a[:, :N], h.a[:, :N], ALU.mult, [gg, h], [yb[j][blk]])

            run_pipelined(items, [p0, p1, p2, p3, p4])
            if has_s:
                for g in range(2):
                    emit_rows([ex3.t[:, g * 4 + i, :, 3] for i in range(4)], NSMP, o_rc_s[l, :, 2, g * 512:(g + 1) * 512], ex3_v)
                    emit_rows([h0t.t[:, g * 4 + i, :] for i in range(4)], NSMP, o_h_s[l, :, g * 512:(g + 1) * 512], h0_v)

        def rstd_pow(ssq, n):
            S.op("pool", lambda e: e.tensor_scalar(ssq.a[0:n, :], ssq.a[0:n, :], 1.0 / 512, EPS, ALU.mult, ALU.add), [ssq], [ssq])
            S.op("pool", lambda e: e.tensor_tensor(ssq.a[0:n, :], ssq.a[0:n, :], mhalf.a[0:n, :], ALU.pow), [ssq, mhalf], [ssq])

        def stage_chunk(c, l):
            svd, wvv = fetch2(w_in[l][:, C_V:C_V + 512])
            sus = [fetch([w_in[l][:, C_U + h_ * 256:C_U + (h_ + 1) * 256]]) for h_ in range(2)]

            def u_pairs(g, blk, N):
                return xn_pairs(sus[g // 2][1][0], slice((g % 2) * 128, (g % 2 + 1) * 128), blk, N), [sus[g // 2][0]]
            if 2 in c.blocks:
                N = NSMP
                ps = P.get()
                mm(ps, ps.a[0:N, :], [(xn[kt][2].a[:, 0:N], wvv[:, kt, :], [xn[kt][2]]) for kt in range(8)], svd)
                vg, junk, ssq, vnf = T["r"].get(), T["sq"].get(), T["ssq"].get(), T["acc"].get()
                ACT(vg.a[0:N, :512], ps.a[0:N, :], AF.Gelu_apprx_tanh, [ps], [vg])
                ACT(junk.a[0:N, :512], vg.a[0:N, :512], AF.Square, [vg], [junk, ssq], accum_out=ssq.a[0:N, :])
                rstd_pow(ssq, N)
                STT(vnf.a[0:N, :512], vg.a[0:N, :512], ssq.a[0:N, :], vgb.t[0:N, l, :], ALU.mult, ALU.mult, [vg, ssq, vgb], [vnf])
                S.dma("sp", ch_cv, o_cv_s[l], vnf.a[0:N, :512], reads=[vnf], final=True)
                for g in range(4):
                    pst = P.get()
                    TR(pst, pst.a[:, 0:N], vnf.a[0:N, g * 128:(g + 1) * 128], ident.a[0:N, 0:N], [vnf])
                    mixs = T["i"].get()
                    lg = l * 4 + g
                    ACT(mixs.a[:, :N], pst.a[:, :N], AF.Identity, [pst, wsb, bsb], [mixs], scale=wsb.t[:, lg:lg + 1], bias=bsb.t[:, lg:lg + 1])
                    psu = P.get()
                    prs, rds = u_pairs(g, 2, N)
                    mm(psu, psu.a[:, :N], prs, rds)
                    ug = T["gg"].get()
                    ACT(ug.a[:, :N], psu.a[:, :N], AF.Gelu_apprx_tanh, [psu], [ug])
                    TT(yc[g][2].a[:, :N], ug.a[:, :N], mixs.a[:, :N], ALU.mult, [ug, mixs], [yc[g][2]])
            mixb = P.b[0:4]
            PG = Ring(P.b[4:8])
            if True:
                items = [(k_, i_, b_) for b_ in range(2) for k_, i_ in [("v", c4) for c4 in range(4)] + [("u", g) for g in range(4)]]

                def q0(it, s):
                    kind, i, blk = it
                    ps = PG.get()
                    if kind == "v":
                        tc = slice(i * 128, (i + 1) * 128)
                        mm(ps, ps.a, [(xn[kt][blk].a[:, tc], wvv[:, kt, :], [xn[kt][blk]]) for kt in range(8)], svd)
                        vg, junk, ssq = T["r"].get(), T["sq"].get(), T["ssq"].get()
                        s.update(vg=vg, ssq=ssq)
                        ACT(vg.a[:, :512], ps.a, AF.Gelu_apprx_tanh, [ps], [vg])
                        ACT(junk.a[:, :512], vg.a[:, :512], AF.Square, [vg], [junk, ssq], accum_out=ssq.a)
                    else:
                        prs, rds = u_pairs(i, blk, 512)
                        mm(ps, ps.a, prs, rds)
                        ug = T["gg"].get()
                        s["ug"] = ug
                        ACT(ug.a[:, :512], ps.a, AF.Gelu_apprx_tanh, [ps], [ug])

                def q1(it, s):
                    kind, i, blk = it
                    if kind == "v":
                        tc = slice(i * 128, (i + 1) * 128)
                        vg, ssq, vn = s["vg"], s["ssq"], T["vn"].get()
                        rstd_pow(ssq, 128)
                        STT(vn.a, vg.a[:, :512], ssq.a, vgb.t[:, l, :], ALU.mult, ALU.mult, [vg, ssq, vgb], [vn])
                        for g in range(4):
                            lg = l * 4 + g
                            MM(mixb[g], mixb[g].a[:, tc], vn.a[:, g * 128:(g + 1) * 128], wsT.t[:, lg, :], True, True, [vn, wsT], g == 3)
                    else:
                        ug, tb = s["ug"], T["i"].get()
                        lg = l * 4 + i
                        TT(tb.a[:, :512].rearrange("p (a b) -> p a b", a=4), mixb[i].a.rearrange("p (a b) -> p a b", a=4),
                           bsbt.t[:, lg, :].unsqueeze(1).to_broadcast([128, 4, 128]), ALU.add, [mixb[i], bsbt], [tb])
                        TT(yc[i][blk].a, ug.a[:, :512], tb.a[:, :512], ALU.mult, [ug, tb], [yc[i][blk]])

                run_pipelined(items, [q0, q1], ascending=True)


        def stage_merge(c, l):
            for mg in range(4):
                c0 = mg * 256
                sA, (gAv,) = fetch([w_in[l][:, C_G + c0:C_G + c0 + 256]])
                sP, (pav, pcv) = fetch([w_pa[l][:, c0:c0 + 256], w_pc[l][:, c0:c0 + 256]])
                sB, (gBv,) = fetch([w_in[l][:, C_G + 1024 + c0:C_G + 1024 + c0 + 256]])
                sPb, (pbv,) = fetch([w_pb[l][:, c0:c0 + 256]])
                sC, (gCv,) = fetch([w_in[l][:, C_G + 2048 + c0:C_G + 2048 + c0 + 256]])
                plan = [(gAv, sA, pav, sP, ya, 4), (gBv, sB, pbv, sPb, yb, 8), (gCv, sC, pcv, sP, yc, 4)]
                for m2 in range(2):
                    m = mg * 2 + m2
                    cs = slice(m2 * 128, (m2 + 1) * 128)
                    for blk in c.blocks:
                        N = NW[blk]
                        acc, tmp = T["acc"].get(), T["i"].get()
                        for bi, (gv, gs, pw, pslot, ys, nk) in enumerate(plan):
                            psg = P.get()
                            mm(psg, psg.a[:, :N], xn_pairs(gv, cs, blk, N), [gs] + xn_reads(blk))
                            gt = T["gg"].get()
                            ACT(gt.a[:, :N], psg.a[:, :N], AF.Sigmoid, [psg], [gt])
                            psp = P.get()
                            mm(psp, psp.a[:, :N], [(pw[:, kt, cs], ys[kt][blk].a[:, :N], [ys[kt][blk]]) for kt in range(nk)], [pslot])
                            if bi == 0:
                                TT(acc.a[:, :N], gt.a[:, :N], psp.a[:, :N], ALU.mult, [gt, psp], [acc])
                            else:
                                TT(tmp.a[:, :N], gt.a[:, :N], psp.a[:, :N], ALU.mult, [gt, psp], [tmp])
                                if bi == 1:
                                    TT(acc.a[:, :N], acc.a[:, :N], tmp.a[:, :N], ALU.add, [acc, tmp], [acc])
                                else:
                                    TT(mrg[m][blk].a[:, :N], acc.a[:, :N], tmp.a[:, :N], ALU.add, [acc, tmp], [mrg[m][blk]])

        def stage_wo(c, l, after_blk):
            def make_group(cg):
                so, (wov,) = fetch([w_o[l][:, cg * 256:(cg + 1) * 256]])

                def grp(m4, blk):
                    m, N = cg * 2 + m4, NW[blk]
                    ps = P.get()
                    mm(ps, ps.a[:, :N],
                       [(wov[:, kt, m4 * 128:(m4 + 1) * 128], mrg[kt][blk].a[:, :N], [mrg[kt][blk]]) for kt in range(8)], [so])
                    TT(x[m][blk].a[:, :N], x[m][blk].a[:, :N], ps.a[:, :N], ALU.add, [x[m][blk], ps], [x[m][blk]])
                return grp
            residual_stage(c, 4, make_group, after_blk)

        def stage_ffn_up(c, l):
            items = [(t, blk) for tp in range(0, 24, 2) for blk in c.blocks for t in (tp, tp + 1)]
            sl = {}

            def wts(t):
                cg = t // 2
                if cg not in sl:
                    sg, (wgv,) = fetch([ffn_wg[l][:, cg * 256:(cg + 1) * 256]])
                    su, (wuv,) = fetch([ffn_wu[l][:, cg * 256:(cg + 1) * 256]])
                    sl[cg] = (sg, wgv, su, wuv)
                return sl[cg]

            def p0(it, s):
                t, blk = it
                N = NW[blk]
                sg, wgv, su, wuv = wts(t)
                cs = slice((t % 2) * 128, (t % 2 + 1) * 128)
                ps = P.get()
                mm(ps, ps.a[:, :N], xn_pairs(wgv, cs, blk, N), [sg] + xn_reads(blk))
                if blk != 2:
                    ext = T["ext"].get()
                    cvi, cvo = cf_v[l][t][blk], cf_v[l][t][1 - blk]
                    CP(ext.a[:, 0:2], cvi.a, [cvi], [ext])
                    CP(ext.a[:, 2:514], ps.a, [ps], [ext], eng="act")
                    CP(cvo.a, ps.a[:, 512 - 2:512], [ps], [cvo], eng="act")
                    s["ext"] = ext
                else:
                    CP(ex2.t[:, t, :, 2], ps.a[:, :N], [ps], [ex2_v[t]], eng="act")

            def p1(it, s):
                t, blk = it
                N = NW[blk]
                wk = [pvc(PV_FCW + (l * 3 + k) * 24 + t) for k in range(3)]
                cb = pvc(PV_FCB + l * 24 + t)
                acc = T["acc"].get()
                s["acc"] = acc
                if blk != 2:
                    ext = s["ext"]
                    TS(acc.a[:, :N], ext.a[:, 2:514], wk[2], cb, ALU.mult, ALU.add, [ext, pv], [acc])
                    for k in range(2):
                        STT(acc.a[:, :N], ext.a[:, k:k + 512], wk[k], acc.a[:, :N], ALU.mult, ALU.add, [ext, acc, pv], [acc])
                else:
                    ev = ex2_v[t]
                    TS(acc.a[:, :N], ex2.t[:, t, :, 2], wk[2], cb, ALU.mult, ALU.add, [ev, pv], [acc])
                    for k in range(2):
                        STT(acc.a[:, :N], ex2.t[:, t, :, k], wk[k], acc.a[:, :N], ALU.mult, ALU.add, [ev, acc, pv], [acc])

            def p2(it, s):
                t, blk = it
                N, acc = NW[blk], s["acc"]
                sg, wgv, su, wuv = wts(t)
                cs = slice((t % 2) * 128, (t % 2 + 1) * 128)
                ACT(acc.a[:, :N], acc.a[:, :N], AF.Gelu_apprx_tanh, [acc], [acc])
                psu = P.get()
                mm(psu, psu.a[:, :N], xn_pairs(wuv, cs, blk, N), [su] + xn_reads(blk))
                s["psu"] = psu

            def p3(it, s):
                t, blk = it
                N, acc, psu = NW[blk], s["acc"], s["psu"]
                TT(hf[t][blk].a[:, :N], acc.a[:, :N], psu.a[:, :N], ALU.mult, [acc, psu], [hf[t][blk]])
                if blk == c.blocks[-1] and t % 4 == 3 and 2 in c.blocks:
                    cg = t // 4
                    emit_rows([ex2.t[:, cg * 4 + i, :, 2] for i in range(4)], NSMP, o_ff_s[l, :, 1, cg * 512:(cg + 1) * 512],
                              ex2_v[cg * 4:cg * 4 + 4])

            run_pipelined(items, [p0, p1, p2, p3])


        def stage_ffn_down(c, l, after_blk):
            def make_group(cg):
                fs = [fetch([ffn_wd[l][kc * 1024:(kc + 1) * 1024, cg * 256:(cg + 1) * 256]]) for kc in range(3)]

                def grp(m4, blk):
                    m, N = cg * 2 + m4, NW[blk]
                    ps = P.get()
                    mm(ps, ps.a[:, :N],
                       [(fs[kt // 8][1][0][:, kt % 8, m4 * 128:(m4 + 1) * 128], hf[kt][blk].a[:, :N], [hf[kt][blk]]) for kt in range(24)],
                       [f[0] for f in fs])
                    TT(x[m][blk].a[:, :N], x[m][blk].a[:, :N], ps.a[:, :N], ALU.add, [x[m][blk], ps], [x[m][blk]])
                return grp
            residual_stage(c, 4, make_group, after_blk)

        def load_sample_state(l):
            for hs in range(2):
                rw = T["rows"].get()
                S.dma("sp", rw.ch, rw.a[0:120, :], st_pool[l, hs * 8:(hs + 1) * 8].rearrange("s r c -> (s r) c"), writes=[rw])
                ps = P.get()
                for j in range(4):
                    TR(ps, ps.a[:, j * 120:(j + 1) * 120], rw.a[0:120, j * 128:(j + 1) * 128], ident.a[0:120, 0:120], [rw], sig=(j == 3))
                for j in range(4):
                    CP(exs.t[:, j, hs * 8:(hs + 1) * 8, 0:15], ps.a[:, j * 120:(j + 1) * 120].rearrange("p (s r) -> p s r", r=15),
                       [ps], [exs_v[j]])
            stg = StageA
            stg.load(st_rc[l].rearrange("s r c -> (s r) c"), 48)
            for g in range(2):
                ps = P.get()
                for i in range(4):
                    j = g * 4 + i
                    TR(ps, ps.a[:, i * 48:(i + 1) * 48], stg.ap(j, 48), ident.a[0:48, 0:48], stg.deps, sig=(i == 3))
                for i in range(4):
                    j = g * 4 + i
                    CP(ex3.t[:, j, :, 0:3], ps.a[:, i * 48:(i + 1) * 48].rearrange("p (s r) -> p s r", r=3), [ps], [ex3_v[j]])
            stg = StageB
            stg.load(st_h[l], NSMP)
            for g in range(2):
                ps = P.get()
                for i in range(4):
                    j = g * 4 + i
                    TR(ps, ps.a[:, i * NSMP:(i + 1) * NSMP], stg.ap(j, NSMP), ident.a[0:NSMP, 0:NSMP], stg.deps, sig=(i == 3))
                CP(h0t.t[:, g * 4:(g + 1) * 4, :], ps.a[:, 0:4 * NSMP].rearrange("p (a b) -> p a b", a=4), [ps],
                   [h0_v[g * 4 + i] for i in range(4)])
            for q in range(3):
                stg = stages[q % 2]
                stg.load(st_ff[l][:, :, q * 1024:(q + 1) * 1024].rearrange("s r c -> (s r) c"), 32)
                for g in range(2):
                    ps = P.get()
                    for i in range(4):
                        t = q * 8 + g * 4 + i
                        TR(ps, ps.a[:, i * 32:(i + 1) * 32], stg.ap(g * 4 + i, 32), ident.a[0:32, 0:32], stg.deps, sig=(i == 3))
                    for i in range(4):
                        t = q * 8 + g * 4 + i
                        CP(ex2.t[:, t, :, 0:2], ps.a[:, i * 32:(i + 1) * 32].rearrange("p (s r) -> p s r", r=2), [ps], [ex2_v[t]])
            S.dma("sp", ch_d2d, o_pool_s[l, :, 0:14, :], st_pool[l, :, 1:15, :], final=True)
            S.dma("sp", ch_d2d, o_rc_s[l, :, 0:2, :], st_rc[l, :, 1:3, :], final=True)
            S.dma("sp", ch_d2d, o_ff_s[l, :, 0:1, :], st_ff[l, :, 1:2, :], final=True)

        def emit_prompt_state(l):
            emit_rows([cp_v[l][j][0].a for j in range(4)], 15, o_pool_p[l], [cp_v[l][j][0] for j in range(4)])
            for g in range(2):
                emit_rows([cr_v[l][g * 4 + i][0].a for i in range(4)], 3, o_rc_p[l, :, g * 512:(g + 1) * 512],
                          [cr_v[l][g * 4 + i][0] for i in range(4)])
                emit_rows([ch_v[l][g * 4 + i].a for i in range(4)], 1, o_h_p[l, :, g * 512:(g + 1) * 512], ch_v[l])
            for g in range(6):
                emit_rows([cf_v[l][g * 4 + i][0].a for i in range(4)], 2, o_ff_p[l, :, g * 512:(g + 1) * 512],
                          [cf_v[l][g * 4 + i][0] for i in range(4)])

        def final_store_blk(c, blk):
            if True:
                N = NW[blk]
                rs = rstd_blk(c, blk)
                for g in range(2):
                    yts = [T["acc"].get(), T["i"].get(), T["gg"].get(), T["sq"].get()]
                    for i in range(4):
                        kt = g * 4 + i
                        STT(yts[i].a[:, :N], x[kt][blk].a[:, :N], pvc(PV_FNG + kt), rs.a[:, :N], ALU.mult, ALU.mult,
                            [x[kt][blk], rs, pv], [yts[i]])
                    if blk == 2:
                        ps = P.get()
                        for i in range(4):
                            TR(ps, ps.a[0:N, i * 128:(i + 1) * 128], yts[i].a[:, 0:N], ident.a, [yts[i]], sig=(i == 3))
                        rw = T["rows"].get()
                        CP(rw.a[0:N, :], ps.a[0:N, :], [ps], [rw])
                        S.dma("sp", rw.ch, y_sample[:, g * 512:(g + 1) * 512], rw.a[0:N, :], reads=[rw], final=True)
                    else:
                        for tt in range(4):
                            ps = P.get()
                            for i in range(4):
                                TR(ps, ps.a[:, i * 128:(i + 1) * 128], yts[i].a[:, tt * 128:(tt + 1) * 128], ident.a, [yts[i]], sig=(i == 3))
                            rw = T["rows"].get()
                            CP(rw.a, ps.a, [ps], [rw], eng="act" if tt % 2 == 0 else "dve")
                            r0 = c.tok0 + blk * NB + tt * 128
                            S.dma("sp", rw.ch, y_prompt[r0:r0 + 128, g * 512:(g + 1) * 512], rw.a, reads=[rw], final=True)

        marks = []

        def mark(name):
            marks.append((name, dict(S.nins)))

        def layer(c, l):
            mark(f"L{l}.smpstate")
            if 2 in c.blocks:
                load_sample_state(l)
            mark(f"L{l}.norm1")
            mark(f"L{l}.pool")
            stage_pool(c, l)
            mark(f"L{l}.rnn")
            stage_rnn(c, l)
            mark(f"L{l}.chunk")
            stage_chunk(c, l)
            mark(f"L{l}.merge")
            stage_merge(c, l)
            mark(f"L{l}.wo")
            stage_wo(c, l, lambda blk: norm_blk(c, PV_N2G + l * 8, blk))
            mark(f"L{l}.ffn_up")
            stage_ffn_up(c, l)
            mark(f"L{l}.ffn_down")
            if l + 1 < NL:
                stage_ffn_down(c, l, lambda blk: norm_blk(c, PV_N1G + (l + 1) * 8, blk))
            else:
                stage_ffn_down(c, l, lambda blk: final_store_blk(c, blk))

        for c in (Ctx(0, (0, 1)), Ctx(TH, (0, 1, 2))):
            mark("load_x")
            load_x(c)
            for l in range(NL):
                layer(c, l)
                if c.tok0 == TH:
                    emit_prompt_state(l)
        S.finish()
        build_program.stats = dict(nins=dict(S.nins), nwait=S.nwait)
        build_program.log = S.log
        mark("end")
        build_program.marks = marks
    return nc


_NC = None


def _pack_pvec(p):
    rows = [p["norm1_g"].reshape(16, 128), p["norm2_g"].reshape(16, 128), p["pool_scale"].reshape(8, 128),
            p["rnn_conv_w"].reshape(64, 128), p["rnn_conv_b"].reshape(16, 128), p["lru_ba"].reshape(16, 128),
            p["lru_bx"].reshape(16, 128), p["lru_lambda"].reshape(16, 128), p["ffn_conv_w"].reshape(144, 128),
            p["ffn_conv_b"].reshape(48, 128), p["final_norm_g"].reshape(8, 128)]
    pv = np.concatenate(rows, axis=0)
    out = np.zeros((PV_ROWS, 128), np.float32)
    out[:pv.shape[0]] = pv
    return out


def kernel(**inputs):
    global _NC
    p = {k: np.ascontiguousarray(np.asarray(v, dtype=np.float32)) for k, v in inputs.items()}
    if _NC is None:
        _NC = build_program()
    nc = _NC
    ncore = 8
    shared = {k: p[k] for k in ("w_in", "pool_w", "lru_wa", "lru_wx", "chunk_ws", "chunk_bs", "chunk_vnorm_g", "w_pa", "w_pb",
                                "w_pc", "w_o", "ffn_wg", "ffn_wu", "ffn_wd")}
    shared["pvec"] = _pack_pvec(p)
    shared["ident"] = np.eye(128, dtype=np.float32)
    shared["tril"] = np.tril(np.ones((128, 128), np.float32))
    ic = np.zeros((4, 16), np.float32)
    for j, w in enumerate(WIN):
        ic[j] = 1.0 / np.minimum(np.arange(16) + 1, w)
    shared["invcnt"] = ic.reshape(1, 64)
    shared["ws00"] = np.ascontiguousarray(p["chunk_ws"][:, :, 0, 0]).reshape(1, 8)
    shared["bs0"] = np.ascontiguousarray(p["chunk_bs"][:, :, 0]).reshape(1, 8)
    in_maps = []
    for i in range(ncore):
        m = dict(shared)
        m["x_prompt"] = np.ascontiguousarray(p["x_prompt"][i])
        sl = slice(i * NSMP, (i + 1) * NSMP)
        m["x_sample"] = np.ascontiguousarray(p["x_sample"][sl, 0, :])
        m["state_pool"] = np.ascontiguousarray(p["state_pool"][:, sl])
        m["state_rnn_conv"] = np.ascontiguousarray(p["state_rnn_conv"][:, sl])
        m["state_rnn_h"] = np.ascontiguousarray(p["state_rnn_h"][:, sl])
        m["state_ffn_conv"] = np.ascontiguousarray(p["state_ffn_conv"][:, sl])
        in_maps.append(m)
    res = run_bass_kernel_spmd(nc, in_maps, core_ids=list(range(ncore)))
    R = res.results
    f32 = np.float32
    y_prompt = np.stack([R[i]["y_prompt"] for i in range(ncore)], 0).astype(f32)
    y_sample = np.concatenate([R[i]["y_sample"] for i in range(ncore)], 0)[:, None, :].astype(f32)
    pool_p = np.stack([R[i]["new_pool_prompt"] for i in range(ncore)], 1).astype(f32)
    pool_s = np.concatenate([R[i]["new_pool_sample"] for i in range(ncore)], 1).astype(f32)
    rc_p = np.stack([R[i]["new_rconv_prompt"] for i in range(ncore)], 1).astype(f32)
    rc_s = np.concatenate([R[i]["new_rconv_sample"] for i in range(ncore)], 1).astype(f32)
    h_p = np.stack([R[i]["new_h_prompt"][:, 0, :] for i in range(ncore)], 1).astype(f32)
    h_s = np.concatenate([R[i]["new_h_sample"] for i in range(ncore)], 1).astype(f32)
    ff_p = np.stack([R[i]["new_ffn_prompt"] for i in range(ncore)], 1).astype(f32)
    ff_s = np.concatenate([R[i]["new_ffn_sample"] for i in range(ncore)], 1).astype(f32)
    cv_s = np.concatenate([R[i]["new_chunk_v_sample"] for i in range(ncore)], 1)[:, :, None, :].astype(f32)
    return (y_prompt, y_sample, pool_p, pool_s, rc_p, rc_s, h_p, h_s, ff_p, ff_s, cv_s)
```
